# Optimizing a Trainium2 kernel written in Bass

```python
import math
import jax
import jax.numpy as jnp
from jax import lax
import numpy as np

D_MODEL = 1024
BATCH = 4
SEQ = 4096
DEPTH = 4
DEC_BATCH = 128
DEC_SEQ = 1
PAST_LEN = 8192
PAGE_SIZE = 128

N_MIXERS = 4
N_L_CONF = (DEPTH + 3) // N_MIXERS
N_L_SWA = (DEPTH + 2) // N_MIXERS
N_L_GDN = (DEPTH + 1) // N_MIXERS
N_L_SCONV = DEPTH // N_MIXERS
D_FF = 4 * D_MODEL
CONF_K = 31
HEAD_DIM = 64
N_HEADS = D_MODEL // HEAD_DIM
N_KV = 4
GQA_GROUP = N_HEADS // N_KV
WINDOW = 128
ROT_DIM = HEAD_DIM // 4
ROPE_THETA = 500000.0
GDN_DK = 128
GDN_DV = 128
GDN_HK = D_MODEL // GDN_DK
GDN_HV = 2 * GDN_HK
GDN_QK_W = GDN_HK * GDN_DK
GDN_V_W = GDN_HV * GDN_DV
GDN_CONV_DIM = 2 * GDN_QK_W + GDN_V_W
GDN_IN_W = GDN_CONV_DIM + GDN_V_W + 2 * GDN_HV
GDN_CONV_K = 4
GDN_CHUNK = 64
SCONV_K = 3
RMS_EPS = 1e-6
LN_EPS = 1e-5
L2_EPS = 1e-6

kernel_name = 'hybrid_conformer_swa_gdn_shortconv_adaln_step'


def rmsnorm(x, g):
    xf = x.astype(jnp.float32)
    y = xf * lax.rsqrt(jnp.mean(xf * xf, axis=-1, keepdims=True) + RMS_EPS)
    return (y * g.astype(jnp.float32)).astype(x.dtype)


def modulate(h, shift, scale):
    return h * (1 + scale[:, None, :]) + shift[:, None, :]


def l2norm(x):
    return x * lax.rsqrt(jnp.sum(x * x, axis=-1, keepdims=True) + L2_EPS)


def partial_rope(x, pos):
    half = ROT_DIM // 2
    inv = jnp.power(jnp.float32(ROPE_THETA), -jnp.arange(half, dtype=jnp.float32) * 2.0 / ROT_DIM)
    ang = pos.astype(jnp.float32)[:, None] * inv[None, :]
    cos = jnp.cos(ang)[None, :, None, :]
    sin = jnp.sin(ang)[None, :, None, :]
    xf = x.astype(jnp.float32)
    x1, x2 = xf[..., :half], xf[..., half:ROT_DIM]
    out = jnp.concatenate([x1 * cos - x2 * sin, x2 * cos + x1 * sin, xf[..., ROT_DIM:]], axis=-1)
    return out.astype(x.dtype)


def causal_dwconv(u, buf, w):
    k = w.shape[0]
    full = jnp.concatenate([buf.astype(u.dtype), u], axis=1)
    y = lax.conv_general_dilated(full, w.astype(u.dtype)[:, None, :], window_strides=(1,),
                                 padding='VALID', dimension_numbers=('NWC', 'WIO', 'NWC'),
                                 feature_group_count=u.shape[-1])
    return y, full[:, full.shape[1] - (k - 1):]


def conformer_conv(h, buf, w_pw1, b_pw1, w_dw, b_dw, ln_g, ln_b, w_pw2, b_pw2):
    a, gate = jnp.split(h @ w_pw1 + b_pw1, 2, axis=-1)
    u = a * jax.nn.sigmoid(gate)
    y, new_buf = causal_dwconv(u, buf, w_dw)
    yf = (y + b_dw).astype(jnp.float32)
    mu = jnp.mean(yf, axis=-1, keepdims=True)
    var = jnp.mean(jnp.square(yf - mu), axis=-1, keepdims=True)
    yn = ((yf - mu) * lax.rsqrt(var + LN_EPS) * ln_g.astype(jnp.float32)
          + ln_b.astype(jnp.float32)).astype(h.dtype)
    return jax.nn.silu(yn) @ w_pw2 + b_pw2, new_buf


def sink_attention(q, k, v, mask, sinks):
    s = jnp.einsum('bnqkgd,bnskd->bnkgqs', q, k).astype(jnp.float32) * (HEAD_DIM ** -0.5)
    s = jnp.where(mask[None, :, None, None], s, -jnp.inf)
    sink = sinks.astype(jnp.float32)[None, None, :, :, None, None]
    m = jnp.maximum(jnp.max(s, axis=-1, keepdims=True), sink)
    p = jnp.exp(s - m)
    p = p / (jnp.sum(p, axis=-1, keepdims=True) + jnp.exp(sink - m))
    return jnp.einsum('bnkgqs,bnskd->bnqkgd', p.astype(v.dtype), v)


def swa_mixer(h, pos, kbuf, vbuf, w_qkv, w_o, sinks, prompt):
    B, T, _ = h.shape
    q, k, v = jnp.split(h @ w_qkv, [N_HEADS * HEAD_DIM, (N_HEADS + N_KV) * HEAD_DIM], axis=-1)
    q = partial_rope(q.reshape(B, T, N_HEADS, HEAD_DIM), pos)
    k = partial_rope(k.reshape(B, T, N_KV, HEAD_DIM), pos)
    v = v.reshape(B, T, N_KV, HEAD_DIM)
    kk = jnp.concatenate([kbuf.astype(k.dtype), k], axis=1)
    vv = jnp.concatenate([vbuf.astype(v.dtype), v], axis=1)
    sink = sinks.reshape(N_KV, GQA_GROUP)
    if prompt:
        nb = T // WINDOW
        kb = kk.reshape(B, nb + 1, WINDOW, N_KV, HEAD_DIM)
        vb = vv.reshape(B, nb + 1, WINDOW, N_KV, HEAD_DIM)
        kb = jnp.concatenate([kb[:, :-1], kb[:, 1:]], axis=2)
        vb = jnp.concatenate([vb[:, :-1], vb[:, 1:]], axis=2)
        qb = q.reshape(B, nb, WINDOW, N_KV, GQA_GROUP, HEAD_DIM)
        qi = jnp.arange(WINDOW)[:, None]
        kj = jnp.arange(2 * WINDOW)[None, :]
        band = (kj >= qi) & (kj <= qi + WINDOW)
        key_ok = (jnp.arange(nb)[:, None] * WINDOW + kj) >= WINDOW
        mask = band[None] & key_ok[:, None, :]
    else:
        kb, vb = kk[:, None], vv[:, None]
        qb = q.reshape(B, 1, T, N_KV, GQA_GROUP, HEAD_DIM)
        qi = jnp.arange(T)[:, None]
        kj = jnp.arange(WINDOW + T)[None, :]
        mask = ((kj >= qi) & (kj <= qi + WINDOW))[None]
    o = sink_attention(qb, kb, vb, mask, sink).reshape(B, T, N_HEADS * HEAD_DIM)
    return o @ w_o, kk[:, T:], vv[:, T:]


def gdn_chunked(q, k, v, g, beta, S0):
    B, T, H, _ = q.shape
    DV = v.shape[-1]
    C = GDN_CHUNK
    N = T // C

    def blk(a):
        return jnp.moveaxis(a.reshape((B, N, C, H) + a.shape[3:]), 3, 2)

    q, k, v, g, beta = blk(q), blk(k), blk(v), blk(g), blk(beta)
    gc = jnp.cumsum(g, axis=-1)
    lower = jnp.tril(jnp.ones((C, C), dtype=bool))
    strict = jnp.tril(jnp.ones((C, C), dtype=bool), -1)
    decay = jnp.exp(jnp.where(lower, gc[..., :, None] - gc[..., None, :], -jnp.inf))
    kb = k * beta[..., None]
    vb = v * beta[..., None]
    L = jnp.where(strict, jnp.einsum('bnhcd,bnhsd->bnhcs', kb, k) * decay, 0.0)
    eye = jnp.eye(C, dtype=jnp.float32)
    A = L + eye
    Tm = lax.linalg.triangular_solve(A, jnp.broadcast_to(eye, A.shape), left_side=True, lower=True)
    u = Tm @ vb
    w = Tm @ (kb * jnp.exp(gc)[..., None])
    attn = jnp.where(lower, jnp.einsum('bnhcd,bnhsd->bnhcs', q, k) * decay, 0.0)

    def step(S, xs):
        q_n, k_n, u_n, w_n, gc_n, attn_n = xs
        v_new = u_n - w_n @ S
        o = (q_n * jnp.exp(gc_n)[..., None]) @ S + attn_n @ v_new
        g_last = gc_n[..., -1]
        S = S * jnp.exp(g_last)[..., None, None] + jnp.einsum(
            'bhck,bhcv->bhkv', k_n * jnp.exp(g_last[..., None] - gc_n)[..., None], v_new)
        return S, o

    xs = tuple(jnp.moveaxis(a, 1, 0) for a in (q, k, u, w, gc, attn))
    S, o = lax.scan(step, S0, xs)
    o = jnp.moveaxis(jnp.moveaxis(o, 0, 1), 3, 2).reshape(B, T, H, DV)
    return o, S


def gdn_recurrent(q, k, v, g, beta, S0):
    def step(S, xs):
        q_t, k_t, v_t, g_t, b_t = xs
        S = S * jnp.exp(g_t)[..., None, None]
        kv = jnp.einsum('bhk,bhkv->bhv', k_t, S)
        S = S + jnp.einsum('bhk,bhv->bhkv', k_t, (v_t - kv) * b_t[..., None])
        return S, jnp.einsum('bhk,bhkv->bhv', q_t, S)

    xs = tuple(jnp.swapaxes(a, 0, 1) for a in (q, k, v, g, beta))
    S, o = lax.scan(step, S0, xs)
    return jnp.swapaxes(o, 0, 1), S


def gdn_mixer(h, S0, cbuf, w_in, w_conv, a_log, dt_bias, norm_g, w_o, prompt):
    B, T, _ = h.shape
    f32 = jnp.float32
    qkv, z, b, a = jnp.split(h @ w_in, [GDN_CONV_DIM, GDN_CONV_DIM + GDN_V_W,
                                        GDN_CONV_DIM + GDN_V_W + GDN_HV], axis=-1)
    qkv, new_cbuf = causal_dwconv(qkv, cbuf, w_conv)
    qkv = jax.nn.silu(qkv).astype(f32)
    q, k, v = jnp.split(qkv, [GDN_QK_W, 2 * GDN_QK_W], axis=-1)
    rep = GDN_HV // GDN_HK
    q = jnp.repeat(l2norm(q.reshape(B, T, GDN_HK, GDN_DK)), rep, axis=2) * (GDN_DK ** -0.5)
    k = jnp.repeat(l2norm(k.reshape(B, T, GDN_HK, GDN_DK)), rep, axis=2)
    v = v.reshape(B, T, GDN_HV, GDN_DV)
    beta = jax.nn.sigmoid(b.astype(f32))
    g = -jnp.exp(a_log.astype(f32)) * jax.nn.softplus(a.astype(f32) + dt_bias.astype(f32))
    if prompt:
        o, S = gdn_chunked(q, k, v, g, beta, S0.astype(f32))
    else:
        o, S = gdn_recurrent(q, k, v, g, beta, S0.astype(f32))
    o = rmsnorm(o, norm_g) * jax.nn.silu(z.reshape(B, T, GDN_HV, GDN_DV).astype(f32))
    o = o.astype(h.dtype).reshape(B, T, GDN_V_W)
    return o @ w_o, S.astype(h.dtype), new_cbuf


def sconv_mixer(h, buf, w_in, w_conv, w_out):
    gate_b, gate_c, hin = jnp.split(h @ w_in, 3, axis=-1)
    y, new_buf = causal_dwconv(gate_c * hin, buf, w_conv)
    return (gate_b * y) @ w_out, new_buf


def sq_relu_mlp(h, w_up, w_down):
    return jnp.square(jax.nn.relu(h @ w_up)) @ w_down


def trunk(x, c, pos, st, P, prompt):
    new = {'conf': [], 'k': [], 'v': [], 'ssm': [], 'gconv': [], 'sconv': []}
    cm = jax.nn.silu(c)
    for i in range(DEPTH):
        m, j = i % N_MIXERS, i // N_MIXERS
        sh1, sc1, g1, sh2, sc2, g2 = jnp.split(cm @ P['w_ada'][i] + P['b_ada'][i], 6, axis=-1)
        h = modulate(rmsnorm(x, P['norm_mix'][i]), sh1, sc1)
        if m == 0:
            out, nb = conformer_conv(h, st['conf'][j], P['conf_w_pw1'][j], P['conf_b_pw1'][j],
                                     P['conf_w_dw'][j], P['conf_b_dw'][j], P['conf_ln_g'][j],
                                     P['conf_ln_b'][j], P['conf_w_pw2'][j], P['conf_b_pw2'][j])
            new['conf'].append(nb)
        elif m == 1:
            out, nk, nv = swa_mixer(h, pos, st['k'][j], st['v'][j], P['swa_w_qkv'][j],
                                    P['swa_w_o'][j], P['swa_sinks'][j], prompt)
            new['k'].append(nk)
            new['v'].append(nv)
        elif m == 2:
            out, ns, nc = gdn_mixer(h, st['ssm'][j], st['gconv'][j], P['gdn_w_in'][j],
                                    P['gdn_w_conv'][j], P['gdn_a_log'][j], P['gdn_dt_bias'][j],
                                    P['gdn_norm'][j], P['gdn_w_o'][j], prompt)
            new['ssm'].append(ns)
            new['gconv'].append(nc)
        else:
            out, nb = sconv_mixer(h, st['sconv'][j], P['sconv_w_in'][j], P['sconv_w_conv'][j],
                                  P['sconv_w_out'][j])
            new['sconv'].append(nb)
        x = x + g1[:, None, :] * out
        h = modulate(rmsnorm(x, P['norm_mlp'][i]), sh2, sc2)
        x = x + g2[:, None, :] * sq_relu_mlp(h, P['w_up'][i], P['w_down'][i])
    stacked = {name: jnp.stack(rows) for name, rows in new.items()}
    return rmsnorm(x, P['norm_final']), stacked


def setup_inputs(seed: int = 0) -> dict:
    key = jax.random.key(seed)
    ks = list(jax.random.split(key, 48))
    f32 = jnp.float32
    d = D_MODEL

    def nrm(shape, scale):
        return jax.random.normal(ks.pop(), shape, f32) * scale

    def gain(shape):
        return 1.0 + nrm(shape, 0.02)

    a_log = jnp.log(jax.random.uniform(ks.pop(), (N_L_GDN, GDN_HV), f32, 1.0, 16.0))
    dt = jnp.exp(jax.random.uniform(ks.pop(), (N_L_GDN, GDN_HV), f32,
                                    math.log(1e-3), math.log(1e-1)))
    dt_bias = dt + jnp.log(-jnp.expm1(-dt))
    return {
        'x_prompt': nrm((BATCH, SEQ, d), 1.0),
        'x_sample': nrm((DEC_BATCH, DEC_SEQ, d), 1.0),
        'c_prompt': nrm((BATCH, d), 1.0),
        'c_sample': nrm((DEC_BATCH, d), 1.0),
        'state_conf_conv': nrm((N_L_CONF, DEC_BATCH, CONF_K - 1, d), 0.5),
        'cache_swa_k': nrm((N_L_SWA, DEC_BATCH, WINDOW, N_KV, HEAD_DIM), 1.0),
        'cache_swa_v': nrm((N_L_SWA, DEC_BATCH, WINDOW, N_KV, HEAD_DIM), 1.0),
        'state_gdn_ssm': nrm((N_L_GDN, DEC_BATCH, GDN_HV, GDN_DK, GDN_DV), 0.1),
        'state_gdn_conv': nrm((N_L_GDN, DEC_BATCH, GDN_CONV_K - 1, GDN_CONV_DIM), 1.0),
        'state_sconv': nrm((N_L_SCONV, DEC_BATCH, SCONV_K - 1, d), 0.5),
        'w_ada': nrm((DEPTH, d, 6 * d), 0.5 * d ** -0.5),
        'b_ada': nrm((DEPTH, 6 * d), 0.02),
        'norm_mix': gain((DEPTH, d)),
        'norm_mlp': gain((DEPTH, d)),
        'w_up': nrm((DEPTH, d, D_FF), d ** -0.5),
        'w_down': nrm((DEPTH, D_FF, d), D_FF ** -0.5),
        'norm_final': gain((d,)),
        'conf_w_pw1': nrm((N_L_CONF, d, 2 * d), d ** -0.5),
        'conf_b_pw1': nrm((N_L_CONF, 2 * d), 0.02),
        'conf_w_dw': nrm((N_L_CONF, CONF_K, d), CONF_K ** -0.5),
        'conf_b_dw': nrm((N_L_CONF, d), 0.02),
        'conf_ln_g': gain((N_L_CONF, d)),
        'conf_ln_b': nrm((N_L_CONF, d), 0.02),
        'conf_w_pw2': nrm((N_L_CONF, d, d), d ** -0.5),
        'conf_b_pw2': nrm((N_L_CONF, d), 0.02),
        'swa_w_qkv': nrm((N_L_SWA, d, (N_HEADS + 2 * N_KV) * HEAD_DIM), d ** -0.5),
        'swa_w_o': nrm((N_L_SWA, N_HEADS * HEAD_DIM, d), (N_HEADS * HEAD_DIM) ** -0.5),
        'swa_sinks': nrm((N_L_SWA, N_HEADS), 0.5),
        'gdn_w_in': nrm((N_L_GDN, d, GDN_IN_W), d ** -0.5),
        'gdn_w_conv': nrm((N_L_GDN, GDN_CONV_K, GDN_CONV_DIM), GDN_CONV_K ** -0.5),
        'gdn_a_log': a_log,
        'gdn_dt_bias': dt_bias,
        'gdn_norm': gain((N_L_GDN, GDN_DV)),
        'gdn_w_o': nrm((N_L_GDN, GDN_V_W, d), GDN_V_W ** -0.5),
        'sconv_w_in': nrm((N_L_SCONV, d, 3 * d), d ** -0.5),
        'sconv_w_conv': nrm((N_L_SCONV, SCONV_K, d), SCONV_K ** -0.5),
        'sconv_w_out': nrm((N_L_SCONV, d, d), d ** -0.5),
    }


def reference(x_prompt, x_sample, c_prompt, c_sample, state_conf_conv, cache_swa_k, cache_swa_v,
              state_gdn_ssm, state_gdn_conv, state_sconv, w_ada, b_ada, norm_mix, norm_mlp,
              w_up, w_down, norm_final, conf_w_pw1, conf_b_pw1, conf_w_dw, conf_b_dw, conf_ln_g,
              conf_ln_b, conf_w_pw2, conf_b_pw2, swa_w_qkv, swa_w_o, swa_sinks, gdn_w_in,
              gdn_w_conv, gdn_a_log, gdn_dt_bias, gdn_norm, gdn_w_o, sconv_w_in, sconv_w_conv,
              sconv_w_out):
    P = {'w_ada': w_ada, 'b_ada': b_ada, 'norm_mix': norm_mix, 'norm_mlp': norm_mlp,
         'w_up': w_up, 'w_down': w_down, 'norm_final': norm_final,
         'conf_w_pw1': conf_w_pw1, 'conf_b_pw1': conf_b_pw1, 'conf_w_dw': conf_w_dw,
         'conf_b_dw': conf_b_dw, 'conf_ln_g': conf_ln_g, 'conf_ln_b': conf_ln_b,
         'conf_w_pw2': conf_w_pw2, 'conf_b_pw2': conf_b_pw2,
         'swa_w_qkv': swa_w_qkv, 'swa_w_o': swa_w_o, 'swa_sinks': swa_sinks,
         'gdn_w_in': gdn_w_in, 'gdn_w_conv': gdn_w_conv, 'gdn_a_log': gdn_a_log,
         'gdn_dt_bias': gdn_dt_bias, 'gdn_norm': gdn_norm, 'gdn_w_o': gdn_w_o,
         'sconv_w_in': sconv_w_in, 'sconv_w_conv': sconv_w_conv, 'sconv_w_out': sconv_w_out}
    bp, dt = x_prompt.shape[0], x_prompt.dtype
    prompt_state = {
        'conf': jnp.zeros((N_L_CONF, bp, CONF_K - 1, D_MODEL), dt),
        'k': jnp.zeros((N_L_SWA, bp, WINDOW, N_KV, HEAD_DIM), dt),
        'v': jnp.zeros((N_L_SWA, bp, WINDOW, N_KV, HEAD_DIM), dt),
        'ssm': jnp.zeros((N_L_GDN, bp, GDN_HV, GDN_DK, GDN_DV), dt),
        'gconv': jnp.zeros((N_L_GDN, bp, GDN_CONV_K - 1, GDN_CONV_DIM), dt),
        'sconv': jnp.zeros((N_L_SCONV, bp, SCONV_K - 1, D_MODEL), dt)}
    sample_state = {'conf': state_conf_conv, 'k': cache_swa_k, 'v': cache_swa_v,
                    'ssm': state_gdn_ssm, 'gconv': state_gdn_conv, 'sconv': state_sconv}
    pos_p = jnp.arange(x_prompt.shape[1], dtype=jnp.int32)
    pos_s = PAST_LEN + jnp.arange(x_sample.shape[1], dtype=jnp.int32)
    y_prompt, sp = trunk(x_prompt, c_prompt, pos_p, prompt_state, P, True)
    y_sample, ss = trunk(x_sample, c_sample, pos_s, sample_state, P, False)
    return (y_prompt, y_sample, sp['conf'], ss['conf'], sp['k'], ss['k'], sp['v'], ss['v'],
            sp['ssm'], ss['ssm'], sp['gconv'], ss['gconv'], sp['sconv'], ss['sconv'])
```

```python
import numpy as np
import ml_dtypes
import concourse.bass as bass
import concourse.mybir as mybir
from concourse.bass_utils import run_bass_kernel_spmd

F32 = mybir.dt.float32
BF16 = mybir.dt.bfloat16
AF = mybir.ActivationFunctionType
ALU = mybir.AluOpType
AX = mybir.AxisListType

CFG = {"NTI": 4, "LAYERS": 4, "MIX": True, "DBG": False}
D = 1024
NCH = 8
NS = 16
TW = 512
ENG = ("pe", "act", "dve", "pool", "sp")
PAIRS = [[0, 1], [2, 3], [4, 5], [6, 7]]


class Buf:
    __slots__ = ("ap", "w", "r")

    def __init__(self, ap, share=None):
        self.ap = ap
        if share is None:
            self.w = {}
            self.r = {}
        else:
            self.w = share.w
            self.r = share.r


class Prog:
    def __init__(self, nc):
        self.nc = nc
        self.ops = {e: [] for e in ENG}
        self.cnt = {}
        self.sems = {}
        self.waited = {}
        for e in ("pe", "act", "dve", "pool"):
            self.sems[e] = nc.alloc_semaphore("c_" + e)
            self.cnt[e] = 0
        self.dkeys = []
        self.ninstr = 0
        self.names = {}

    def _line(self):
        if not CFG.get("DBG"):
            return 0
        import sys
        f = sys._getframe(2)
        out = []
        while f is not None and len(out) < 4:
            if f.f_code.co_name not in ("op", "ACT", "TT", "STT", "TS", "CP", "MM", "mm_group", "dma"):
                out.append(f.f_lineno)
            f = f.f_back
        return out

    def dkey(self, key):
        if key not in self.sems:
            self.sems[key] = self.nc.alloc_semaphore("d_" + key)
            self.cnt[key] = 0
            self.dkeys.append(key)
        return key

    def _wait(self, eng, k, n):
        if k == "pe" and eng == "pe":
            return
        if self.waited.get((eng, k), 0) >= n:
            return
        self.waited[(eng, k)] = n
        self.ops[eng].append(("w", self.sems[k], n))
        self.ninstr += 1

    def _deps(self, eng, R, W):
        deps = {}
        for b in R:
            for k, n in b.w.items():
                if deps.get(k, 0) < n:
                    deps[k] = n
        for b in W:
            for k, n in b.w.items():
                if deps.get(k, 0) < n:
                    deps[k] = n
            for k, n in b.r.items():
                if deps.get(k, 0) < n:
                    deps[k] = n
        for k, n in deps.items():
            self._wait(eng, k, n)

    def op(self, eng, fn, R=(), W=(), signal=True):
        self._deps(eng, R, W)
        if signal:
            self.cnt[eng] += 1
            tag = self.cnt[eng]
        else:
            tag = self.cnt[eng] + 1
        self.ops[eng].append(("i", fn, self.sems[eng] if signal else None, 1, self._line()))
        self.ninstr += 1
        for b in R:
            if b.r.get(eng, 0) < tag:
                b.r[eng] = tag
        for b in W:
            if b.w.get(eng, 0) < tag:
                b.w[eng] = tag

    def dma(self, q, out_ap, in_ap, key, R=(), W=()):
        self.dkey(key)
        self._deps(q, R, W)
        self.cnt[key] += 16
        tag = self.cnt[key]
        self.ops[q].append(("i", ("dma_start", dict(out=out_ap, in_=in_ap)), self.sems[key], 16))
        self.ninstr += 1
        for b in R:
            if b.r.get(key, 0) < tag:
                b.r[key] = tag
        for b in W:
            if b.w.get(key, 0) < tag:
                b.w[key] = tag

    def collective(self, ins_ap, outs_ap, key, R=(), W=()):
        self.dkey(key)
        self._deps("pool", R, W)
        self.cnt[key] += 1
        tag = self.cnt[key]

        def fn(e):
            return e.collective_compute("AllGather", ALU.bypass, replica_groups=PAIRS, ins=[ins_ap], outs=[outs_ap])
        self.ops["pool"].append(("c", fn, self.sems[key]))
        self.ninstr += 1
        for b in R:
            b.r[key] = tag
        for b in W:
            b.w[key] = tag

    def barrier(self):
        for e in ENG:
            for k in self.sems:
                if self.cnt[k] > 0:
                    self._wait(e, k, self.cnt[k])

    def replay(self, e, name):
        for it in self.ops[name]:
            if it[0] == "w":
                e.wait_ge(it[1], it[2])
            elif it[0] == "c":
                it[1](e).then_inc(it[2])
            else:
                f = it[1]
                if CFG.get("DBG") and len(it) > 4:
                    self.names[self.nc.get_next_instruction_name()] = it[4]
                if isinstance(f, tuple):
                    ins = getattr(e, f[0])(**f[1])
                else:
                    ins = f(e)
                if it[2] is not None:
                    ins.then_inc(it[2], it[3])


class Arena:
    def __init__(self, P, nwords, name):
        self.P = P
        self.t = P.nc.alloc_sbuf_tensor(name, [128, nwords], F32).ap()
        self.n = nwords
        self.off = 0

    def alloc(self, shape, dt=F32, name=None):
        if name is not None:
            self.P.names["buf:" + name] = (self.off, list(shape), "f32" if dt == F32 else "bf16")
        n = 1
        for s in shape:
            n *= s
        words = n if dt == F32 else (n + 1) // 2
        assert self.off + words <= self.n, ("arena overflow", self.off, words, self.n)
        ap = self.t[:, self.off:self.off + words]
        self.off += words
        if dt != F32:
            ap = ap.bitcast(dt)
            if n % 2:
                ap = ap[:, 0:n]
        if len(shape) == 2:
            ap = ap.rearrange("p (a b) -> p a b", a=shape[0])
        elif len(shape) == 3:
            ap = ap.rearrange("p (a b c) -> p a b c", a=shape[0], b=shape[1])
        return Buf(ap)

    def reset(self):
        self.P.barrier()
        self.off = 0


class WStream:
    def __init__(self, P, nslots):
        self.P = P
        self.slots = []
        for i in range(nslots):
            t = P.nc.alloc_sbuf_tensor("slab%d" % i, [128, 8, 1024], BF16).ap()
            self.slots.append(Buf(t))
        self.free = list(range(nslots))
        self.specs = []
        self.nxt = 0
        self.loaded = []

    def start(self, specs):
        self.specs = specs
        while self.free and self.nxt < len(self.specs):
            self._load()

    def _load(self):
        slot = self.free.pop(0)
        name, pieces = self.specs[self.nxt]
        b = self.slots[slot]
        for (src, col0, n) in pieces:
            self.P.dma("pool", b.ap[:, :, col0:col0 + n], src.rearrange("(kc p) n -> p kc n", p=128),
                       "slab%d" % slot, W=[b])
        self.loaded.append((name, slot))
        self.nxt += 1

    def acquire(self, name):
        nm, slot = self.loaded.pop(0)
        assert nm == name, (nm, name)
        return self.slots[slot], slot

    def release(self, slot):
        self.free.append(slot)
        if self.nxt < len(self.specs):
            self._load()


def mm_group(P, out_ap, pairs, R, W):
    n = len(pairs)
    for i, (l, r) in enumerate(pairs):
        P.op("pe", ("matmul", dict(out=out_ap, lhsT=l, rhs=r, start=(i == 0), stop=(i == n - 1))),
             R=R, W=W, signal=(i == n - 1))


class VecPack:
    def __init__(self):
        self.cols = []
        self.idx = {}
        self.n = 0

    def add(self, name, arr):
        arr = np.ascontiguousarray(arr, dtype=np.float32)
        assert arr.shape[0] == 128
        self.idx[name] = (self.n, arr.shape[1])
        self.cols.append(arr)
        self.n += arr.shape[1]

    def build(self):
        return np.concatenate(self.cols, axis=1)


def fm(v):
    v = np.asarray(v, dtype=np.float32)
    return np.ascontiguousarray(v.reshape(-1, 128).T)


def vec_layout(inp=None, hf=0):
    vp = VecPack()
    z = (lambda *s: np.zeros(s, np.float32))
    g = (lambda k, *s: (np.asarray(inp[k], np.float32) if inp is not None else np.zeros(s, np.float32)))
    for i in range(4):
        vp.add("b_ada%d" % i, fm(g("b_ada", 4, 6144)[i]))
        vp.add("norm_mix%d" % i, fm(g("norm_mix", 4, 1024)[i]))
        vp.add("norm_mlp%d" % i, fm(g("norm_mlp", 4, 1024)[i]))
    vp.add("norm_final", fm(g("norm_final", 1024)))
    vp.add("conf_b_pw1", fm(g("conf_b_pw1", 1, 2048)[0]))
    vp.add("conf_b_dw", fm(g("conf_b_dw", 1, 1024)[0]))
    vp.add("conf_ln_g", fm(g("conf_ln_g", 1, 1024)[0]))
    vp.add("conf_ln_b", fm(g("conf_ln_b", 1, 1024)[0]))
    vp.add("conf_b_pw2", fm(g("conf_b_pw2", 1, 1024)[0]))
    wdw = g("conf_w_dw", 1, 31, 1024)[0]
    vp.add("conf_w_dw", np.ascontiguousarray(wdw.T.reshape(8, 128, 31).transpose(1, 0, 2)).reshape(128, 248))
    wsc = g("sconv_w_conv", 1, 3, 1024)[0]
    vp.add("sconv_w_conv", np.ascontiguousarray(wsc.T.reshape(8, 128, 3).transpose(1, 0, 2)).reshape(128, 24))
    sinks = g("swa_sinks", 1, 16)[0]
    es = np.zeros((128, 8), np.float32)
    for c in range(8):
        es[0:64, c] = sinks[2 * c]
        es[64:128, c] = sinks[2 * c + 1]
    vp.add("swa_sink", es)
    wc = g("gdn_w_conv", 1, 4, 4096)[0]
    chans = gdn_channels(hf)
    wcc = wc[:, chans]
    vp.add("gdn_wconv", np.ascontiguousarray(wcc.T.reshape(16, 128, 4).transpose(1, 0, 2)).reshape(128, 64))
    hs_ = slice(8 * hf, 8 * hf + 8)
    vp.add("gdn_dtb", np.tile(g("gdn_dt_bias", 1, 16)[0][hs_][None, :], (128, 1)))
    vp.add("gdn_alog", np.tile(g("gdn_a_log", 1, 16)[0][hs_][None, :], (128, 1)))
    vp.add("gdn_norm", np.tile(g("gdn_norm", 1, 128)[0][None, :], (128, 1)))
    vp.add("one", np.ones((128, 1), np.float32))
    hm = np.zeros((128, 2), np.float32)
    hm[0:64, 0] = 1.0
    hm[64:128, 1] = 1.0
    vp.add("hmask", hm)
    vp.add("eps_rms", np.full((128, 1), 1e-6, np.float32))
    vp.add("eps_ln", np.full((128, 1), 1e-5, np.float32))
    return vp


def gdn_channels(hf):
    q = np.concatenate([np.arange(kh * 128, (kh + 1) * 128) for kh in range(4 * hf, 4 * hf + 4)])
    k = 1024 + q
    v = 2048 + np.concatenate([np.arange(h * 128, (h + 1) * 128) for h in range(8 * hf, 8 * hf + 8)])
    return np.concatenate([q, k, v])


def const_mats():
    i = np.arange(128)
    ident = np.eye(128, dtype=np.float32)
    triLE = (i[:, None] <= i[None, :]).astype(np.float32)
    triGE = (i[:, None] >= i[None, :]).astype(np.float32)
    triGT = (i[:, None] > i[None, :]).astype(np.float32)
    triLT = (i[:, None] < i[None, :]).astype(np.float32)
    half = ((i[:, None] // 64) == (i[None, :] // 64)).astype(np.float32)
    RT = np.zeros((128, 128), np.float32)
    for hb in (0, 64):
        for d in range(8):
            RT[hb + d + 8, hb + d] = -1.0
            RT[hb + d, hb + d + 8] = 1.0
    out = {"ident": ident, "triLE": triLE, "triGE": triGE, "triGT": triGT, "triLT": triLT, "half": half, "ropeRT": RT}
    out["bd8"] = ((i[:, None] // 8) == (i[None, :] // 8)).astype(np.float32)
    for b in (8, 16, 32, 64):
        out["cm%d" % b] = (((i[:, None] // (2 * b)) == (i[None, :] // (2 * b))) & ((i[:, None] // b) != (i[None, :] // b))).astype(np.float32)
    return out


CM_NAMES = ["ident", "triLE", "triGE", "triGT", "triLT", "half", "ropeRT", "bd8", "cm8", "cm16", "cm32", "cm64"]
NCMB = 7


def build_program():
    NTI = CFG["NTI"]
    TP = NTI * TW
    NTOK = TP + NS
    nc = bass.Bass("TRN2", target_bir_lowering=False)
    P = Prog(nc)
    vidx = vec_layout(None)
    NV = vidx.n

    def din(name, shape, dt=F32):
        return nc.dram_tensor(name, list(shape), dt, kind="ExternalInput").ap()

    def dout(name, shape, dt=F32):
        return nc.dram_tensor(name, list(shape), dt, kind="ExternalOutput").ap()

    xT_d = din("xT", [D, NTOK])
    cT_d = din("cT", [D, 17])
    vecs_d = din("vecs", [128, NV])
    cm_d = din("cmats", [128, len(CM_NAMES) * 128])
    W = {}
    W["w_ada"] = din("w_ada", [4, D, 6144])
    W["w_up"] = din("w_up", [4, D, 4096])
    W["w_down"] = din("w_down", [4, 4096, D])
    W["conf_w_pw1"] = din("conf_w_pw1", [1, D, 2048])
    W["conf_w_pw2"] = din("conf_w_pw2", [1, D, D])
    xh_d = din("xhalo", [D, 32])
    flag_d = din("flag", [128, 2])
    stconfT_d = din("st_confT", [D, NS, 30])
    stconf_d = din("st_conf", [NS, 30, D])
    W["swa_w_qkv"] = din("swa_w_qkv", [1, D, 1536])
    W["swa_w_o"] = din("swa_w_o", [1, D, D])
    ropeC_d = din("ropeC", [128, NTOK])
    ropeS_d = din("ropeS", [128, NTOK])
    ckT_d = din("ckT", [128, NS, 4, 128])
    ck_d = din("ck", [NS, 128, 256])
    cv_d = din("cv", [NS, 128, 256])
    swah_src = nc.dram_tensor("swah_src", [128, 768], F32)
    swah_dst = nc.dram_tensor("swah_dst", [256, 768], F32)
    kp_d = dout("swa_kp", [128, 4, 128])
    vp_d = dout("swa_vp", [128, 256])
    ks_d = dout("swa_ks", [NS, 128, 256])
    vs_d = dout("swa_vs", [NS, 128, 256])
    TG = 2 * NTOK
    W["gdn_w_in"] = din("gdn_w_in_c", [D, 3072])
    W["gdn_ba"] = din("gdn_ba_c", [D, 16])
    W["gdn_w_o"] = din("gdn_w_o", [1, 2048, D])
    ssm_d = din("g_ssm", [2 * NS, 8, 128, 128])
    gcvT_d = din("g_convT", [2048, 2 * NS, 4])
    gcv_d = din("g_conv", [2 * NS, 3, 2048])
    tws = [TW] * NTI + [NS]
    gh_src = [nc.dram_tensor("gh_src%d" % i, [D, w_], F32) for i, w_ in enumerate(tws)]
    gh_dst = [nc.dram_tensor("gh_dst%d" % i, [2 * D, w_], F32) for i, w_ in enumerate(tws)]
    gws = [TW] * (2 * NTI) + [2 * NS]
    go_src = [nc.dram_tensor("go_src%d" % i, [D, w_], F32) for i, w_ in enumerate(gws)]
    go_dst = [nc.dram_tensor("go_dst%d" % i, [2 * D, w_], F32) for i, w_ in enumerate(gws)]
    ssmp_d = dout("g_ssm_p", [8, 128, 128])
    ssms_d = dout("g_ssm_s", [2 * NS, 8, 128, 128])
    gcvp_d = dout("g_conv_p", [2048, 3])
    gcvs_d = dout("g_conv_s", [2 * NS, 3, 2048])
    W["sconv_w_in"] = din("sconv_w_in", [1, D, 3072])
    W["sconv_w_out"] = din("sconv_w_out", [1, D, D])
    stscT_d = din("st_scT", [D, NS, 2])
    stsc_d = din("st_sc", [NS, 2, D])
    sch_src = nc.dram_tensor("sch_src", [128, 16], F32)
    sch_dst = nc.dram_tensor("sch_dst", [256, 16], F32)
    scp_d = dout("sc_p", [D, 2])
    scs_d = dout("sc_s", [NS, 2, D])
    yT_d = dout("yT", [D, NTOK])
    confp_d = dout("conf_p", [D, 30])
    confs_d = dout("conf_s", [NS, 30, D])
    dbg_d = dout("dbg", [8, D, NTOK]) if CFG["DBG"] else None

    tiles = [(i * TW, TW, False) for i in range(NTI)] + [(TP, NS, True)]
    xt_all = nc.alloc_sbuf_tensor("sb_xT", [128, NCH, NTOK], F32).ap()
    xb = [Buf(xt_all[:, :, t0:t0 + w]) for (t0, w, _) in tiles]
    vecs = Buf(nc.alloc_sbuf_tensor("sb_vecs", [128, NV], F32).ap())
    cmat = Buf(nc.alloc_sbuf_tensor("sb_cmats", [128, len(CM_NAMES) * 128], F32).ap())
    cmatb = Buf(nc.alloc_sbuf_tensor("cmatsb", [128, NCMB * 128], BF16).ap())
    onesb = Buf(nc.alloc_sbuf_tensor("onesb", [128, 128], BF16).ap())
    onesf = Buf(nc.alloc_sbuf_tensor("onesf", [128, 128], F32).ap())
    mods = Buf(nc.alloc_sbuf_tensor("mods", [128, 48, 17], F32).ap())
    cmT = Buf(nc.alloc_sbuf_tensor("cmT", [128, NCH, 17], BF16).ap())
    cTf = Buf(nc.alloc_sbuf_tensor("cTf", [128, NCH, 17], F32).ap())
    flag = Buf(nc.alloc_sbuf_tensor("sb_flag", [128, 2], F32).ap())
    xh = Buf(nc.alloc_sbuf_tensor("sb_xh", [128, NCH, 32], F32).ap())
    psb = [Buf(nc.alloc_psum_tensor("ps%d" % i, [128, 512], F32).ap()) for i in range(8)]
    psi = [0]

    def ps():
        b = psb[psi[0] % 7]
        psi[0] += 1
        return b

    ws = WStream(P, 4)
    baw = Buf(nc.alloc_sbuf_tensor("sb_baw", [128, 8, 16], BF16).ap())
    arena = Arena(P, (nc.sbuf_bytes_remaining - 2048) // 4, "arena")
    print("[kernel] arena words", arena.n)

    def V(name, c0=0, n=None):
        o, k = vidx.idx[name]
        if n is None:
            n = k - c0
        return vecs.ap[:, o + c0:o + c0 + n]

    def CMf(name):
        i = CM_NAMES.index(name)
        return cmat.ap[:, i * 128:(i + 1) * 128]

    def CMb(name):
        i = CM_NAMES.index(name)
        return cmatb.ap[:, i * 128:(i + 1) * 128]

    specs = []
    for li in range(CFG["LAYERS"]):
        for g in range(6):
            specs.append(("ada%d_%d" % (li, g), [(W["w_ada"][li, :, g * 1024:(g + 1) * 1024], 0, 1024)]))
        if CFG["MIX"] and li == 0:
            w1 = W["conf_w_pw1"][0]
            specs.append(("pw1_0", [(w1[:, 0:512], 0, 512), (w1[:, 1024:1536], 512, 512)]))
            specs.append(("pw1_1", [(w1[:, 512:1024], 0, 512), (w1[:, 1536:2048], 512, 512)]))
            specs.append(("pw2", [(W["conf_w_pw2"][0], 0, 1024)]))
        if CFG["MIX"] and li == 1:
            wq = W["swa_w_qkv"][0]
            specs.append(("qkv_0", [(wq[:, 0:1024], 0, 1024)]))
            pcs = []
            for g in range(4):
                pcs.append((wq[:, 1024 + g * 64:1024 + (g + 1) * 64], g * 128, 64))
                pcs.append((wq[:, 1024 + g * 64:1024 + (g + 1) * 64], g * 128 + 64, 64))
            pcs.append((wq[:, 1280:1536], 512, 256))
            specs.append(("qkv_1", pcs))
            specs.append(("wo", [(W["swa_w_o"][0], 0, 1024)]))
        if CFG["MIX"] and li == 2:
            wi = W["gdn_w_in"]
            specs.append(("g_qk", [(wi[:, 0:1024], 0, 1024)]))
            specs.append(("g_v", [(wi[:, 1024:2048], 0, 1024)]))
            specs.append(("g_z", [(wi[:, 2048:3072], 0, 1024)]))
            specs.append(("g_wo0", [(W["gdn_w_o"][0, 0:1024, :], 0, 1024)]))
            specs.append(("g_wo1", [(W["gdn_w_o"][0, 1024:2048, :], 0, 1024)]))
        if CFG["MIX"] and li == 3:
            wsi = W["sconv_w_in"][0]
            specs.append(("sc_0", [(wsi[:, 1024:1536], 0, 512), (wsi[:, 2048:2560], 512, 512)]))
            specs.append(("sc_1", [(wsi[:, 1536:2048], 0, 512), (wsi[:, 2560:3072], 512, 512)]))
            specs.append(("sc_b", [(wsi[:, 0:1024], 0, 1024)]))
            specs.append(("sc_o", [(W["sconv_w_out"][0], 0, 1024)]))
        for f in range(4):
            specs.append(("up%d_%d" % (li, f), [(W["w_up"][li, :, f * 1024:(f + 1) * 1024], 0, 1024)]))
            specs.append(("down%d_%d" % (li, f), [(W["w_down"][li, f * 1024:(f + 1) * 1024, :], 0, 1024)]))

    def ACT(R, W, **kw):
        P.op("act", ("activation", kw), R=R, W=W)

    def TT(eng, R, W, **kw):
        P.op(eng, ("tensor_tensor", kw), R=R, W=W)

    def STT(eng, R, W, **kw):
        P.op(eng, ("scalar_tensor_tensor", kw), R=R, W=W)

    def TS(eng, R, W, **kw):
        P.op(eng, ("tensor_scalar", kw), R=R, W=W)

    def CP(eng, R, W, **kw):
        P.op(eng, ("tensor_copy", kw), R=R, W=W)

    def MM(out, lhsT, rhs, start, stop, R, W, signal=None):
        P.op("pe", ("matmul", dict(out=out, lhsT=lhsT, rhs=rhs, start=start, stop=stop)), R=R, W=W,
             signal=(stop if signal is None else signal))

    P.dma("sp", vecs.ap, vecs_d[:, :], "ld_vecs", W=[vecs])
    P.dma("sp", cmat.ap, cm_d[:, :], "ld_cm", W=[cmat])
    P.dma("sp", cTf.ap, cT_d.rearrange("(c p) s -> p c s", p=128), "ld_c", W=[cTf])
    for ti, (t0, w, _) in enumerate(tiles):
        P.dma("sp", xb[ti].ap, xT_d[:, t0:t0 + w].rearrange("(c p) t -> p c t", p=128), "ldx%d" % ti, W=[xb[ti]])
    P.dma("sp", flag.ap, flag_d[:, :], "ld_flag", W=[flag])
    P.dma("sp", xh.ap, xh_d.rearrange("(c p) t -> p c t", p=128), "ld_xh", W=[xh])
    P.dma("pool", baw.ap, W["gdn_ba"].rearrange("(kc p) n -> p kc n", p=128), "ld_baw", W=[baw])
    ws.start(specs)
    CP("dve", [cmat], [cmatb], out=cmatb.ap, in_=cmat.ap[:, 0:NCMB * 128])
    P.op("dve", ("memset", dict(ap=onesb.ap, constant=1.0)), W=[onesb])
    P.op("dve", ("memset", dict(ap=onesf.ap, constant=1.0)), W=[onesf])
    ACT([cTf], [cmT], out=cmT.ap, in_=cTf.ap, func=AF.Silu)

    def adaln(li):
        for g in range(6):
            sb, slot = ws.acquire("ada%d_%d" % (li, g))
            pb = ps()
            for n in range(8):
                mm_group(P, pb.ap[:, n * 17:(n + 1) * 17],
                         [(sb.ap[:, k, n * 128:(n + 1) * 128], cmT.ap[:, k, :]) for k in range(8)],
                         R=[sb, cmT], W=[pb])
            ws.release(slot)
            bo, _ = vidx.idx["b_ada%d" % li]
            TT("dve", [pb, vecs], [mods], out=mods.ap[:, g * 8:(g + 1) * 8, :],
               in0=pb.ap[:, 0:136].rearrange("p (n s) -> p n s", n=8),
               in1=vecs.ap[:, bo + g * 8:bo + (g + 1) * 8].unsqueeze(2).to_broadcast([128, 8, 17]), op=ALU.add)
        for g, nm in ((1, "norm_mix%d" % li), (4, "norm_mlp%d" % li)):
            STT("dve", [mods, vecs], [mods], out=mods.ap[:, g * 8:(g + 1) * 8, :], in0=mods.ap[:, g * 8:(g + 1) * 8, :],
                scalar=1.0, in1=V(nm).unsqueeze(2).to_broadcast([128, 8, 17]), op0=ALU.add, op1=ALU.mult)

    def colsum_rstd(srcs, w, eps_name, scratch, scale):
        pb = ps()
        n = len(srcs)
        for c, (sap, sbuf_) in enumerate(srcs):
            sq = scratch["sq"][c % 2]
            ACT([sbuf_], [sq], out=sq.ap[:, 0:w], in_=sap, func=AF.Square)
            MM(pb.ap[:, 0:w], onesb.ap, sq.ap[:, 0:w], c == 0, c == n - 1, R=[sq, onesb], W=[pb], signal=True)
        rs = scratch["rs"]
        ACT([pb, vecs], [rs], out=rs.ap[:, 0:w], in_=pb.ap[:, 0:w], func=AF.Ln, bias=V(eps_name), scale=scale)
        ACT([rs], [rs], out=rs.ap[:, 0:w], in_=rs.ap[:, 0:w], func=AF.Exp, scale=-0.5)
        return rs

    def norm_mod_src(xbuf, xap, w, smp, gA, gB, out_buf, scratch):
        rs = colsum_rstd([(xap[:, c, :], xbuf) for c in range(8)], w, "eps_rms", scratch, 1.0 / D)
        for c in range(8):
            tmp = scratch["tmp"][c % 2]
            TT("dve", [xbuf, rs], [tmp], out=tmp.ap[:, 0:w], in0=xap[:, c, :], in1=rs.ap[:, 0:w], op=ALU.mult)
            if not smp:
                ACT([tmp, mods], [out_buf], out=out_buf.ap[:, c, 0:w], in_=tmp.ap[:, 0:w], func=AF.Identity,
                    bias=mods.ap[:, gB * 8 + c, 0:1], scale=mods.ap[:, gA * 8 + c, 0:1])
            else:
                TT("dve", [tmp, mods], [tmp], out=tmp.ap[:, 0:w], in0=tmp.ap[:, 0:w], in1=mods.ap[:, gA * 8 + c, 1:17],
                   op=ALU.mult)
                TT("dve", [tmp, mods], [out_buf], out=out_buf.ap[:, c, 0:w], in0=tmp.ap[:, 0:w],
                   in1=mods.ap[:, gB * 8 + c, 1:17], op=ALU.add)

    def rstd_tile(ti, scratch):
        t0, w, _ = tiles[ti]
        return colsum_rstd([(xb[ti].ap[:, c, :], xb[ti]) for c in range(8)], w, "eps_rms", scratch, 1.0 / D)

    def norm_mod(ti, gA, gB, out_buf, scratch):
        t0, w, smp = tiles[ti]
        norm_mod_src(xb[ti], xb[ti].ap, w, smp, gA, gB, out_buf, scratch)

    def resid_add(ti, c, pb_ap, pbuf, gG, scratch, bias_ap=None, col0=0, w=None):
        t0, wt, smp = tiles[ti]
        if w is None:
            w = wt
        xs = xb[ti].ap[:, c, col0:col0 + w]
        src = pb_ap
        extraR = [pbuf]
        if bias_ap is not None:
            tb = scratch["tmp"][c % 2]
            ACT([pbuf, vecs], [tb], out=tb.ap[:, 0:w], in_=pb_ap, func=AF.Identity, bias=bias_ap)
            src = tb.ap[:, 0:w]
            extraR = [tb]
        if not smp:
            STT("dve", extraR + [mods, xb[ti]], [xb[ti]], out=xs, in0=src,
                scalar=mods.ap[:, gG * 8 + c, 0:1], in1=xs, op0=ALU.mult, op1=ALU.add)
        else:
            t2 = scratch["tmp2"]
            TT("dve", extraR + [mods], [t2], out=t2.ap[:, 0:w], in0=src, in1=mods.ap[:, gG * 8 + c, 1:17], op=ALU.mult)
            TT("dve", [t2, xb[ti]], [xb[ti]], out=xs, in0=xs, in1=t2.ap[:, 0:w], op=ALU.add)

    def to_token_major(srcs, sbufs, scratch_tok):
        n = len(srcs)
        for b0 in range(0, n, 4):
            pb = ps()
            for j in range(b0, min(n, b0 + 4)):
                MM(pb.ap[0:NS, (j - b0) * 128:(j - b0 + 1) * 128], srcs[j], CMf("ident"), True, True,
                   R=sbufs + [cmat], W=[pb], signal=True)
            nn = min(n, b0 + 4) - b0
            CP("dve", [pb], [scratch_tok], out=scratch_tok.ap[0:NS, b0 * 128:(b0 + nn) * 128], in_=pb.ap[0:NS, 0:nn * 128])

    def conf_mixer():
        nt = len(tiles) - 1
        s0, sl0 = ws.acquire("pw1_0")
        s1, sl1 = ws.acquire("pw1_1")
        s2, sl2 = ws.acquire("pw2")
        pw1 = [s0, s1]

        def glu(hT_b, w, outs_fn, scratch):
            for c in range(8):
                sb = pw1[c // 4]
                j = c % 4
                pa = ps()
                mm_group(P, pa.ap[:, 0:w], [(sb.ap[:, k, j * 128:(j + 1) * 128], hT_b.ap[:, k, 0:w]) for k in range(8)],
                         R=[sb, hT_b], W=[pa])
                pg = ps()
                mm_group(P, pg.ap[:, 0:w], [(sb.ap[:, k, 512 + j * 128:512 + (j + 1) * 128], hT_b.ap[:, k, 0:w]) for k in range(8)],
                         R=[sb, hT_b], W=[pg])
                sg = scratch["tmp"][c % 2]
                ACT([pg, vecs], [sg], out=sg.ap[:, 0:w], in_=pg.ap[:, 0:w], func=AF.Sigmoid, bias=V("conf_b_pw1", 8 + c, 1))
                for (oap, obuf, lo, hi) in outs_fn(c):
                    STT("dve", [pa, sg, vecs], [obuf], out=oap, in0=pa.ap[:, lo:hi], scalar=V("conf_b_pw1", c, 1),
                        in1=sg.ap[:, lo:hi], op0=ALU.add, op1=ALU.mult)

        def ln_silu_pw2(yf, w, ti, col0, zb, st1, st2, scratch):
            pS = ps()
            pQ = ps()
            for c in range(8):
                yb_ = scratch["sq"][c % 2]
                ys_ = scratch["sq2"][c % 2]
                ACT([yf], [yb_], out=yb_.ap[:, 0:w], in_=yf.ap[:, c, 0:w], func=AF.Identity)
                MM(pS.ap[:, 0:w], onesb.ap, yb_.ap[:, 0:w], c == 0, c == 7, R=[yb_, onesb], W=[pS], signal=True)
                ACT([yf], [ys_], out=ys_.ap[:, 0:w], in_=yf.ap[:, c, 0:w], func=AF.Square)
                MM(pQ.ap[:, 0:w], onesb.ap, ys_.ap[:, 0:w], c == 0, c == 7, R=[ys_, onesb], W=[pQ], signal=True)
            ACT([pS], [st1], out=st1.ap[:, 0:w], in_=pS.ap[:, 0:w], func=AF.Identity, scale=1.0 / D)
            msq = scratch["tmp"][0]
            TT("dve", [st1], [msq], out=msq.ap[:, 0:w], in0=st1.ap[:, 0:w], in1=st1.ap[:, 0:w], op=ALU.mult)
            STT("dve", [pQ, msq], [st2], out=st2.ap[:, 0:w], in0=pQ.ap[:, 0:w], scalar=1.0 / D, in1=msq.ap[:, 0:w],
                op0=ALU.mult, op1=ALU.subtract)
            ACT([st2, vecs], [st2], out=st2.ap[:, 0:w], in_=st2.ap[:, 0:w], func=AF.Ln, bias=V("eps_ln"), scale=1.0)
            ACT([st2], [st2], out=st2.ap[:, 0:w], in_=st2.ap[:, 0:w], func=AF.Exp, scale=-0.5)
            for c in range(8):
                t = scratch["tmp"][c % 2]
                TT("dve", [yf, st1], [t], out=t.ap[:, 0:w], in0=yf.ap[:, c, 0:w], in1=st1.ap[:, 0:w], op=ALU.subtract)
                TT("dve", [t, st2], [t], out=t.ap[:, 0:w], in0=t.ap[:, 0:w], in1=st2.ap[:, 0:w], op=ALU.mult)
                ACT([t, vecs], [zb], out=zb.ap[:, c, 0:w], in_=t.ap[:, 0:w], func=AF.Silu, scale=V("conf_ln_g", c, 1),
                    bias=V("conf_ln_b", c, 1))
            for n2 in range(8):
                pb = ps()
                mm_group(P, pb.ap[:, 0:w], [(s2.ap[:, k, n2 * 128:(n2 + 1) * 128], zb.ap[:, k, 0:w]) for k in range(8)],
                         R=[s2, zb], W=[pb])
                resid_add(ti, n2, pb.ap[:, 0:w], pb, 2, scratch, bias_ap=V("conf_b_pw2", n2, 1), col0=col0, w=w)

        arena.reset()
        scratch = mk_scratch()
        scratch["sq2"] = [arena.alloc([TW], BF16) for _ in range(2)]
        SW = 256
        if CFG.get("SKIP") == "prompt":
            nt = 0
        uW = arena.alloc([8, 30 + TW], BF16)
        hT = arena.alloc([8, TW], BF16)
        yf = arena.alloc([8, SW], F32)
        zb = arena.alloc([8, SW], BF16)
        dg = [arena.alloc([31, 128], BF16) for _ in range(2)]
        st1 = arena.alloc([SW], F32)
        st2 = arena.alloc([SW], F32)
        uL = arena.alloc([8, 30], F32)
        uh = arena.alloc([8, 30], F32)
        norm_mod_src(xh, xh.ap[:, :, 0:30], 30, False, 1, 0, hT, scratch)
        glu(hT, 30, lambda c: [(uh.ap[:, c, :], uh, 0, 30)], scratch)
        TS("dve", [uh, flag], [uW], out=uW.ap[:, :, 0:30], in0=uh.ap, scalar1=flag.ap[:, 0:1], scalar2=None, op0=ALU.mult)
        wdw = V("conf_w_dw").rearrange("p (c k) -> p c k", c=8)
        for ti in range(nt):
            t0, w, _ = tiles[ti]
            norm_mod(ti, 1, 0, hT, scratch)
            last = (ti == nt - 1)
            glu(hT, w, (lambda c: [(uW.ap[:, c, 30:30 + w], uW, 0, w)] + ([(uL.ap[:, c, :], uL, w - 30, w)] if last else [])),
                scratch)
            for sub in range(w // SW):
                cs = sub * SW
                for c in range(8):
                    d = dg[c % 2]
                    TT("dve", [cmatb, vecs], [d], out=d.ap, in0=CMb("ident").unsqueeze(1).to_broadcast([128, 31, 128]),
                       in1=wdw[:, c, :].unsqueeze(2).to_broadcast([128, 31, 128]), op=ALU.mult)
                    pb = ps()
                    for k in range(31):
                        MM(pb.ap[:, 0:SW], d.ap[:, k, :], uW.ap[:, c, cs + k:cs + k + SW], k == 0, k == 30, R=[d, uW], W=[pb])
                    ACT([pb, vecs], [yf], out=yf.ap[:, c, :], in_=pb.ap[:, 0:SW], func=AF.Identity, bias=V("conf_b_dw", c, 1))
                ln_silu_pw2(yf, SW, ti, cs, zb, st1, st2, scratch)
            if not last:
                CP("dve", [uW], [uW], out=uW.ap[:, :, 0:30], in_=uW.ap[:, :, w:w + 30])
        if nt > 0:
            P.dma("sp", confp_d.rearrange("(c p) t -> p c t", p=128), uL.ap, "st_confp", R=[uL])
        nt = len(tiles) - 1
        if CFG.get("SKIP") == "sample":
            ws.release(sl0)
            ws.release(sl1)
            ws.release(sl2)
            return

        arena.reset()
        scratch = mk_scratch()
        scratch["sq2"] = [arena.alloc([TW], BF16) for _ in range(2)]
        hT = arena.alloc([8, NS], BF16)
        uS = arena.alloc([8, NS], F32)
        yf = arena.alloc([8, NS], F32)
        y1 = arena.alloc([8, NS], F32)
        zb = arena.alloc([8, NS], BF16)
        st1 = arena.alloc([NS], F32)
        st2 = arena.alloc([NS], F32)
        stt = arena.alloc([8, NS, 30], F32)
        prod = arena.alloc([8, NS, 30], F32)
        tok = arena.alloc([D], F32)
        ti = nt
        P.dma("sp", stt.ap, stconfT_d.rearrange("(c p) s k -> p c s k", p=128), "ld_stconf", W=[stt])
        norm_mod(ti, 1, 0, hT, scratch)
        glu(hT, NS, lambda c: [(uS.ap[:, c, :], uS, 0, NS)], scratch)
        TT("dve", [stt, vecs], [prod], out=prod.ap, in0=stt.ap,
           in1=wdw[:, :, 0:30].unsqueeze(2).to_broadcast([128, 8, NS, 30]), op=ALU.mult)
        P.op("dve", ("tensor_reduce", dict(out=y1.ap, in_=prod.ap, axis=AX.X, op=ALU.add)), R=[prod], W=[y1])
        for c in range(8):
            STT("dve", [uS, y1, vecs], [yf], out=yf.ap[:, c, :], in0=uS.ap[:, c, :], scalar=wdw[:, c, 30:31],
                in1=y1.ap[:, c, :], op0=ALU.mult, op1=ALU.add)
            TS("dve", [yf, vecs], [yf], out=yf.ap[:, c, :], in0=yf.ap[:, c, :], scalar1=V("conf_b_dw", c, 1), scalar2=None,
               op0=ALU.add)
        ln_silu_pw2(yf, NS, ti, 0, zb, st1, st2, scratch)
        P.dma("sp", confs_d[:, 0:29, :], stconf_d[:, 1:30, :], "st_confs")
        to_token_major([uS.ap[:, c, :] for c in range(8)], [uS], tok)
        P.dma("sp", confs_d[:, 29, :], tok.ap[0:NS, :], "st_confs", R=[tok])
        ws.release(sl0)
        ws.release(sl1)
        ws.release(sl2)

    def swa_mixer():
        nt = len(tiles) - 1
        sq_, slq = ws.acquire("qkv_0")
        sk_, slk = ws.acquire("qkv_1")
        so_, slo = ws.acquire("wo")
        SC = 0.125

        def rope(pb, w, rc, rsn, out_ap, out_buf, scratch, f32_out=None):
            qf = scratch["qf"]
            rC = scratch["rC"]
            rS = scratch["rS"]
            ACT([pb], [qf], out=qf.ap[:, 0:w], in_=pb.ap[:, 0:w], func=AF.Identity)
            pr = ps()
            MM(pr.ap[:, 0:w], CMf("ropeRT"), qf.ap[:, 0:w], True, True, R=[cmat, qf], W=[pr])
            t1 = scratch["tmp"][0]
            t2 = scratch["tmp"][1]
            TT("dve", [qf, rC], [t1], out=t1.ap[:, 0:w], in0=qf.ap[:, 0:w], in1=rc, op=ALU.mult)
            TT("dve", [pr, rS], [t2], out=t2.ap[:, 0:w], in0=pr.ap[:, 0:w], in1=rsn, op=ALU.mult)
            TT("dve", [t1, t2], [out_buf], out=out_ap, in0=t1.ap[:, 0:w], in1=t2.ap[:, 0:w], op=ALU.add)
            if f32_out is not None:
                TT("dve", [t1, t2], [f32_out[1]], out=f32_out[0], in0=t1.ap[:, 0:w], in1=t2.ap[:, 0:w], op=ALU.add)

        def proj_k(hT_b, col0, w, rc, rsn, out_fn, scratch):
            for g in range(4):
                pb = ps()
                mm_group(P, pb.ap[:, 0:w], [(sk_.ap[:, k, g * 128:(g + 1) * 128], hT_b.ap[:, k, col0:col0 + w]) for k in range(8)],
                         R=[sk_, hT_b], W=[pb])
                oap, obuf, f32o = out_fn(g)
                rope(pb, w, rc, rsn, oap, obuf, scratch, f32o)

        def proj_v_block(hT_b, col0, out_aps, out_buf, f32_out=None):
            pb = ps()
            mm_group(P, pb.ap[:, 0:256], [(hT_b.ap[:, k, col0:col0 + 128], sk_.ap[:, k, 512:768]) for k in range(8)],
                     R=[sk_, hT_b], W=[pb])
            for oap in out_aps:
                ACT([pb], [out_buf], out=oap, in_=pb.ap[:, 0:256].rearrange("p (g x) -> p g x", g=4), func=AF.Identity)
            if f32_out is not None:
                ACT([pb], [f32_out[1]], out=f32_out[0], in_=pb.ap[:, 0:256], func=AF.Identity)

        arena.reset()
        scratch = mk_scratch()
        scratch["qf"] = arena.alloc([TW], F32)
        scratch["qq"] = arena.alloc([TW], BF16)
        hT = arena.alloc([8, TW], BF16)
        rC = arena.alloc([TW], F32)
        rS = arena.alloc([TW], F32)
        scratch["rC"] = rC
        scratch["rS"] = rS
        qAB = [arena.alloc([TW], BF16) for _ in range(2)]
        kW = arena.alloc([4, 128 + TW], BF16)
        vW = arena.alloc([5, 4 * 2 * 128], BF16)
        onesAB = arena.alloc([2, 128], BF16)
        P.op("dve", ("memset", dict(ap=vW.ap, constant=0.0)), W=[vW])
        P.op("dve", ("memset", dict(ap=onesAB.ap, constant=0.0)), W=[onesAB])
        P.op("dve", ("memset", dict(ap=onesAB.ap[:, 0, 0:64], constant=1.0)), W=[onesAB])
        P.op("dve", ("memset", dict(ap=onesAB.ap[:, 1, 64:128], constant=1.0)), W=[onesAB])

        def vview(blk, var):
            v5 = vW.ap[:, blk, :].rearrange("p (g v x) -> p g v x", g=4, v=2)
            return v5[:, :, var, var * 64:(var + 1) * 64]

        def v_lhsT(blk, g, var):
            o = (g * 2 + var) * 128
            return vW.ap[:, blk, o:o + 128]
        pT = [arena.alloc([512], BF16) for _ in range(2)]
        oT = arena.alloc([8, TW], BF16)
        mask4 = arena.alloc([512], BF16)
        mask4f = arena.alloc([512], BF16)
        rec = arena.alloc([TW], F32)
        esink = arena.alloc([8], F32)
        hst = arena.alloc([768], F32)
        ACT([vecs], [esink], out=esink.ap, in_=V("swa_sink"), func=AF.Exp)
        for q in range(4):
            nm = "triGE" if q % 2 == 0 else "triLE"
            CP("dve", [cmatb], [mask4], out=mask4.ap[:, q * 128:(q + 1) * 128], in_=CMb(nm))
            if q % 2 == 0:
                TS("dve", [cmatb, flag], [mask4f], out=mask4f.ap[:, q * 128:(q + 1) * 128], in0=CMb(nm),
                   scalar1=flag.ap[:, 0:1], scalar2=None, op0=ALU.mult)
            else:
                CP("dve", [cmatb], [mask4f], out=mask4f.ap[:, q * 128:(q + 1) * 128], in_=CMb(nm))
        tl = nt - 1
        t0l, wl, _ = tiles[tl]
        lc = wl - 128
        P.dma("sp", rC.ap[:, 0:128], ropeC_d[:, t0l + lc:t0l + wl], "ld_rC", W=[rC])
        P.dma("sp", rS.ap[:, 0:128], ropeS_d[:, t0l + lc:t0l + wl], "ld_rS", W=[rS])
        norm_mod_src(xb[tl], xb[tl].ap[:, :, lc:wl], 128, False, 1, 0, hT, scratch)
        hk = hst.ap[:, 0:512].rearrange("p (g t) -> p g t", g=4)
        proj_k(hT, 0, 128, rC.ap[:, 0:128], rS.ap[:, 0:128],
               lambda g: (kW.ap[:, g, 0:128], kW, (hk[:, g, :], hst)), scratch)
        proj_v_block(hT, 0, [vview(0, 0), vview(0, 1)], vW, (hst.ap[:, 512:768], hst))
        P.dma("sp", kp_d[:, :, :], hk, "st_kvp", R=[hst])
        P.dma("sp", vp_d[:, :], hst.ap[:, 512:768], "st_kvp", R=[hst])
        if CFG.get("COLL", True):
            hsrc = Buf(swah_src.ap())
            hdst = Buf(swah_dst.ap())
            P.dma("sp", hsrc.ap[:, :], hst.ap, "swah_w", R=[hst], W=[hsrc])
            P.collective(swah_src.ap().opt(), swah_dst.ap().opt(), "cc_swa", R=[hsrc], W=[hdst])
            P.dma("sp", hst.ap, hdst.ap[0:128, :], "swah_r", R=[hdst], W=[hst])
            CP("dve", [hst], [kW], out=kW.ap[:, :, 0:128], in_=hk)
            for var in range(2):
                CP("dve", [hst], [vW], out=vview(0, var), in_=hst.ap[:, 512:768].rearrange("p (g x) -> p g x", g=4))
        for ti in range(0 if CFG.get("SKIP") == "swa_prompt" else nt):
            t0, w, _ = tiles[ti]
            nb = w // 128
            P.dma("sp", rC.ap[:, 0:w], ropeC_d[:, t0:t0 + w], "ld_rC", W=[rC])
            P.dma("sp", rS.ap[:, 0:w], ropeS_d[:, t0:t0 + w], "ld_rS", W=[rS])
            norm_mod(ti, 1, 0, hT, scratch)
            proj_k(hT, 0, w, rC.ap[:, 0:w], rS.ap[:, 0:w], lambda g: (kW.ap[:, g, 128:128 + w], kW, None), scratch)
            for b in range(nb):
                proj_v_block(hT, b * 128, [vview(1 + b, 0), vview(1 + b, 1)], vW)
            for c in range(0 if CFG.get("SKIP") == "swa_noattn" else 8):
                g = c // 2
                pb = ps()
                mm_group(P, pb.ap[:, 0:w], [(sq_.ap[:, k, c * 128:(c + 1) * 128], hT.ap[:, k, 0:w]) for k in range(8)],
                         R=[sq_, hT], W=[pb])
                qq = scratch["qq"]
                rope(pb, w, rC.ap[:, 0:w], rS.ap[:, 0:w], qq.ap[:, 0:w], qq, scratch)
                for hh in range(2):
                    TS("dve", [qq, vecs], [qAB[hh]], out=qAB[hh].ap[:, 0:w], in0=qq.ap[:, 0:w], scalar1=V("hmask", hh, 1), scalar2=None,
                       op0=ALU.mult)
                po = ps()
                pd = ps()
                for qb in range(nb):
                    psc = ps()
                    for hh in range(2):
                        for kb in range(2):
                            MM(psc.ap[:, (hh * 2 + kb) * 128:(hh * 2 + kb + 1) * 128],
                               kW.ap[:, g, (qb + kb) * 128:(qb + kb + 1) * 128],
                               qAB[hh].ap[:, qb * 128:(qb + 1) * 128], True, True,
                               R=[kW, qAB[hh]], W=[psc], signal=(hh == 1 and kb == 1))
                    pt = pT[qb % 2]
                    ACT([psc], [pt], out=pt.ap, in_=psc.ap, func=AF.Exp, scale=SC)
                    mk = mask4f if (ti == 0 and qb == 0) else mask4
                    TT("dve", [pt, mk], [pt], out=pt.ap, in0=pt.ap, in1=mk.ap, op=ALU.mult)
                    i_ = 0
                    for hh in range(2):
                        for kb in range(2):
                            MM(po.ap[:, qb * 128:(qb + 1) * 128], v_lhsT(qb + kb, g, hh),
                               pt.ap[:, (hh * 2 + kb) * 128:(hh * 2 + kb + 1) * 128], i_ == 0, i_ == 3, R=[vW, pt], W=[po], signal=False)
                            i_ += 1
                    i_ = 0
                    for hh in range(2):
                        for kb in range(2):
                            MM(pd.ap[:, qb * 128:(qb + 1) * 128], onesAB.ap[:, hh, :],
                               pt.ap[:, (hh * 2 + kb) * 128:(hh * 2 + kb + 1) * 128], i_ == 0, i_ == 3, R=[onesAB, pt], W=[pd],
                               signal=(i_ == 3))
                            i_ += 1
                TS("dve", [pd, esink], [rec], out=rec.ap[:, 0:w], in0=pd.ap[:, 0:w], scalar1=esink.ap[:, c:c + 1], scalar2=None,
                   op0=ALU.add)
                P.op("dve", ("reciprocal", dict(out=rec.ap[:, 0:w], in_=rec.ap[:, 0:w])), R=[rec], W=[rec])
                TT("dve", [po, rec], [oT], out=oT.ap[:, c, 0:w], in0=po.ap[:, 0:w], in1=rec.ap[:, 0:w], op=ALU.mult)
            for n2 in range(8):
                pb = ps()
                mm_group(P, pb.ap[:, 0:w], [(so_.ap[:, k, n2 * 128:(n2 + 1) * 128], oT.ap[:, k, 0:w]) for k in range(8)],
                         R=[so_, oT], W=[pb])
                resid_add(ti, n2, pb.ap[:, 0:w], pb, 2, scratch)
            if ti != nt - 1:
                CP("dve", [kW], [kW], out=kW.ap[:, :, 0:128], in_=kW.ap[:, :, w:w + 128])
                CP("dve", [vW], [vW], out=vW.ap[:, 0, :], in_=vW.ap[:, nb, :])

        if CFG.get("SKIP") == "swa_sample":
            ws.release(slq)
            ws.release(slk)
            ws.release(slo)
            return
        arena.reset()
        scratch = mk_scratch()
        scratch["qf"] = arena.alloc([TW], F32)
        ti = nt
        t0 = tiles[ti][0]
        hT = arena.alloc([8, NS], BF16)
        rC = arena.alloc([NS], F32)
        rS = arena.alloc([NS], F32)
        scratch["rC"] = rC
        scratch["rS"] = rS
        qS = arena.alloc([8, NS], BF16)
        knT = arena.alloc([4, NS], F32)
        knb = arena.alloc([4, NS], BF16)
        vnT = arena.alloc([4, NS], F32)
        ckT = arena.alloc([NS, 4, 128], BF16)
        cV = arena.alloc([NS, 256], BF16)
        pP = arena.alloc([256], BF16)
        prodn = arena.alloc([8, NS], BF16)
        pnew = arena.alloc([8, NS], F32)
        num = arena.alloc([8, NS], F32)
        den = arena.alloc([8, NS], F32)
        oS = arena.alloc([8, NS], BF16)
        esink = arena.alloc([8], F32)
        tok = arena.alloc([512], F32)
        tok2 = arena.alloc([512], F32)
        ACT([vecs], [esink], out=esink.ap, in_=V("swa_sink"), func=AF.Exp)
        P.dma("pool", ckT.ap, ckT_d[:, :, :, :], "ld_ck", W=[ckT])
        P.dma("pool", cV.ap, cv_d.rearrange("s j c -> j s c"), "ld_cv", W=[cV])
        P.dma("sp", rC.ap, ropeC_d[:, t0:t0 + NS], "ld_rC", W=[rC])
        P.dma("sp", rS.ap, ropeS_d[:, t0:t0 + NS], "ld_rS", W=[rS])
        norm_mod(ti, 1, 0, hT, scratch)
        for c in range(8):
            pb = ps()
            mm_group(P, pb.ap[:, 0:NS], [(sq_.ap[:, k, c * 128:(c + 1) * 128], hT.ap[:, k, :]) for k in range(8)],
                     R=[sq_, hT], W=[pb])
            rope(pb, NS, rC.ap, rS.ap, qS.ap[:, c, :], qS, scratch)
        proj_k(hT, 0, NS, rC.ap, rS.ap, lambda g: (knb.ap[:, g, :], knb, (knT.ap[:, g, :], knT)), scratch)
        pv = ps()
        for g in range(4):
            for hh in range(2):
                mm_group(P, pv.ap[hh * 64:(hh + 1) * 64, g * NS:(g + 1) * NS],
                         [(sk_.ap[:, k, 512 + g * 64:512 + (g + 1) * 64], hT.ap[:, k, :]) for k in range(8)],
                         R=[sk_, hT], W=[pv])
        CP("dve", [pv], [vnT], out=vnT.ap, in_=pv.ap[:, 0:4 * NS].rearrange("p (g s) -> p g s", g=4))
        pS_ = ps()
        for s_ in range(NS):
            for g in range(4):
                for hh in range(2):
                    col = ((s_ * 4 + g) * 2 + hh) * 2
                    MM(pS_.ap[:, col:col + 2], ckT.ap[hh * 64:(hh + 1) * 64, s_, g, :],
                       qS.ap[hh * 64:(hh + 1) * 64, 2 * g:2 * g + 2, s_], True, True, R=[ckT, qS], W=[pS_],
                       signal=(s_ == NS - 1 and g == 3 and hh == 1))
        ACT([pS_], [pP], out=pP.ap, in_=pS_.ap[:, 0:256], func=AF.Exp, scale=SC)
        TT("dve", [qS, knb], [prodn], out=prodn.ap.rearrange("p (g i) s -> p g i s", g=4),
           in0=qS.ap.rearrange("p (g i) s -> p g i s", g=4),
           in1=knb.ap.unsqueeze(2).to_broadcast([128, 4, 2, NS]), op=ALU.mult)
        pn = ps()
        MM(pn.ap[:, 0:128], CMb("half"), prodn.ap.rearrange("p c s -> p (c s)"), True, True, R=[cmatb, prodn], W=[pn])
        ACT([pn], [pnew], out=pnew.ap.rearrange("p c s -> p (c s)"), in_=pn.ap[:, 0:128], func=AF.Exp, scale=SC)
        po2 = ps()
        for s_ in range(NS):
            for g in range(4):
                for hh in range(2):
                    col = ((s_ * 4 + g) * 2 + hh) * 2
                    MM(po2.ap[hh * 64:(hh + 1) * 64, col:col + 2], cV.ap[:, s_, g * 64:(g + 1) * 64], pP.ap[:, col:col + 2],
                       True, True, R=[cV, pP], W=[po2], signal=(s_ == NS - 1 and g == 3 and hh == 1))
        pd2 = ps()
        MM(pd2.ap[:, 0:256], onesb.ap, pP.ap, True, True, R=[onesb, pP], W=[pd2])
        for hh in range(2):
            sl_ = slice(hh * 64, (hh + 1) * 64)
            pov = po2.ap[sl_, 0:256].rearrange("p (s g h i) -> p g i h s", s=NS, g=4, h=2)[:, :, :, hh, :]
            pdv = pd2.ap[sl_, 0:256].rearrange("p (s g h i) -> p g i h s", s=NS, g=4, h=2)[:, :, :, hh, :]
            n4 = num.ap[sl_].rearrange("p (g i) s -> p g i s", g=4)
            d4 = den.ap[sl_].rearrange("p (g i) s -> p g i s", g=4)
            pn4 = pnew.ap[sl_].rearrange("p (g i) s -> p g i s", g=4)
            TT("dve", [pnew, vnT], [num], out=n4, in0=pn4, in1=vnT.ap[sl_].unsqueeze(2).to_broadcast([64, 4, 2, NS]), op=ALU.mult)
            TT("dve", [num, po2], [num], out=n4, in0=n4, in1=pov, op=ALU.add)
            TT("dve", [pnew, pd2], [den], out=d4, in0=pn4, in1=pdv, op=ALU.add)
            TT("dve", [den, esink], [den], out=den.ap[sl_], in0=den.ap[sl_],
               in1=esink.ap[sl_].unsqueeze(2).to_broadcast([64, 8, NS]), op=ALU.add)
            P.op("dve", ("reciprocal", dict(out=den.ap[sl_], in_=den.ap[sl_])), R=[den], W=[den])
            TT("dve", [num, den], [oS], out=oS.ap[sl_], in0=num.ap[sl_], in1=den.ap[sl_], op=ALU.mult)
        for n2 in range(8):
            pb = ps()
            mm_group(P, pb.ap[:, 0:NS], [(so_.ap[:, k, n2 * 128:(n2 + 1) * 128], oS.ap[:, k, :]) for k in range(8)],
                     R=[so_, oS], W=[pb])
            resid_add(ti, n2, pb.ap[:, 0:NS], pb, 2, scratch)
        P.dma("sp", ks_d[:, 0:127, :], ck_d[:, 1:128, :], "st_kvs")
        P.dma("sp", vs_d[:, 0:127, :], cv_d[:, 1:128, :], "st_kvs")
        to_token_major([knT.ap[:, g, :] for g in range(4)], [knT], tok)
        P.dma("sp", ks_d[:, 127, :].rearrange("s (g x) -> s g x", g=4),
              tok.ap[0:NS, :].rearrange("s (g x) -> s g x", g=4)[:, :, 0:64], "st_kvs", R=[tok])
        to_token_major([vnT.ap[:, g, :] for g in range(4)], [vnT], tok2)
        P.dma("sp", vs_d[:, 127, :].rearrange("s (g x) -> s g x", g=4),
              tok2.ap[0:NS, :].rearrange("s (g x) -> s g x", g=4)[:, :, 0:64], "st_kvs", R=[tok2])
        ws.release(slq)
        ws.release(slk)
        ws.release(slo)

    def gdn_mixer():
        nt = len(tiles) - 1
        NB = TP // 128
        arena.reset()
        scratch = mk_scratch()
        hf32 = [arena.alloc([8, TW], F32) for _ in range(1)]
        hb16 = arena.alloc([8, TW], BF16)
        ghs = [Buf(t.ap()) for t in gh_src]
        ghd = [Buf(t.ap()) for t in gh_dst]

        def gather(src_t, dst_t, sb_, db_, key):
            if CFG.get("COLL", True):
                P.collective(src_t.ap().opt(), dst_t.ap().opt(), key, R=[sb_], W=[db_])
            else:
                P.dma("sp", db_.ap[0:D, :], sb_.ap[:, :], key + "_cp", R=[sb_], W=[db_])
                P.dma("sp", db_.ap[D:2 * D, :], sb_.ap[:, :], key + "_cp", R=[sb_], W=[db_])
        for ti, (t0, w, smp) in enumerate(tiles):
            norm_mod(ti, 1, 0, hb16, scratch)
            CP("dve", [hb16], [hf32[0]], out=hf32[0].ap[:, :, 0:w], in_=hb16.ap[:, :, 0:w])
            P.dma("sp", ghs[ti].ap[:, :].rearrange("(c p) t -> p c t", p=128), hf32[0].ap[:, :, 0:w], "gh_w",
                  R=[hf32[0]], W=[ghs[ti]])
            gather(gh_src[ti], gh_dst[ti], ghs[ti], ghd[ti], "cc_gh%d" % ti)

        arena.reset()
        sqk, slqk = ws.acquire("g_qk")
        sv, slv = ws.acquire("g_v")
        sz, slz = ws.acquire("g_z")
        gos = [Buf(t.ap()) for t in go_src]
        god = [Buf(t.ap()) for t in go_dst]
        wcv = V("gdn_wconv").rearrange("p (c k) -> p c k", c=16)
        BW = 128
        hT = arena.alloc([8, BW], BF16)
        pre = arena.alloc([16, 3 + BW], F32)
        cacc = None
        csil = None
        sqb = arena.alloc([8, BW], BF16)
        rn = arena.alloc([8, BW], F32, name="rn")
        qkT = arena.alloc([8, BW], BF16, name="qkT")
        vTb = arena.alloc([8, BW], BF16, name="vTb")
        k_tok = arena.alloc([4, 128], BF16, name="k_tok")
        S = arena.alloc([8, 128], F32, name="S")
        Sb = arena.alloc([8, 128], BF16)
        ogT = arena.alloc([8, BW], F32)
        sm = {nm: arena.alloc([8], F32, name="sm_" + nm) for nm in ["g", "beta", "nbeta", "gc", "gl", "ek", "eg", "gam", "ss", "t8"]}

        def mkset():
            B = {}
            blk = arena.alloc([7 * 256], F32)
            for i_, nm in enumerate(("i1", "i2", "i3", "i4", "i5", "i6", "grhs")):
                B[nm] = Buf(blk.ap[:, i_ * 256:(i_ + 1) * 256].rearrange("p (a b) -> p a b", a=2))
            B["blk"] = blk
            for nm in ("sA", "sB", "sC", "sD", "sE", "sF", "attnT", "qtT", "kgT", "ktil", "v_tok", "z_tok"):
                B[nm] = arena.alloc([2, 128], BF16)
            B["ss"] = arena.alloc([2], F32)
            return B
        gsets = [mkset(), mkset()]
        cacc = Buf(gsets[0]["blk"].ap[:, 0:1024].rearrange("p (a b) -> p a b", a=8))
        csil = Buf(gsets[1]["blk"].ap[:, 0:1024].rearrange("p (a b) -> p a b", a=8))
        alias_tiles = [gsets[q][nm] for q in range(2) for nm in ("i1", "i2", "i3", "i4")]
        fsc = arena.alloc([2], F32)

        def fence(reads, writes):
            P.op("dve", ("memset", dict(ap=fsc.ap, constant=0.0)), R=reads, W=writes + [fsc])
        P.op("dve", ("memset", dict(ap=S.ap, constant=0.0)), W=[S])
        P.op("dve", ("memset", dict(ap=Sb.ap, constant=0.0)), W=[Sb])
        P.op("dve", ("memset", dict(ap=pre.ap, constant=0.0)), W=[pre])
        nalog = arena.alloc([8], F32)
        ACT([vecs], [nalog], out=nalog.ap, in_=V("gdn_alog"), func=AF.Exp)
        TS("dve", [nalog], [nalog], out=nalog.ap, in0=nalog.ap, scalar1=-1.0, scalar2=None, op0=ALU.mult)

        def load_h(col0, w):
            for r in range(1):
                pass

        def inproj_conv(hT_b, w, hist_io):
            for grp in range(4):
                pb = ps()
                for j in range(4):
                    ch = grp * 4 + j
                    sb_, col = (sqk, ch * 128) if ch < 8 else (sv, (ch - 8) * 128)
                    mm_group(P, pb.ap[:, j * w:(j + 1) * w], [(sb_.ap[:, k, col:col + 128], hT_b.ap[:, k, 0:w]) for k in range(8)],
                             R=[sb_, hT_b], W=[pb])
                ACT([pb], [pre], out=pre.ap[:, grp * 4:(grp + 1) * 4, 3:3 + w],
                    in_=pb.ap[:, 0:4 * w].rearrange("p (c t) -> p c t", c=4), func=AF.Identity)
            for part in range(2):
                cs8 = slice(part * 8, (part + 1) * 8)
                for k in range(4):
                    wk = wcv[:, cs8, k:k + 1].to_broadcast([128, 8, w])
                    if k == 0:
                        TT("dve", [pre, vecs], [cacc], out=cacc.ap[:, :, 0:w], in0=pre.ap[:, cs8, k:k + w], in1=wk, op=ALU.mult)
                    else:
                        TT("dve", [pre, vecs], [csil], out=csil.ap[:, :, 0:w], in0=pre.ap[:, cs8, k:k + w], in1=wk, op=ALU.mult)
                        TT("dve", [csil, cacc], [cacc], out=cacc.ap[:, :, 0:w], in0=cacc.ap[:, :, 0:w], in1=csil.ap[:, :, 0:w], op=ALU.add)
                ACT([cacc], [csil], out=csil.ap[:, :, 0:w], in_=cacc.ap[:, :, 0:w], func=AF.Silu)
                if part == 1:
                    CP("dve", [csil], [vTb], out=vTb.ap[:, :, 0:w], in_=csil.ap[:, :, 0:w])
                    continue
                ACT([csil], [sqb], out=sqb.ap[:, :, 0:w], in_=csil.ap[:, :, 0:w], func=AF.Square)
                for hh in range(2):
                    pb = ps()
                    for j in range(4):
                        MM(pb.ap[:, j * w:(j + 1) * w], onesb.ap, sqb.ap[:, hh * 4 + j, 0:w], True, True, R=[onesb, sqb], W=[pb],
                           signal=(j == 3))
                    ACT([pb, vecs], [rn], out=rn.ap[:, hh * 4:(hh + 1) * 4, 0:w], in_=pb.ap[:, 0:4 * w].rearrange("p (c t) -> p c t", c=4),
                        func=AF.Ln, bias=V("eps_rms"), scale=1.0)
                ACT([rn], [rn], out=rn.ap[:, :, 0:w], in_=rn.ap[:, :, 0:w], func=AF.Exp, scale=-0.5)
                STT("dve", [csil, rn], [qkT], out=qkT.ap[:, 0:4, 0:w], in0=csil.ap[:, 0:4, 0:w], scalar=float(128 ** -0.5),
                    in1=rn.ap[:, 0:4, 0:w], op0=ALU.mult, op1=ALU.mult)
                TT("dve", [csil, rn], [qkT], out=qkT.ap[:, 4:8, 0:w], in0=csil.ap[:, 4:8, 0:w], in1=rn.ap[:, 4:8, 0:w], op=ALU.mult)

        def softplus_gate(src_ps, np_, gout, bout):
            ACT([src_ps], [bout], out=bout.ap[0:np_, :], in_=src_ps.ap[0:np_, 0:8], func=AF.Sigmoid)
            t8 = sm["t8"]
            TT("dve", [src_ps, vecs], [t8], out=t8.ap[0:np_, :], in0=src_ps.ap[0:np_, 8:16], in1=V("gdn_dtb")[0:np_, :], op=ALU.add)
            ACT([t8], [t8], out=t8.ap[0:np_, :], in_=t8.ap[0:np_, :], func=AF.Exp)
            ACT([t8], [t8], out=t8.ap[0:np_, :], in_=t8.ap[0:np_, :], func=AF.Ln, bias=V("one")[0:np_, :], scale=1.0)
            TT("dve", [t8, nalog], [gout], out=gout.ap[0:np_, :], in0=t8.ap[0:np_, :], in1=nalog.ap[0:np_, :], op=ALU.mult)

        ntile_g = 2 * NB
        for gb in range(ntile_g):
            r, lb = gb // NB, gb % NB
            col0 = lb * 128
            tsrc = ghd[lb // 4]
            cc0 = (lb % 4) * 128
            P.dma("pool", hT.ap, tsrc.ap[r * D:(r + 1) * D, cc0:cc0 + 128].rearrange("(c p) t -> p c t", p=128), "ld_gh",
                  R=[tsrc], W=[hT])
            fence(alias_tiles, [cacc, csil])
            inproj_conv(hT, BW, None)
            fence([cacc, csil], alias_tiles)
            if gb == ntile_g - 1:
                P.dma("sp", gcvp_d.rearrange("(c p) k -> p c k", p=128), pre.ap[:, :, BW:BW + 3], "st_gcvp", R=[pre])
            CP("dve", [pre], [pre], out=pre.ap[:, :, 0:3], in_=pre.ap[:, :, BW:BW + 3])
            pb = ps()
            for j in range(4):
                MM(pb.ap[:, j * 128:(j + 1) * 128], qkT.ap[:, 4 + j, :], CMb("ident"), True, True, R=[qkT, cmatb], W=[pb], signal=(j == 3))
            CP("dve", [pb], [k_tok], out=k_tok.ap, in_=pb.ap.rearrange("p (c t) -> p c t", c=4))
            pba = ps()
            mm_group(P, pba.ap[:, 0:16], [(hT.ap[:, k, :], baw.ap[:, k, :]) for k in range(8)], R=[baw, hT], W=[pba])
            softplus_gate(pba, 128, sm["g"], sm["beta"])
            TS("dve", [sm["beta"]], [sm["nbeta"]], out=sm["nbeta"].ap, in0=sm["beta"].ap, scalar1=-1.0, scalar2=None, op0=ALU.mult)
            pg = ps()
            MM(pg.ap[:, 0:8], CMf("triLE"), sm["g"].ap, True, True, R=[cmat, sm["g"]], W=[pg])
            MM(pg.ap[:, 8:16], onesf.ap, sm["g"].ap, True, True, R=[onesf, sm["g"]], W=[pg])
            CP("dve", [pg], [sm["gc"]], out=sm["gc"].ap, in_=pg.ap[:, 0:8])
            ACT([pg], [sm["eg"]], out=sm["eg"].ap, in_=pg.ap[:, 0:8], func=AF.Exp)
            ACT([pg], [sm["gam"]], out=sm["gam"].ap, in_=pg.ap[:, 8:16], func=AF.Exp)
            TT("dve", [pg, sm["gc"]], [sm["ek"]], out=sm["ek"].ap, in0=pg.ap[:, 8:16], in1=sm["gc"].ap, op=ALU.subtract)
            ACT([sm["ek"]], [sm["ek"]], out=sm["ek"].ap, in_=sm["ek"].ap, func=AF.Exp)
            def chain(hq, B):
                hs = slice(hq * 2, hq * 2 + 2)
                i1, i2, i3, i4, i5, i6, grhs = [B[n_] for n_ in ("i1", "i2", "i3", "i4", "i5", "i6", "grhs")]
                decI, decS, osq, dgE, og, rb, vnew, TTb = B["sA"], B["sB"], B["sB"], B["sC"], B["sC"], B["sD"], B["sE"], B["sF"]
                attnT, qtT, kgT, ktil, v_tok, z_tok = B["attnT"], B["qtT"], B["kgT"], B["ktil"], B["v_tok"], B["z_tok"]

                def p2(pq_):
                    return pq_.ap[:, 0:256].rearrange("p (c t) -> p c t", c=2)

                def bcf(nm):
                    return CMf(nm).unsqueeze(1).to_broadcast([128, 2, 128])

                def bcb(nm):
                    return CMb(nm).unsqueeze(1).to_broadcast([128, 2, 128])

                def mm2(lhs, rhs, Rb):
                    pq_ = ps()
                    for j in range(2):
                        MM(pq_.ap[:, j * 128:(j + 1) * 128], lhs.ap[:, j, :], rhs.ap[:, j, :] if rhs is not None else CMf("ident"),
                           True, True, R=Rb, W=[pq_], signal=(j == 1))
                    return pq_
                pb = ps()
                for j in range(2):
                    MM(pb.ap[:, j * 128:(j + 1) * 128], vTb.ap[:, hq * 2 + j, :], CMb("ident"), True, True, R=[vTb, cmatb], W=[pb], signal=(j == 1))
                CP("dve", [pb], [v_tok], out=v_tok.ap, in_=p2(pb))
                pb = ps()
                mm_group(P, pb.ap[:, 0:256], [(hT.ap[:, k, :], sz.ap[:, k, hq * 256:(hq + 1) * 256]) for k in range(8)], R=[sz, hT], W=[pb])
                ACT([pb], [z_tok], out=z_tok.ap, in_=p2(pb), func=AF.Silu)
                TT("dve", [cmat, sm["g"]], [grhs], out=grhs.ap, in0=bcf("triLE"),
                   in1=sm["g"].ap[:, hs].unsqueeze(2).to_broadcast([128, 2, 128]), op=ALU.mult)
                pdec = ps()
                MM(pdec.ap[:, 0:256], CMf("triGT"), grhs.ap.rearrange("p c t -> p (c t)"), True, True, R=[cmat, grhs], W=[pdec])
                ACT([pdec], [decI], out=decI.ap, in_=p2(pdec), func=AF.Exp)
                yield
                TT("dve", [decI, cmatb], [decS], out=decS.ap, in0=decI.ap, in1=bcb("triLT"), op=ALU.mult)
                TT("dve", [decI, cmatb], [decI], out=decI.ap, in0=decI.ap, in1=bcb("triLE"), op=ALU.mult)
                TT("dve", [cmatb, sm["eg"]], [dgE], out=dgE.ap, in0=bcb("ident"),
                   in1=sm["eg"].ap[:, hs].unsqueeze(2).to_broadcast([128, 2, 128]), op=ALU.mult)
                pE = ps()
                MM(pE.ap[:, 0:256], onesb.ap, dgE.ap.rearrange("p c t -> p (c t)"), True, True, R=[onesb, dgE], W=[pE])
                TT("dve", [qkT, pE], [qtT], out=qtT.ap, in0=qkT.ap[:, hq:hq + 1, :].to_broadcast([128, 2, 128]), in1=p2(pE), op=ALU.mult)
                TT("dve", [qkT, pE], [kgT], out=kgT.ap, in0=qkT.ap[:, 4 + hq:5 + hq, :].to_broadcast([128, 2, 128]), in1=p2(pE), op=ALU.mult)
                TT("dve", [k_tok, sm["ek"]], [ktil], out=ktil.ap, in0=k_tok.ap[:, hq:hq + 1, :].to_broadcast([128, 2, 128]),
                   in1=sm["ek"].ap[:, hs].unsqueeze(2).to_broadcast([128, 2, 128]), op=ALU.mult)
                pkk = ps()
                MM(pkk.ap[:, 0:128], qkT.ap[:, 4 + hq, :], qkT.ap[:, 4 + hq, :], True, True, R=[qkT], W=[pkk], signal=False)
                MM(pkk.ap[:, 128:256], qkT.ap[:, 4 + hq, :], qkT.ap[:, hq, :], True, True, R=[qkT], W=[pkk], signal=True)
                X0 = i1
                TT("dve", [pkk, decS], [X0], out=X0.ap, in0=decS.ap, in1=pkk.ap[:, 0:128].unsqueeze(1).to_broadcast([128, 2, 128]), op=ALU.mult)
                TT("dve", [X0, sm["nbeta"]], [X0], out=X0.ap, in0=X0.ap, in1=sm["nbeta"].ap[:, hs].unsqueeze(2).to_broadcast([128, 2, 128]),
                   op=ALU.mult)
                TT("dve", [pkk, decI], [attnT], out=attnT.ap, in0=decI.ap, in1=pkk.ap[:, 128:256].unsqueeze(1).to_broadcast([128, 2, 128]),
                   op=ALU.mult)
                yield
                Y0 = i2
                py = mm2(X0, None, [X0, cmat])
                ACT([py], [Y0], out=Y0.ap, in_=p2(py), func=AF.Identity)
                Xbd, Ybd, Rm = i3, i4, i5
                TT("dve", [X0, cmat], [Xbd], out=Xbd.ap, in0=X0.ap, in1=bcf("bd8"), op=ALU.mult)
                TT("dve", [Xbd, cmat], [Rm], out=Rm.ap, in0=Xbd.ap, in1=bcf("ident"), op=ALU.add)
                yield
                TT("dve", [Y0, cmat], [Ybd], out=Ybd.ap, in0=Y0.ap, in1=bcf("bd8"), op=ALU.mult)
                X1, Y1 = i1, i6
                px = mm2(Ybd, Xbd, [Xbd, Ybd])
                py = mm2(Xbd, Ybd, [Xbd, Ybd])
                CP("dve", [px], [X1], out=X1.ap, in_=p2(px))
                ACT([py], [Y1], out=Y1.ap, in_=p2(py), func=AF.Identity)
                yield
                pr_ = mm2(Y1, Rm, [Y1, Rm])
                Y2 = i3
                py = mm2(X1, Y1, [X1, Y1])
                TT("dve", [pr_, Rm], [Rm], out=Rm.ap, in0=Rm.ap, in1=p2(pr_), op=ALU.add)
                ACT([py], [Y2], out=Y2.ap, in_=p2(py), func=AF.Identity)
                yield
                pr_ = mm2(Y2, Rm, [Y2, Rm])
                TT("dve", [pr_, Rm], [Rm], out=Rm.ap, in0=Rm.ap, in1=p2(pr_), op=ALU.add)
                yield
                RT_ = i6
                pt2 = mm2(Rm, None, [Rm, cmat])
                ACT([pt2], [RT_], out=RT_.ap, in_=p2(pt2), func=AF.Identity)
                Ct, Z1s = i4, i1
                for bsz in (8, 16, 32, 64):
                    TT("dve", [Y0, cmat], [Ct], out=Ct.ap, in0=Y0.ap, in1=bcf("cm%d" % bsz), op=ALU.mult)
                    yield
                    pz1 = mm2(Ct, Rm, [Ct, Rm])
                    ACT([pz1], [Z1s], out=Z1s.ap, in_=p2(pz1), func=AF.Identity)
                    yield
                    pz2 = mm2(RT_, Z1s, [RT_, Z1s])
                    if bsz != 64:
                        TT("dve", [pz2, Rm], [Rm], out=Rm.ap, in0=Rm.ap, in1=p2(pz2), op=ALU.add)
                        yield
                        pt2 = mm2(Rm, None, [Rm, cmat])
                        ACT([pt2], [RT_], out=RT_.ap, in_=p2(pt2), func=AF.Identity)
                    else:
                        TT("dve", [pz2, Rm], [TTb], out=TTb.ap, in0=Rm.ap, in1=p2(pz2), op=ALU.add)
                yield
                TTt = TTb
                pks = ps()
                for j in range(2):
                    MM(pks.ap[:, j * 128:(j + 1) * 128], kgT.ap[:, j, :], Sb.ap[:, hq * 2 + j, :], True, True, R=[kgT, Sb], W=[pks], signal=(j == 1))
                TT("dve", [v_tok, pks], [rb], out=rb.ap, in0=v_tok.ap, in1=p2(pks), op=ALU.subtract)
                yield
                pvn = ps()
                for j in range(2):
                    MM(pvn.ap[:, j * 128:(j + 1) * 128], TTt.ap[:, j, :], rb.ap[:, j, :], True, True, R=[TTt, rb], W=[pvn], signal=(j == 1))
                TT("dve", [pvn, sm["beta"]], [vnew], out=vnew.ap, in0=p2(pvn),
                   in1=sm["beta"].ap[:, hs].unsqueeze(2).to_broadcast([128, 2, 128]), op=ALU.mult)
                yield
                po_ = ps()
                for j in range(2):
                    MM(po_.ap[:, j * 128:(j + 1) * 128], qtT.ap[:, j, :], Sb.ap[:, hq * 2 + j, :], True, False, R=[qtT, Sb], W=[po_], signal=False)
                    MM(po_.ap[:, j * 128:(j + 1) * 128], attnT.ap[:, j, :], vnew.ap[:, j, :], False, True, R=[attnT, vnew], W=[po_], signal=(j == 1))
                pss = ps()
                for j in range(2):
                    MM(pss.ap[:, j * 128:(j + 1) * 128], ktil.ap[:, j, :], vnew.ap[:, j, :], True, True, R=[ktil, vnew], W=[pss], signal=(j == 1))
                TT("dve", [S, sm["gam"]], [S], out=S.ap[:, hs, :], in0=S.ap[:, hs, :],
                   in1=sm["gam"].ap[:, hs].unsqueeze(2).to_broadcast([128, 2, 128]), op=ALU.mult)
                TT("dve", [S, pss], [S], out=S.ap[:, hs, :], in0=S.ap[:, hs, :], in1=p2(pss), op=ALU.add)
                ACT([S], [Sb], out=Sb.ap[:, hs, :], in_=S.ap[:, hs, :], func=AF.Identity)
                po4 = p2(po_)
                ACT([po_], [osq], out=osq.ap, in_=po4, func=AF.Square)
                ssb = B["ss"]
                P.op("dve", ("tensor_reduce", dict(out=ssb.ap, in_=osq.ap, axis=AX.X, op=ALU.add)), R=[osq], W=[ssb])
                yield
                ACT([ssb, vecs], [ssb], out=ssb.ap, in_=ssb.ap, func=AF.Ln, bias=V("eps_rms"), scale=1.0 / 128)
                ACT([ssb], [ssb], out=ssb.ap, in_=ssb.ap, func=AF.Exp, scale=-0.5)
                TT("dve", [po_, ssb], [og], out=og.ap, in0=po4, in1=ssb.ap.unsqueeze(2).to_broadcast([128, 2, 128]), op=ALU.mult)
                TT("dve", [og, vecs], [og], out=og.ap, in0=og.ap, in1=V("gdn_norm").unsqueeze(1).to_broadcast([128, 2, 128]), op=ALU.mult)
                TT("dve", [og, z_tok], [og], out=og.ap, in0=og.ap, in1=z_tok.ap, op=ALU.mult)
                yield
                pt_ = ps()
                for j in range(2):
                    MM(pt_.ap[:, j * 128:(j + 1) * 128], og.ap[:, j, :], CMb("ident"), True, True, R=[og, cmatb], W=[pt_], signal=(j == 1))
                ACT([pt_], [ogT], out=ogT.ap[:, hs, :], in_=p2(pt_), func=AF.Identity)

            for pair in ((0, 1), (2, 3)):
                gens = [chain(pair[0], gsets[0]), chain(pair[1], gsets[1])]
                alive = [True, True]
                while any(alive):
                    for gi_ in range(2):
                        if alive[gi_]:
                            try:
                                next(gens[gi_])
                            except StopIteration:
                                alive[gi_] = False
            gi = r * NTI + lb // 4
            P.dma("sp", gos[gi].ap[:, cc0:cc0 + 128].rearrange("(c p) t -> p c t", p=128), ogT.ap, "go_w", R=[ogT], W=[gos[gi]])
            if lb % 4 == 3:
                gather(go_src[gi], go_dst[gi], gos[gi], god[gi], "cc_go%d" % gi)
        P.dma("sp", ssmp_d.rearrange("h k v -> k h v"), S.ap, "st_ssmp", R=[S])

        NS2 = 2 * NS
        arena.reset()
        nalog = arena.alloc([8], F32)
        ACT([vecs], [nalog], out=nalog.ap, in_=V("gdn_alog"), func=AF.Exp)
        TS("dve", [nalog], [nalog], out=nalog.ap, in0=nalog.ap, scalar1=-1.0, scalar2=None, op0=ALU.mult)
        sm = {"t8": arena.alloc([8], F32)}
        hS = arena.alloc([8, NS2], BF16)
        preS = arena.alloc([16, NS2, 4], F32)
        prodS = arena.alloc([16, NS2, 4], F32)
        cs_ = arena.alloc([16, NS2], F32)
        sqS = arena.alloc([8, NS2], BF16)
        rnS = arena.alloc([8, NS2], F32)
        qkS = arena.alloc([8, NS2], F32)
        vS = arena.alloc([8, NS2], F32)
        i32 = CMf("ident")[0:NS2, 0:NS2]
        for r in range(2):
            P.dma("pool", hS.ap[:, :, r * NS:(r + 1) * NS], ghd[nt].ap[r * D:(r + 1) * D, 0:NS].rearrange("(c p) t -> p c t", p=128),
                  "ld_ghs%d" % r, R=[ghd[nt]], W=[hS])
        P.dma("sp", preS.ap, gcvT_d.rearrange("(c p) s k -> p c s k", p=128), "ld_gcv", W=[preS])
        for grp in range(4):
            pb = ps()
            for j in range(4):
                ch = grp * 4 + j
                sb_, col = (sqk, ch * 128) if ch < 8 else (sv, (ch - 8) * 128)
                mm_group(P, pb.ap[:, j * NS2:(j + 1) * NS2], [(sb_.ap[:, k, col:col + 128], hS.ap[:, k, :]) for k in range(8)],
                         R=[sb_, hS], W=[pb])
            ACT([pb], [preS], out=preS.ap[:, grp * 4:(grp + 1) * 4, :, 3],
                in_=pb.ap[:, 0:4 * NS2].rearrange("p (c t) -> p c t", c=4), func=AF.Identity)
        TT("dve", [preS, vecs], [prodS], out=prodS.ap, in0=preS.ap, in1=wcv.unsqueeze(2).to_broadcast([128, 16, NS2, 4]), op=ALU.mult)
        P.op("dve", ("tensor_reduce", dict(out=cs_.ap, in_=prodS.ap, axis=AX.X, op=ALU.add)), R=[prodS], W=[cs_])
        ACT([cs_], [cs_], out=cs_.ap, in_=cs_.ap, func=AF.Silu)
        ACT([cs_], [sqS], out=sqS.ap, in_=cs_.ap[:, 0:8, :], func=AF.Square)
        pb = ps()
        for j in range(8):
            MM(pb.ap[:, j * NS2:(j + 1) * NS2], onesb.ap, sqS.ap[:, j, :], True, True, R=[onesb, sqS], W=[pb], signal=(j == 7))
        ACT([pb, vecs], [rnS], out=rnS.ap, in_=pb.ap[:, 0:8 * NS2].rearrange("p (c t) -> p c t", c=8), func=AF.Ln, bias=V("eps_rms"), scale=1.0)
        ACT([rnS], [rnS], out=rnS.ap, in_=rnS.ap, func=AF.Exp, scale=-0.5)
        STT("dve", [cs_, rnS], [qkS], out=qkS.ap[:, 0:4, :], in0=cs_.ap[:, 0:4, :], scalar=float(128 ** -0.5), in1=rnS.ap[:, 0:4, :],
            op0=ALU.mult, op1=ALU.mult)
        TT("dve", [cs_, rnS], [qkS], out=qkS.ap[:, 4:8, :], in0=cs_.ap[:, 4:8, :], in1=rnS.ap[:, 4:8, :], op=ALU.mult)
        CP("dve", [cs_], [vS], out=vS.ap, in_=cs_.ap[:, 8:16, :])
        P.dma("sp", gcvs_d[:, 0:2, :], gcv_d[:, 1:3, :], "st_gcvs")
        tokc = Buf(prodS.ap.rearrange("p c s k -> p (c s k)"), share=prodS)
        for b0 in range(0, 16, 4):
            pb = ps()
            for j in range(4):
                MM(pb.ap[0:NS2, j * 128:(j + 1) * 128], preS.ap[:, b0 + j, :, 3], CMf("ident"), True, True, R=[preS, cmat], W=[pb], signal=(j == 3))
            CP("dve", [pb], [tokc], out=tokc.ap[0:NS2, b0 * 128:(b0 + 4) * 128], in_=pb.ap[0:NS2, :])
        P.dma("sp", gcvs_d[:, 2, :], tokc.ap[0:NS2, :], "st_gcvs", R=[tokc])
        ktS = arena.alloc([512], F32)
        vtS = arena.alloc([1024], F32)
        ztS = arena.alloc([1024], F32)
        gS = arena.alloc([8], F32)
        bS = arena.alloc([8], F32)
        pb = ps()
        for j in range(4):
            MM(pb.ap[0:NS2, j * 128:(j + 1) * 128], qkS.ap[:, 4 + j, :], CMf("ident"), True, True, R=[qkS, cmat], W=[pb], signal=(j == 3))
        CP("dve", [pb], [ktS], out=ktS.ap[0:NS2, :], in_=pb.ap[0:NS2, :])
        for hh in range(2):
            pb = ps()
            for j in range(4):
                MM(pb.ap[0:NS2, j * 128:(j + 1) * 128], vS.ap[:, hh * 4 + j, :], CMf("ident"), True, True, R=[vS, cmat], W=[pb], signal=(j == 3))
            CP("dve", [pb], [vtS], out=vtS.ap[0:NS2, hh * 512:(hh + 1) * 512], in_=pb.ap[0:NS2, :])
            pb = ps()
            mm_group(P, pb.ap[0:NS2, :], [(hS.ap[:, k, :], sz.ap[:, k, hh * 512:(hh + 1) * 512]) for k in range(8)], R=[sz, hS], W=[pb])
            ACT([pb], [ztS], out=ztS.ap[0:NS2, hh * 512:(hh + 1) * 512], in_=pb.ap[0:NS2, :], func=AF.Silu)
        pba = ps()
        mm_group(P, pba.ap[0:NS2, 0:16], [(hS.ap[:, k, :], baw.ap[:, k, :]) for k in range(8)], R=[baw, hS], W=[pba])
        softplus_gate(pba, NS2, gS, bS)
        ACT([gS], [gS], out=gS.ap[0:NS2, :], in_=gS.ap[0:NS2, :], func=AF.Exp)
        ws.release(slqk)
        ws.release(slv)
        ws.release(slz)
        St = [[arena.alloc([4, 128], F32) for _ in range(2)] for _ in range(2)]
        rowk = arena.alloc([512], F32)
        rowdh = [arena.alloc([512], F32) for _ in range(2)]
        rowoh = [arena.alloc([512], F32) for _ in range(2)]
        rowzh = [arena.alloc([512], F32) for _ in range(2)]
        ss1h = [arena.alloc([4], F32) for _ in range(2)]
        rows = arena.alloc([16], F32)
        ogS = arena.alloc([8, NS2], F32)
        egb = arena.alloc([8], F32)
        pogS = psb[7]
        for s_ in range(NS2):
            Sxh = St[s_ % 2]
            for hh in range(2):
                P.dma("sp", Sxh[hh].ap, ssm_d[s_, hh * 4:(hh + 1) * 4].rearrange("h k v -> k h v"), "ld_ssm%d_%d" % (s_ % 2, hh), W=[Sxh[hh]])
            sel = i32[:, s_:s_ + 1]
            prow = ps()
            MM(prow.ap[0:1, 0:512], sel, ktS.ap[0:NS2, :], True, True, R=[cmat, ktS], W=[prow])
            CP("dve", [prow], [rowk], out=rowk.ap[0:1, :], in_=prow.ap[0:1, 0:512])
            pge = ps()
            MM(pge.ap[0:1, 0:8], sel, gS.ap[0:NS2, :], True, True, R=[cmat, gS], W=[pge], signal=False)
            MM(pge.ap[0:1, 8:16], sel, bS.ap[0:NS2, :], True, True, R=[cmat, bS], W=[pge])
            CP("dve", [pge], [rows], out=rows.ap[0:1, :], in_=pge.ap[0:1, 0:16])
            pbe = ps()
            MM(pbe.ap[:, 0:8], onesf.ap[0:1, :], rows.ap[0:1, 0:8], True, True, R=[onesf, rows], W=[pbe])
            CP("dve", [pbe], [egb], out=egb.ap, in_=pbe.ap[:, 0:8])

            def hchain(hh):
                Sx = Sxh[hh]
                rowd, rowo, rowz, ss1 = rowdh[hh], rowoh[hh], rowzh[hh], ss1h[hh]
                TT("dve", [Sx, egb], [Sx], out=Sx.ap, in0=Sx.ap, in1=egb.ap[:, hh * 4:(hh + 1) * 4].unsqueeze(2).to_broadcast([128, 4, 128]),
                   op=ALU.mult)
                pv_ = ps()
                MM(pv_.ap[0:1, 0:512], sel, vtS.ap[0:NS2, hh * 512:(hh + 1) * 512], True, True, R=[cmat, vtS], W=[pv_])
                dl = rowd.ap[0:1, :]
                CP("dve", [pv_], [rowd], out=dl, in_=pv_.ap[0:1, 0:512])
                pz_ = ps()
                MM(pz_.ap[0:1, 0:512], sel, ztS.ap[0:NS2, hh * 512:(hh + 1) * 512], True, True, R=[cmat, ztS], W=[pz_])
                ACT([pz_], [rowz], out=rowz.ap[0:1, :], in_=pz_.ap[0:1, 0:512], func=AF.Identity)
                yield
                pkv = ps()
                for j in range(4):
                    h_ = hh * 4 + j
                    MM(pkv.ap[0:1, j * 128:(j + 1) * 128], qkS.ap[:, 4 + h_ // 2, s_:s_ + 1], Sx.ap[:, j, :], True, True, R=[qkS, Sx], W=[pkv],
                       signal=(j == 3))
                TT("dve", [rowd, pkv], [rowd], out=dl, in0=dl, in1=pkv.ap[0:1, 0:512], op=ALU.subtract)
                TT("dve", [rowd, rows], [rowd], out=dl.rearrange("p (c t) -> p c t", c=4), in0=dl.rearrange("p (c t) -> p c t", c=4),
                   in1=rows.ap[0:1, 8 + hh * 4:8 + hh * 4 + 4].unsqueeze(2).to_broadcast([1, 4, 128]), op=ALU.mult)
                yield
                pou = ps()
                for j in range(4):
                    h_ = hh * 4 + j
                    MM(pou.ap[:, j * 128:(j + 1) * 128], rowk.ap[0:1, (h_ // 2) * 128:(h_ // 2 + 1) * 128], rowd.ap[0:1, j * 128:(j + 1) * 128],
                       True, True, R=[rowk, rowd], W=[pou], signal=(j == 3))
                TT("dve", [Sx, pou], [Sx], out=Sx.ap, in0=Sx.ap, in1=pou.ap.rearrange("p (c t) -> p c t", c=4), op=ALU.add)
                yield
                P.dma("sp", ssms_d[s_, hh * 4:(hh + 1) * 4].rearrange("h k v -> k h v"), Sx.ap, "st_ssms%d_%d" % (s_ % 2, hh), R=[Sx])
                pq = ps()
                for j in range(4):
                    h_ = hh * 4 + j
                    MM(pq.ap[0:1, j * 128:(j + 1) * 128], qkS.ap[:, h_ // 2, s_:s_ + 1], Sx.ap[:, j, :], True, True, R=[qkS, Sx], W=[pq],
                       signal=(j == 3))
                ol = rowo.ap[0:1, :]
                o4 = ol.rearrange("p (c t) -> p c t", c=4)
                ACT([pq], [rowo], out=ol, in_=pq.ap[0:1, 0:512], func=AF.Square)
                P.op("dve", ("tensor_reduce", dict(out=ss1.ap[0:1, :], in_=o4, axis=AX.X, op=ALU.add)), R=[rowo], W=[ss1])
                ACT([ss1, vecs], [ss1], out=ss1.ap[0:1, :], in_=ss1.ap[0:1, :], func=AF.Ln, bias=V("eps_rms")[0:1, :], scale=1.0 / 128)
                ACT([ss1], [ss1], out=ss1.ap[0:1, :], in_=ss1.ap[0:1, :], func=AF.Exp, scale=-0.5)
                TT("dve", [pq, ss1], [rowo], out=o4, in0=pq.ap[0:1, 0:512].rearrange("p (c t) -> p c t", c=4),
                   in1=ss1.ap[0:1, :].unsqueeze(2).to_broadcast([1, 4, 128]), op=ALU.mult)
                yield
                TT("dve", [rowo, vecs], [rowo], out=o4, in0=o4, in1=V("gdn_norm")[0:1, :].unsqueeze(1).to_broadcast([1, 4, 128]), op=ALU.mult)
                TT("dve", [rowo, rowz], [rowo], out=ol, in0=ol, in1=rowz.ap[0:1, :], op=ALU.mult)
                for j in range(4):
                    h_ = hh * 4 + j
                    MM(pogS.ap[:, h_ * NS2 + s_:h_ * NS2 + s_ + 1], rowo.ap[0:1, j * 128:(j + 1) * 128], onesf.ap[0:1, 0:1], True, True,
                       R=[rowo, onesf], W=[pogS], signal=(j == 3))
            gens = [hchain(0), hchain(1)]
            alive = [True, True]
            while any(alive):
                for gi_ in range(2):
                    if alive[gi_]:
                        try:
                            next(gens[gi_])
                        except StopIteration:
                            alive[gi_] = False
        CP("dve", [pogS], [ogS], out=ogS.ap, in_=pogS.ap[:, 0:8 * NS2].rearrange("p (c t) -> p c t", c=8))
        gsi = 2 * NTI
        P.dma("sp", gos[gsi].ap[:, :].rearrange("(c p) t -> p c t", p=128), ogS.ap, "go_ws", R=[ogS], W=[gos[gsi]])
        gather(go_src[gsi], go_dst[gsi], gos[gsi], god[gsi], "cc_gos")

        arena.reset()
        scratch = mk_scratch()
        wo0, sl0 = ws.acquire("g_wo0")
        wo1, sl1 = ws.acquire("g_wo1")
        oA = arena.alloc([16, TW], BF16)
        oB = arena.alloc([16, TW], BF16)
        osel = arena.alloc([16, TW], BF16)
        for ti, (t0, w, smp) in enumerate(tiles):
            for (dst, rr_) in ((oA, 0), (oB, 1)):
                if smp:
                    gsrc, cA = god[2 * NTI], rr_ * NS
                else:
                    gsrc, cA = god[rr_ * NTI + ti], 0
                P.dma("pool", dst.ap[:, :, 0:w], gsrc.ap[:, cA:cA + w].rearrange("(c p) t -> p c t", p=128), "ld_go%d" % rr_, R=[gsrc], W=[dst])
            TS("dve", [oA, flag], [osel], out=osel.ap[:, :, 0:w], in0=oA.ap[:, :, 0:w], scalar1=flag.ap[:, 1:2], scalar2=None, op0=ALU.mult)
            STT("dve", [oB, flag, osel], [osel], out=osel.ap[:, :, 0:w], in0=oB.ap[:, :, 0:w], scalar=flag.ap[:, 0:1], in1=osel.ap[:, :, 0:w],
                op0=ALU.mult, op1=ALU.add)
            for n2 in range(8):
                pb = ps()
                mm_group(P, pb.ap[:, 0:w], [((wo0 if k < 8 else wo1).ap[:, k % 8, n2 * 128:(n2 + 1) * 128], osel.ap[:, k, 0:w]) for k in range(16)],
                         R=[wo0, wo1, osel], W=[pb])
                resid_add(ti, n2, pb.ap[:, 0:w], pb, 2, scratch)
        ws.release(sl0)
        ws.release(sl1)

    def sconv_mixer():
        nt = len(tiles) - 1
        c0_, sl0 = ws.acquire("sc_0")
        c1_, sl1 = ws.acquire("sc_1")
        cb_, slb = ws.acquire("sc_b")
        co_, slo = ws.acquire("sc_o")
        scs = [c0_, c1_]
        wsc = V("sconv_w_conv").rearrange("p (c k) -> p c k", c=8)

        def proj_p(hT_b, col0, w, out_fn, scratch):
            for c in range(8):
                sb = scs[c // 4]
                j = c % 4
                pg = ps()
                mm_group(P, pg.ap[:, 0:w], [(sb.ap[:, k, j * 128:(j + 1) * 128], hT_b.ap[:, k, col0:col0 + w]) for k in range(8)],
                         R=[sb, hT_b], W=[pg])
                ph = ps()
                mm_group(P, ph.ap[:, 0:w], [(sb.ap[:, k, 512 + j * 128:512 + (j + 1) * 128], hT_b.ap[:, k, col0:col0 + w]) for k in range(8)],
                         R=[sb, hT_b], W=[ph])
                gt = scratch["tmp"][c % 2]
                ACT([pg], [gt], out=gt.ap[:, 0:w], in_=pg.ap[:, 0:w], func=AF.Identity)
                oap, obuf = out_fn(c)
                TT("dve", [gt, ph], [obuf], out=oap, in0=gt.ap[:, 0:w], in1=ph.ap[:, 0:w], op=ALU.mult)

        def gate_out(hT_b, w, y_fn, ybufs, ti, zT, scratch):
            for c in range(8):
                pgb = ps()
                mm_group(P, pgb.ap[:, 0:w], [(cb_.ap[:, k, c * 128:(c + 1) * 128], hT_b.ap[:, k, 0:w]) for k in range(8)],
                         R=[cb_, hT_b], W=[pgb])
                yap = y_fn(c)
                TT("dve", ybufs + [pgb], [zT], out=zT.ap[:, c, 0:w], in0=yap, in1=pgb.ap[:, 0:w], op=ALU.mult)
            for n2 in range(8):
                pb = ps()
                mm_group(P, pb.ap[:, 0:w], [(co_.ap[:, k, n2 * 128:(n2 + 1) * 128], zT.ap[:, k, 0:w]) for k in range(8)],
                         R=[co_, zT], W=[pb])
                resid_add(ti, n2, pb.ap[:, 0:w], pb, 2, scratch)

        arena.reset()
        scratch = mk_scratch()
        hT = arena.alloc([8, TW], BF16)
        pW = arena.alloc([8, 2 + TW], F32)
        yy = arena.alloc([8, TW], F32)
        zT = arena.alloc([8, TW], BF16)
        ph2 = arena.alloc([8, 2], F32)
        tl = nt - 1
        t0l, wl, _ = tiles[tl]
        norm_mod_src(xb[tl], xb[tl].ap[:, :, wl - 2:wl], 2, False, 1, 0, hT, scratch)
        proj_p(hT, 0, 2, lambda c: (ph2.ap[:, c, :], ph2), scratch)
        P.dma("sp", scp_d.rearrange("(c p) t -> p c t", p=128), ph2.ap, "st_scp", R=[ph2])
        if CFG.get("COLL", True):
            hsrc = Buf(sch_src.ap())
            hdst = Buf(sch_dst.ap())
            P.dma("sp", hsrc.ap[:, :], ph2.ap.rearrange("p c t -> p (c t)"), "sch_w", R=[ph2], W=[hsrc])
            P.collective(sch_src.ap().opt(), sch_dst.ap().opt(), "cc_sc", R=[hsrc], W=[hdst])
            P.dma("sp", ph2.ap.rearrange("p c t -> p (c t)"), hdst.ap[0:128, :], "sch_r", R=[hdst], W=[ph2])
        TS("dve", [ph2, flag], [pW], out=pW.ap[:, :, 0:2], in0=ph2.ap, scalar1=flag.ap[:, 0:1], scalar2=None, op0=ALU.mult)
        for ti in range(nt):
            t0, w, _ = tiles[ti]
            norm_mod(ti, 1, 0, hT, scratch)
            proj_p(hT, 0, w, lambda c: (pW.ap[:, c, 2:2 + w], pW), scratch)
            for c in range(8):
                TS("dve", [pW, vecs], [yy], out=yy.ap[:, c, 0:w], in0=pW.ap[:, c, 0:w], scalar1=wsc[:, c, 0:1], scalar2=None, op0=ALU.mult)
                for k in (1, 2):
                    STT("dve", [pW, vecs, yy], [yy], out=yy.ap[:, c, 0:w], in0=pW.ap[:, c, k:k + w], scalar=wsc[:, c, k:k + 1],
                        in1=yy.ap[:, c, 0:w], op0=ALU.mult, op1=ALU.add)
            gate_out(hT, w, lambda c: yy.ap[:, c, 0:w], [yy], ti, zT, scratch)
            if ti != nt - 1:
                CP("dve", [pW], [pW], out=pW.ap[:, :, 0:2], in_=pW.ap[:, :, w:w + 2])

        arena.reset()
        scratch = mk_scratch()
        ti = nt
        hT = arena.alloc([8, NS], BF16)
        pS_ = arena.alloc([8, NS], F32)
        st = arena.alloc([8, NS, 2], F32)
        yy = arena.alloc([8, NS], F32)
        t3 = arena.alloc([8, NS], F32)
        zT = arena.alloc([8, NS], BF16)
        tok = arena.alloc([D], F32)
        P.dma("sp", st.ap, stscT_d.rearrange("(c p) s k -> p c s k", p=128), "ld_stsc", W=[st])
        norm_mod(ti, 1, 0, hT, scratch)
        proj_p(hT, 0, NS, lambda c: (pS_.ap[:, c, :], pS_), scratch)
        TT("dve", [st, vecs], [yy], out=yy.ap, in0=st.ap[:, :, :, 0], in1=wsc[:, :, 0:1].to_broadcast([128, 8, NS]), op=ALU.mult)
        TT("dve", [st, vecs], [t3], out=t3.ap, in0=st.ap[:, :, :, 1], in1=wsc[:, :, 1:2].to_broadcast([128, 8, NS]), op=ALU.mult)
        TT("dve", [yy, t3], [yy], out=yy.ap, in0=yy.ap, in1=t3.ap, op=ALU.add)
        TT("dve", [pS_, vecs], [t3], out=t3.ap, in0=pS_.ap, in1=wsc[:, :, 2:3].to_broadcast([128, 8, NS]), op=ALU.mult)
        TT("dve", [yy, t3], [yy], out=yy.ap, in0=yy.ap, in1=t3.ap, op=ALU.add)
        gate_out(hT, NS, lambda c: yy.ap[:, c, :], [yy], ti, zT, scratch)
        P.dma("sp", scs_d[:, 0, :], stsc_d[:, 1, :], "st_scs")
        to_token_major([pS_.ap[:, c, :] for c in range(8)], [pS_], tok)
        P.dma("sp", scs_d[:, 1, :], tok.ap[0:NS, :], "st_scs", R=[tok])
        ws.release(sl0)
        ws.release(sl1)
        ws.release(slb)
        ws.release(slo)

    def mk_scratch():
        return {"sq": [arena.alloc([TW], BF16) for _ in range(2)],
                "tmp": [arena.alloc([TW], F32) for _ in range(2)],
                "tmp2": arena.alloc([NS], F32),
                "rs": arena.alloc([TW], F32)}

    def mlp(li):
        arena.reset()
        scratch = mk_scratch()
        hT = [arena.alloc([8, w], BF16) for (t0, w, _) in tiles]
        hid = [arena.alloc([8, TW], BF16) for _ in range(2)]
        rr = scratch["tmp"]
        for f in range(4):
            up, us = ws.acquire("up%d_%d" % (li, f))
            dn, ds = ws.acquire("down%d_%d" % (li, f))

            def up_stage(ti, hb):
                t0, w, smp = tiles[ti]
                for n in range(8):
                    pb = ps()
                    mm_group(P, pb.ap[:, 0:w], [(up.ap[:, k, n * 128:(n + 1) * 128], hT[ti].ap[:, k, :]) for k in range(8)],
                             R=[up, hT[ti]], W=[pb])
                    r = rr[n % 2]
                    ACT([pb], [r], out=r.ap[:, 0:w], in_=pb.ap[:, 0:w], func=AF.Relu)
                    TT("dve", [r], [hb], out=hb.ap[:, n, 0:w], in0=r.ap[:, 0:w], in1=r.ap[:, 0:w], op=ALU.mult)

            def down_stage(ti, hb):
                t0, w, smp = tiles[ti]
                for n2 in range(8):
                    pb = ps()
                    mm_group(P, pb.ap[:, 0:w], [(dn.ap[:, k, n2 * 128:(n2 + 1) * 128], hb.ap[:, k, 0:w]) for k in range(8)],
                             R=[dn, hb], W=[pb])
                    resid_add(ti, n2, pb.ap[:, 0:w], pb, 5, scratch)
            ntl = len(tiles)
            if f == 0:
                norm_mod(0, 4, 3, hT[0], scratch)
                if ntl > 1:
                    norm_mod(1, 4, 3, hT[1], scratch)
            up_stage(0, hid[0])
            for ti in range(ntl):
                if ti + 1 < ntl:
                    up_stage(ti + 1, hid[(ti + 1) % 2])
                    if f == 0 and ti + 2 < ntl:
                        norm_mod(ti + 2, 4, 3, hT[ti + 2], scratch)
                down_stage(ti, hid[ti % 2])
            ws.release(us)
            ws.release(ds)

    def dbg_dump(idx):
        if dbg_d is None:
            return
        for ti, (t0, w, _) in enumerate(tiles):
            P.dma("sp", dbg_d[idx, :, t0:t0 + w].rearrange("(c p) t -> p c t", p=128), xb[ti].ap, "st_dbg", R=[xb[ti]])

    def final_norm():
        arena.reset()
        scratch = mk_scratch()
        yo = [arena.alloc([8, TW], F32) for _ in range(2)]
        for ti, (t0, w, smp) in enumerate(tiles):
            rs = rstd_tile(ti, scratch)
            yb = yo[ti % 2]
            for c in range(8):
                STT("dve", [xb[ti], rs, vecs], [yb], out=yb.ap[:, c, 0:w], in0=xb[ti].ap[:, c, :],
                    scalar=V("norm_final", c, 1), in1=rs.ap[:, 0:w], op0=ALU.mult, op1=ALU.mult)
            P.dma("sp", yT_d[:, t0:t0 + w].rearrange("(c p) t -> p c t", p=128), yb.ap[:, :, 0:w], "st_y%d" % (ti % 2), R=[yb])

    for li in range(CFG["LAYERS"]):
        adaln(li)
        if CFG["MIX"] and li == 0:
            conf_mixer()
        if CFG["MIX"] and li == 1:
            swa_mixer()
        if CFG["MIX"] and li == 2:
            gdn_mixer()
        if CFG["MIX"] and li == 3:
            sconv_mixer()
        dbg_dump(2 * li)
        mlp(li)
        dbg_dump(2 * li + 1)
    final_norm()

    for k in P.dkeys:
        if k.startswith("st_"):
            P._wait("sp", k, P.cnt[k])

    with nc.Block() as block:
        @block.tensor
        def _(e):
            P.replay(e, "pe")

        @block.scalar
        def _(e):
            P.replay(e, "act")

        @block.vector
        def _(e):
            P.replay(e, "dve")

        @block.gpsimd
        def _(e):
            P.replay(e, "pool")

        @block.sync
        def _(e):
            P.replay(e, "sp")
    print("[kernel] instructions:", P.ninstr, {k: len(v) for k, v in P.ops.items()})
    nc._dbg_names = P.names
    return nc


def make_in_maps(inp):
    NTI = CFG["NTI"]
    TP = NTI * TW
    vecs_hf = [vec_layout(inp, 0).build(), vec_layout(inp, 1).build()]
    cm = const_mats()
    cmats = np.concatenate([cm[n] for n in CM_NAMES], axis=1).astype(np.float32)
    maps = []
    for c in range(8):
        s, hf = c // 2, c % 2
        base = hf * 2048
        xp = inp["x_prompt"][s, base:base + TP, :]
        xs = inp["x_sample"][c * NS:(c + 1) * NS, 0, :]
        xT = np.ascontiguousarray(np.concatenate([xp, xs], 0).T)
        cT = np.ascontiguousarray(np.concatenate([inp["c_prompt"][s:s + 1], inp["c_sample"][c * NS:(c + 1) * NS]], 0).T)
        vecs = vecs_hf[hf]
        m = {"xT": xT, "cT": cT, "vecs": vecs, "cmats": cmats,
             "w_ada": inp["w_ada"], "w_up": inp["w_up"], "w_down": inp["w_down"]}
        xh = np.zeros((D, 32), np.float32)
        if hf == 1:
            xh[:, 0:30] = inp["x_prompt"][s, base - 30:base, :].T
        m["xhalo"] = xh
        fl = np.zeros((128, 2), np.float32)
        fl[:, 0] = hf
        fl[:, 1] = 1 - hf
        m["flag"] = fl
        sl = slice(c * NS, (c + 1) * NS)
        m["conf_w_pw1"] = inp["conf_w_pw1"]
        m["conf_w_pw2"] = inp["conf_w_pw2"]
        m["st_conf"] = np.ascontiguousarray(inp["state_conf_conv"][0, sl])
        m["swa_w_qkv"] = inp["swa_w_qkv"]
        m["swa_w_o"] = inp["swa_w_o"]
        pos = np.concatenate([np.arange(base, base + TP), np.full(NS, 8192)]).astype(np.float32)
        inv = np.power(np.float32(500000.0), -np.arange(8, dtype=np.float32) * np.float32(2.0) / np.float32(16.0)).astype(np.float32)
        ang = (pos[None, :] * inv[:, None]).astype(np.float32)
        rc = np.ones((128, TP + NS), np.float32)
        rs_ = np.zeros((128, TP + NS), np.float32)
        for hb in (0, 64):
            rc[hb:hb + 8] = np.cos(ang)
            rc[hb + 8:hb + 16] = np.cos(ang)
            rs_[hb:hb + 8] = np.sin(ang)
            rs_[hb + 8:hb + 16] = np.sin(ang)
        m["ropeC"] = rc
        m["ropeS"] = rs_
        ck = inp["cache_swa_k"][0, sl].reshape(NS, 128, 256)
        m["ck"] = np.ascontiguousarray(ck)
        m["cv"] = np.ascontiguousarray(inp["cache_swa_v"][0, sl].reshape(NS, 128, 256))
        ckt = ck.reshape(NS, 128, 4, 64).transpose(3, 0, 2, 1)
        m["ckT"] = np.ascontiguousarray(np.concatenate([ckt, ckt], 0))
        chans = gdn_channels(hf)
        wi = inp["gdn_w_in"][0]
        zc = 4096 + np.concatenate([np.arange(h * 128, (h + 1) * 128) for h in range(8 * hf, 8 * hf + 8)])
        m["gdn_w_in_c"] = np.ascontiguousarray(np.concatenate([wi[:, chans], wi[:, zc]], axis=1))
        m["gdn_ba_c"] = np.ascontiguousarray(np.concatenate([wi[:, 6144 + 8 * hf:6144 + 8 * hf + 8], wi[:, 6160 + 8 * hf:6160 + 8 * hf + 8]], axis=1))
        m["gdn_w_o"] = inp["gdn_w_o"]
        ps_ = slice((c // 2) * 2 * NS, (c // 2 + 1) * 2 * NS)
        m["g_ssm"] = np.ascontiguousarray(inp["state_gdn_ssm"][0, ps_, 8 * hf:8 * hf + 8])
        gcv = inp["state_gdn_conv"][0, ps_][:, :, chans]
        m["g_conv"] = np.ascontiguousarray(gcv)
        gT = np.zeros((2048, 2 * NS, 4), np.float32)
        gT[:, :, 0:3] = gcv.transpose(2, 0, 1)
        m["g_convT"] = gT
        m["sconv_w_in"] = inp["sconv_w_in"]
        m["sconv_w_out"] = inp["sconv_w_out"]
        m["st_sc"] = np.ascontiguousarray(inp["state_sconv"][0, sl])
        m["st_scT"] = np.ascontiguousarray(inp["state_sconv"][0, sl].transpose(2, 0, 1))
        m["st_confT"] = np.ascontiguousarray(inp["state_conf_conv"][0, sl].transpose(2, 0, 1))
        maps.append(m)
    return maps


_NC_CACHE = {}


def kernel(**inp):
    inp = {k: np.asarray(v) for k, v in inp.items()}
    NTI = CFG["NTI"]
    TP = NTI * TW
    key = (NTI, CFG["LAYERS"], CFG["MIX"], CFG["DBG"])
    if key not in _NC_CACHE:
        _NC_CACHE[key] = build_program()
    nc = _NC_CACHE[key]
    maps = make_in_maps(inp)
    res = run_bass_kernel_spmd(nc, maps, core_ids=list(range(8)))
    R = res.results
    y_p = np.zeros((4, 4096, D), np.float32)
    y_s = np.zeros((128, 1, D), np.float32)
    conf_p = np.zeros((1, 4, 30, D), np.float32)
    conf_s = np.zeros((1, 128, 30, D), np.float32)
    k_p = np.zeros((1, 4, 128, 4, 64), np.float32)
    v_p = np.zeros((1, 4, 128, 4, 64), np.float32)
    k_s = np.zeros((1, 128, 128, 4, 64), np.float32)
    v_s = np.zeros((1, 128, 128, 4, 64), np.float32)
    ssm_p = np.zeros((1, 4, 16, 128, 128), np.float32)
    ssm_s = np.zeros((1, 128, 16, 128, 128), np.float32)
    gc_p = np.zeros((1, 4, 3, 4096), np.float32)
    gc_s = np.zeros((1, 128, 3, 4096), np.float32)
    sc_p = np.zeros((1, 4, 2, D), np.float32)
    sc_s = np.zeros((1, 128, 2, D), np.float32)
    for c in range(8):
        s, hf = c // 2, c % 2
        r = R[c]
        sl = slice(c * NS, (c + 1) * NS)
        ps_ = slice(s * 2 * NS, (s + 1) * 2 * NS)
        yT = np.asarray(r["yT"]).reshape(D, -1)
        y_p[s, hf * 2048:hf * 2048 + TP] = yT[:, :TP].T
        y_s[sl, 0] = yT[:, TP:].T
        if "conf_s" in r:
            conf_s[0, sl] = np.asarray(r["conf_s"]).reshape(NS, 30, D)
            if hf == 1:
                conf_p[0, s] = np.asarray(r["conf_p"]).reshape(D, 30).T
        if "swa_ks" in r:
            k_s[0, sl] = np.asarray(r["swa_ks"]).reshape(NS, 128, 4, 64)
            v_s[0, sl] = np.asarray(r["swa_vs"]).reshape(NS, 128, 4, 64)
            if hf == 1:
                kp = np.asarray(r["swa_kp"]).reshape(128, 4, 128)[0:64]
                k_p[0, s] = kp.transpose(2, 1, 0)
                v_p[0, s] = np.asarray(r["swa_vp"]).reshape(128, 4, 64)
        if "g_ssm_s" in r:
            chans = gdn_channels(hf)
            ssm_p[0, s, 8 * hf:8 * hf + 8] = np.asarray(r["g_ssm_p"]).reshape(8, 128, 128)
            ssm_s[0, ps_, 8 * hf:8 * hf + 8] = np.asarray(r["g_ssm_s"]).reshape(2 * NS, 8, 128, 128)
            gc_p[0, s][:, chans] = np.asarray(r["g_conv_p"]).reshape(2048, 3).T
            gs = np.asarray(r["g_conv_s"]).reshape(2 * NS, 3, 2048)
            tmp = gc_s[0, ps_]
            tmp[:, :, chans] = gs
            gc_s[0, ps_] = tmp
        if "sc_s" in r:
            sc_s[0, sl] = np.asarray(r["sc_s"]).reshape(NS, 2, D)
            if hf == 1:
                sc_p[0, s] = np.asarray(r["sc_p"]).reshape(D, 2).T
    kernel.last = R
    return (y_p, y_s, conf_p, conf_s, k_p, k_s, v_p, v_s, ssm_p, ssm_s, gc_p, gc_s, sc_p, sc_s)
```

```python
import numpy as np
import ml_dtypes
import concourse.bass as bass
import concourse.mybir as mybir
from concourse.bass_utils import run_bass_kernel_spmd

F32 = mybir.dt.float32
BF16 = mybir.dt.bfloat16
AF = mybir.ActivationFunctionType
ALU = mybir.AluOpType
AX = mybir.AxisListType

CFG = {"NTI": 4, "LAYERS": 4, "MIX": True, "DBG": False}
D = 1024
NCH = 8
NS = 16
TW = 512
ENG = ("pe", "act", "dve", "pool", "sp")
PAIRS = [[0, 1], [2, 3], [4, 5], [6, 7]]


class Buf:
    __slots__ = ("ap", "w", "r")

    def __init__(self, ap, share=None):
        self.ap = ap
        if share is None:
            self.w = {}
            self.r = {}
        else:
            self.w = share.w
            self.r = share.r


class Prog:
    def __init__(self, nc):
        self.nc = nc
        self.ops = {e: [] for e in ENG}
        self.cnt = {}
        self.sems = {}
        self.waited = {}
        for e in ("pe", "act", "dve", "pool"):
            self.sems[e] = nc.alloc_semaphore("c_" + e)
            self.cnt[e] = 0
        self.dkeys = []
        self.ninstr = 0
        self.names = {}

    def _line(self):
        if not CFG.get("DBG"):
            return 0
        import sys
        f = sys._getframe(2)
        out = []
        while f is not None and len(out) < 4:
            if f.f_code.co_name not in ("op", "ACT", "TT", "STT", "TS", "CP", "MM", "mm_group", "dma"):
                out.append(f.f_lineno)
            f = f.f_back
        return out

    def dkey(self, key):
        if key not in self.sems:
            self.sems[key] = self.nc.alloc_semaphore("d_" + key)
            self.cnt[key] = 0
            self.dkeys.append(key)
        return key

    def _wait(self, eng, k, n):
        if k == "pe" and eng == "pe":
            return
        if self.waited.get((eng, k), 0) >= n:
            return
        self.waited[(eng, k)] = n
        self.ops[eng].append(("w", self.sems[k], n))
        self.ninstr += 1

    def _deps(self, eng, R, W):
        deps = {}
        for b in R:
            for k, n in b.w.items():
                if deps.get(k, 0) < n:
                    deps[k] = n
        for b in W:
            for k, n in b.w.items():
                if deps.get(k, 0) < n:
                    deps[k] = n
            for k, n in b.r.items():
                if deps.get(k, 0) < n:
                    deps[k] = n
        for k, n in deps.items():
            self._wait(eng, k, n)

    def op(self, eng, fn, R=(), W=(), signal=True):
        self._deps(eng, R, W)
        if signal:
            self.cnt[eng] += 1
            tag = self.cnt[eng]
        else:
            tag = self.cnt[eng] + 1
        self.ops[eng].append(("i", fn, self.sems[eng] if signal else None, 1, self._line()))
        self.ninstr += 1
        for b in R:
            if b.r.get(eng, 0) < tag:
                b.r[eng] = tag
        for b in W:
            if b.w.get(eng, 0) < tag:
                b.w[eng] = tag

    def dma(self, q, out_ap, in_ap, key, R=(), W=()):
        self.dkey(key)
        self._deps(q, R, W)
        self.cnt[key] += 16
        tag = self.cnt[key]
        self.ops[q].append(("i", ("dma_start", dict(out=out_ap, in_=in_ap)), self.sems[key], 16))
        self.ninstr += 1
        for b in R:
            if b.r.get(key, 0) < tag:
                b.r[key] = tag
        for b in W:
            if b.w.get(key, 0) < tag:
                b.w[key] = tag

    def collective(self, ins_ap, outs_ap, key, R=(), W=()):
        self.dkey(key)
        self._deps("pool", R, W)
        self.cnt[key] += 1
        tag = self.cnt[key]

        def fn(e):
            return e.collective_compute("AllGather", ALU.bypass, replica_groups=PAIRS, ins=[ins_ap], outs=[outs_ap])
        self.ops["pool"].append(("c", fn, self.sems[key]))
        self.ninstr += 1
        for b in R:
            b.r[key] = tag
        for b in W:
            b.w[key] = tag

    def barrier(self):
        for e in ENG:
            for k in self.sems:
                if self.cnt[k] > 0:
                    self._wait(e, k, self.cnt[k])

    def replay(self, e, name):
        for it in self.ops[name]:
            if it[0] == "w":
                e.wait_ge(it[1], it[2])
            elif it[0] == "c":
                it[1](e).then_inc(it[2])
            else:
                f = it[1]
                if CFG.get("DBG") and len(it) > 4:
                    self.names[self.nc.get_next_instruction_name()] = it[4]
                if isinstance(f, tuple):
                    ins = getattr(e, f[0])(**f[1])
                else:
                    ins = f(e)
                if it[2] is not None:
                    ins.then_inc(it[2], it[3])


class Arena:
    def __init__(self, P, nwords, name):
        self.P = P
        self.t = P.nc.alloc_sbuf_tensor(name, [128, nwords], F32).ap()
        self.n = nwords
        self.off = 0

    def alloc(self, shape, dt=F32, name=None):
        if name is not None:
            self.P.names["buf:" + name] = (self.off, list(shape), "f32" if dt == F32 else "bf16")
        n = 1
        for s in shape:
            n *= s
        words = n if dt == F32 else (n + 1) // 2
        assert self.off + words <= self.n, ("arena overflow", self.off, words, self.n)
        ap = self.t[:, self.off:self.off + words]
        self.off += words
        if dt != F32:
            ap = ap.bitcast(dt)
            if n % 2:
                ap = ap[:, 0:n]
        if len(shape) == 2:
            ap = ap.rearrange("p (a b) -> p a b", a=shape[0])
        elif len(shape) == 3:
            ap = ap.rearrange("p (a b c) -> p a b c", a=shape[0], b=shape[1])
        return Buf(ap)

    def reset(self):
        self.P.barrier()
        self.off = 0


class WStream:
    def __init__(self, P, nslots):
        self.P = P
        self.slots = []
        for i in range(nslots):
            t = P.nc.alloc_sbuf_tensor("slab%d" % i, [128, 8, 1024], BF16).ap()
            self.slots.append(Buf(t))
        self.free = list(range(nslots))
        self.specs = []
        self.nxt = 0
        self.loaded = []

    def start(self, specs):
        self.specs = specs
        while self.free and self.nxt < len(self.specs):
            self._load()

    def _load(self):
        slot = self.free.pop(0)
        name, pieces = self.specs[self.nxt]
        b = self.slots[slot]
        for (src, col0, n) in pieces:
            self.P.dma("pool", b.ap[:, :, col0:col0 + n], src.rearrange("(kc p) n -> p kc n", p=128),
                       "slab%d" % slot, W=[b])
        self.loaded.append((name, slot))
        self.nxt += 1

    def acquire(self, name):
        nm, slot = self.loaded.pop(0)
        assert nm == name, (nm, name)
        return self.slots[slot], slot

    def release(self, slot):
        self.free.append(slot)
        if self.nxt < len(self.specs):
            self._load()


def mm_group(P, out_ap, pairs, R, W):
    n = len(pairs)
    for i, (l, r) in enumerate(pairs):
        P.op("pe", ("matmul", dict(out=out_ap, lhsT=l, rhs=r, start=(i == 0), stop=(i == n - 1))),
             R=R, W=W, signal=(i == n - 1))


class VecPack:
    def __init__(self):
        self.cols = []
        self.idx = {}
        self.n = 0

    def add(self, name, arr):
        arr = np.ascontiguousarray(arr, dtype=np.float32)
        assert arr.shape[0] == 128
        self.idx[name] = (self.n, arr.shape[1])
        self.cols.append(arr)
        self.n += arr.shape[1]

    def build(self):
        return np.concatenate(self.cols, axis=1)


def fm(v):
    v = np.asarray(v, dtype=np.float32)
    return np.ascontiguousarray(v.reshape(-1, 128).T)


def vec_layout(inp=None, hf=0):
    vp = VecPack()
    z = (lambda *s: np.zeros(s, np.float32))
    g = (lambda k, *s: (np.asarray(inp[k], np.float32) if inp is not None else np.zeros(s, np.float32)))
    for i in range(4):
        vp.add("b_ada%d" % i, fm(g("b_ada", 4, 6144)[i]))
        vp.add("norm_mix%d" % i, fm(g("norm_mix", 4, 1024)[i]))
        vp.add("norm_mlp%d" % i, fm(g("norm_mlp", 4, 1024)[i]))
    vp.add("norm_final", fm(g("norm_final", 1024)))
    vp.add("conf_b_pw1", fm(g("conf_b_pw1", 1, 2048)[0]))
    vp.add("conf_b_dw", fm(g("conf_b_dw", 1, 1024)[0]))
    vp.add("conf_ln_g", fm(g("conf_ln_g", 1, 1024)[0]))
    vp.add("conf_ln_b", fm(g("conf_ln_b", 1, 1024)[0]))
    vp.add("conf_b_pw2", fm(g("conf_b_pw2", 1, 1024)[0]))
    wdw = g("conf_w_dw", 1, 31, 1024)[0]
    vp.add("conf_w_dw", np.ascontiguousarray(wdw.T.reshape(8, 128, 31).transpose(1, 0, 2)).reshape(128, 248))
    wsc = g("sconv_w_conv", 1, 3, 1024)[0]
    vp.add("sconv_w_conv", np.ascontiguousarray(wsc.T.reshape(8, 128, 3).transpose(1, 0, 2)).reshape(128, 24))
    sinks = g("swa_sinks", 1, 16)[0]
    es = np.zeros((128, 8), np.float32)
    for c in range(8):
        es[0:64, c] = sinks[2 * c]
        es[64:128, c] = sinks[2 * c + 1]
    vp.add("swa_sink", es)
    wc = g("gdn_w_conv", 1, 4, 4096)[0]
    chans = gdn_channels(hf)
    wcc = wc[:, chans]
    vp.add("gdn_wconv", np.ascontiguousarray(wcc.T.reshape(16, 128, 4).transpose(1, 0, 2)).reshape(128, 64))
    hs_ = slice(8 * hf, 8 * hf + 8)
    vp.add("gdn_dtb", np.tile(g("gdn_dt_bias", 1, 16)[0][hs_][None, :], (128, 1)))
    vp.add("gdn_alog", np.tile(g("gdn_a_log", 1, 16)[0][hs_][None, :], (128, 1)))
    vp.add("gdn_norm", np.tile(g("gdn_norm", 1, 128)[0][None, :], (128, 1)))
    vp.add("one", np.ones((128, 1), np.float32))
    hm = np.zeros((128, 2), np.float32)
    hm[0:64, 0] = 1.0
    hm[64:128, 1] = 1.0
    vp.add("hmask", hm)
    vp.add("eps_rms", np.full((128, 1), 1e-6, np.float32))
    vp.add("eps_ln", np.full((128, 1), 1e-5, np.float32))
    return vp


def gdn_channels(hf):
    q = np.concatenate([np.arange(kh * 128, (kh + 1) * 128) for kh in range(4 * hf, 4 * hf + 4)])
    k = 1024 + q
    v = 2048 + np.concatenate([np.arange(h * 128, (h + 1) * 128) for h in range(8 * hf, 8 * hf + 8)])
    return np.concatenate([q, k, v])


def const_mats():
    i = np.arange(128)
    ident = np.eye(128, dtype=np.float32)
    triLE = (i[:, None] <= i[None, :]).astype(np.float32)
    triGE = (i[:, None] >= i[None, :]).astype(np.float32)
    triGT = (i[:, None] > i[None, :]).astype(np.float32)
    triLT = (i[:, None] < i[None, :]).astype(np.float32)
    half = ((i[:, None] // 64) == (i[None, :] // 64)).astype(np.float32)
    RT = np.zeros((128, 128), np.float32)
    for hb in (0, 64):
        for d in range(8):
            RT[hb + d + 8, hb + d] = -1.0
            RT[hb + d, hb + d + 8] = 1.0
    out = {"ident": ident, "triLE": triLE, "triGE": triGE, "triGT": triGT, "triLT": triLT, "half": half, "ropeRT": RT}
    out["bd8"] = ((i[:, None] // 8) == (i[None, :] // 8)).astype(np.float32)
    for b in (8, 16, 32, 64):
        out["cm%d" % b] = (((i[:, None] // (2 * b)) == (i[None, :] // (2 * b))) & ((i[:, None] // b) != (i[None, :] // b))).astype(np.float32)
    return out


CM_NAMES = ["ident", "triLE", "triGE", "triGT", "triLT", "half", "ropeRT", "bd8", "cm8", "cm16", "cm32", "cm64"]
NCMB = 7


def build_program():
    NTI = CFG["NTI"]
    TP = NTI * TW
    NTOK = TP + NS
    nc = bass.Bass("TRN2", target_bir_lowering=False)
    P = Prog(nc)
    vidx = vec_layout(None)
    NV = vidx.n

    def din(name, shape, dt=F32):
        return nc.dram_tensor(name, list(shape), dt, kind="ExternalInput").ap()

    def dout(name, shape, dt=F32):
        return nc.dram_tensor(name, list(shape), dt, kind="ExternalOutput").ap()

    xT_d = din("xT", [D, NTOK])
    cT_d = din("cT", [D, 17])
    vecs_d = din("vecs", [128, NV])
    cm_d = din("cmats", [128, len(CM_NAMES) * 128])
    W = {}
    W["w_ada"] = din("w_ada", [4, D, 6144])
    W["w_up"] = din("w_up", [4, D, 4096])
    W["w_down"] = din("w_down", [4, 4096, D])
    W["conf_w_pw1"] = din("conf_w_pw1", [1, D, 2048])
    W["conf_w_pw2"] = din("conf_w_pw2", [1, D, D])
    xh_d = din("xhalo", [D, 32])
    flag_d = din("flag", [128, 2])
    stconfT_d = din("st_confT", [D, NS, 30])
    stconf_d = din("st_conf", [NS, 30, D])
    W["swa_w_qkv"] = din("swa_w_qkv", [1, D, 1536])
    W["swa_w_o"] = din("swa_w_o", [1, D, D])
    ropeC_d = din("ropeC", [128, NTOK])
    ropeS_d = din("ropeS", [128, NTOK])
    ckT_d = din("ckT", [128, NS, 4, 128])
    ck_d = din("ck", [NS, 128, 256])
    cv_d = din("cv", [NS, 128, 256])
    swah_src = nc.dram_tensor("swah_src", [128, 768], F32)
    swah_dst = nc.dram_tensor("swah_dst", [256, 768], F32)
    kp_d = dout("swa_kp", [128, 4, 128])
    vp_d = dout("swa_vp", [128, 256])
    ks_d = dout("swa_ks", [NS, 128, 256])
    vs_d = dout("swa_vs", [NS, 128, 256])
    TG = 2 * NTOK
    W["gdn_w_in"] = din("gdn_w_in_c", [D, 3072])
    W["gdn_ba"] = din("gdn_ba_c", [D, 16])
    W["gdn_w_o"] = din("gdn_w_o", [1, 2048, D])
    ssm_d = din("g_ssm", [2 * NS, 8, 128, 128])
    gcvT_d = din("g_convT", [2048, 2 * NS, 4])
    gcv_d = din("g_conv", [2 * NS, 3, 2048])
    tws = [TW] * NTI + [NS]
    gh_src = [nc.dram_tensor("gh_src%d" % i, [D, w_], F32) for i, w_ in enumerate(tws)]
    gh_dst = [nc.dram_tensor("gh_dst%d" % i, [2 * D, w_], F32) for i, w_ in enumerate(tws)]
    gws = [TW] * (2 * NTI) + [2 * NS]
    go_src = [nc.dram_tensor("go_src%d" % i, [D, w_], F32) for i, w_ in enumerate(gws)]
    go_dst = [nc.dram_tensor("go_dst%d" % i, [2 * D, w_], F32) for i, w_ in enumerate(gws)]
    ssmp_d = dout("g_ssm_p", [8, 128, 128])
    ssms_d = dout("g_ssm_s", [2 * NS, 8, 128, 128])
    gcvp_d = dout("g_conv_p", [2048, 3])
    gcvs_d = dout("g_conv_s", [2 * NS, 3, 2048])
    W["sconv_w_in"] = din("sconv_w_in", [1, D, 3072])
    W["sconv_w_out"] = din("sconv_w_out", [1, D, D])
    stscT_d = din("st_scT", [D, NS, 2])
    stsc_d = din("st_sc", [NS, 2, D])
    sch_src = nc.dram_tensor("sch_src", [128, 16], F32)
    sch_dst = nc.dram_tensor("sch_dst", [256, 16], F32)
    scp_d = dout("sc_p", [D, 2])
    scs_d = dout("sc_s", [NS, 2, D])
    yT_d = dout("yT", [D, NTOK])
    confp_d = dout("conf_p", [D, 30])
    confs_d = dout("conf_s", [NS, 30, D])
    dbg_d = dout("dbg", [8, D, NTOK]) if CFG["DBG"] else None

    tiles = [(i * TW, TW, False) for i in range(NTI)] + [(TP, NS, True)]
    xt_all = nc.alloc_sbuf_tensor("sb_xT", [128, NCH, NTOK], F32).ap()
    xb = [Buf(xt_all[:, :, t0:t0 + w]) for (t0, w, _) in tiles]
    vecs = Buf(nc.alloc_sbuf_tensor("sb_vecs", [128, NV], F32).ap())
    cmat = Buf(nc.alloc_sbuf_tensor("sb_cmats", [128, len(CM_NAMES) * 128], F32).ap())
    cmatb = Buf(nc.alloc_sbuf_tensor("cmatsb", [128, NCMB * 128], BF16).ap())
    onesb = Buf(nc.alloc_sbuf_tensor("onesb", [128, 128], BF16).ap())
    onesf = Buf(nc.alloc_sbuf_tensor("onesf", [128, 128], F32).ap())
    mods = Buf(nc.alloc_sbuf_tensor("mods", [128, 48, 17], F32).ap())
    cmT = Buf(nc.alloc_sbuf_tensor("cmT", [128, NCH, 17], BF16).ap())
    cTf = Buf(nc.alloc_sbuf_tensor("cTf", [128, NCH, 17], F32).ap())
    flag = Buf(nc.alloc_sbuf_tensor("sb_flag", [128, 2], F32).ap())
    xh = Buf(nc.alloc_sbuf_tensor("sb_xh", [128, NCH, 32], F32).ap())
    psb = [Buf(nc.alloc_psum_tensor("ps%d" % i, [128, 512], F32).ap()) for i in range(8)]
    psi = [0]

    def ps():
        b = psb[psi[0] % 7]
        psi[0] += 1
        return b

    ws = WStream(P, 4)
    baw = Buf(nc.alloc_sbuf_tensor("sb_baw", [128, 8, 16], BF16).ap())
    arena = Arena(P, (nc.sbuf_bytes_remaining - 2048) // 4, "arena")
    print("[kernel] arena words", arena.n)

    def V(name, c0=0, n=None):
        o, k = vidx.idx[name]
        if n is None:
            n = k - c0
        return vecs.ap[:, o + c0:o + c0 + n]

    def CMf(name):
        i = CM_NAMES.index(name)
        return cmat.ap[:, i * 128:(i + 1) * 128]

    def CMb(name):
        i = CM_NAMES.index(name)
        return cmatb.ap[:, i * 128:(i + 1) * 128]

    specs = []
    for li in range(CFG["LAYERS"]):
        for g in range(6):
            specs.append(("ada%d_%d" % (li, g), [(W["w_ada"][li, :, g * 1024:(g + 1) * 1024], 0, 1024)]))
        if CFG["MIX"] and li == 0:
            w1 = W["conf_w_pw1"][0]
            specs.append(("pw1_0", [(w1[:, 0:512], 0, 512), (w1[:, 1024:1536], 512, 512)]))
            specs.append(("pw1_1", [(w1[:, 512:1024], 0, 512), (w1[:, 1536:2048], 512, 512)]))
            specs.append(("pw2", [(W["conf_w_pw2"][0], 0, 1024)]))
        if CFG["MIX"] and li == 1:
            wq = W["swa_w_qkv"][0]
            specs.append(("qkv_0", [(wq[:, 0:1024], 0, 1024)]))
            pcs = []
            for g in range(4):
                pcs.append((wq[:, 1024 + g * 64:1024 + (g + 1) * 64], g * 128, 64))
                pcs.append((wq[:, 1024 + g * 64:1024 + (g + 1) * 64], g * 128 + 64, 64))
            pcs.append((wq[:, 1280:1536], 512, 256))
            specs.append(("qkv_1", pcs))
            specs.append(("wo", [(W["swa_w_o"][0], 0, 1024)]))
        if CFG["MIX"] and li == 2:
            wi = W["gdn_w_in"]
            specs.append(("g_qk", [(wi[:, 0:1024], 0, 1024)]))
            specs.append(("g_v", [(wi[:, 1024:2048], 0, 1024)]))
            specs.append(("g_z", [(wi[:, 2048:3072], 0, 1024)]))
            specs.append(("g_wo0", [(W["gdn_w_o"][0, 0:1024, :], 0, 1024)]))
            specs.append(("g_wo1", [(W["gdn_w_o"][0, 1024:2048, :], 0, 1024)]))
        if CFG["MIX"] and li == 3:
            wsi = W["sconv_w_in"][0]
            specs.append(("sc_0", [(wsi[:, 1024:1536], 0, 512), (wsi[:, 2048:2560], 512, 512)]))
            specs.append(("sc_1", [(wsi[:, 1536:2048], 0, 512), (wsi[:, 2560:3072], 512, 512)]))
            specs.append(("sc_b", [(wsi[:, 0:1024], 0, 1024)]))
            specs.append(("sc_o", [(W["sconv_w_out"][0], 0, 1024)]))
        for f in range(4):
            specs.append(("up%d_%d" % (li, f), [(W["w_up"][li, :, f * 1024:(f + 1) * 1024], 0, 1024)]))
            specs.append(("down%d_%d" % (li, f), [(W["w_down"][li, f * 1024:(f + 1) * 1024, :], 0, 1024)]))

    def ACT(R, W, **kw):
        P.op("act", ("activation", kw), R=R, W=W)

    def TT(eng, R, W, **kw):
        P.op(eng, ("tensor_tensor", kw), R=R, W=W)

    def STT(eng, R, W, **kw):
        P.op(eng, ("scalar_tensor_tensor", kw), R=R, W=W)

    def TS(eng, R, W, **kw):
        P.op(eng, ("tensor_scalar", kw), R=R, W=W)

    def CP(eng, R, W, **kw):
        P.op(eng, ("tensor_copy", kw), R=R, W=W)

    def MM(out, lhsT, rhs, start, stop, R, W, signal=None):
        P.op("pe", ("matmul", dict(out=out, lhsT=lhsT, rhs=rhs, start=start, stop=stop)), R=R, W=W,
             signal=(stop if signal is None else signal))

    P.dma("sp", vecs.ap, vecs_d[:, :], "ld_vecs", W=[vecs])
    P.dma("sp", cmat.ap, cm_d[:, :], "ld_cm", W=[cmat])
    P.dma("sp", cTf.ap, cT_d.rearrange("(c p) s -> p c s", p=128), "ld_c", W=[cTf])
    for ti, (t0, w, _) in enumerate(tiles):
        P.dma("sp", xb[ti].ap, xT_d[:, t0:t0 + w].rearrange("(c p) t -> p c t", p=128), "ldx%d" % ti, W=[xb[ti]])
    P.dma("sp", flag.ap, flag_d[:, :], "ld_flag", W=[flag])
    P.dma("sp", xh.ap, xh_d.rearrange("(c p) t -> p c t", p=128), "ld_xh", W=[xh])
    P.dma("pool", baw.ap, W["gdn_ba"].rearrange("(kc p) n -> p kc n", p=128), "ld_baw", W=[baw])
    ws.start(specs)
    CP("dve", [cmat], [cmatb], out=cmatb.ap, in_=cmat.ap[:, 0:NCMB * 128])
    P.op("dve", ("memset", dict(ap=onesb.ap, constant=1.0)), W=[onesb])
    P.op("dve", ("memset", dict(ap=onesf.ap, constant=1.0)), W=[onesf])
    ACT([cTf], [cmT], out=cmT.ap, in_=cTf.ap, func=AF.Silu)

    def adaln(li):
        for g in range(6):
            sb, slot = ws.acquire("ada%d_%d" % (li, g))
            pb = ps()
            for n in range(8):
                mm_group(P, pb.ap[:, n * 17:(n + 1) * 17],
                         [(sb.ap[:, k, n * 128:(n + 1) * 128], cmT.ap[:, k, :]) for k in range(8)],
                         R=[sb, cmT], W=[pb])
            ws.release(slot)
            bo, _ = vidx.idx["b_ada%d" % li]
            TT("dve", [pb, vecs], [mods], out=mods.ap[:, g * 8:(g + 1) * 8, :],
               in0=pb.ap[:, 0:136].rearrange("p (n s) -> p n s", n=8),
               in1=vecs.ap[:, bo + g * 8:bo + (g + 1) * 8].unsqueeze(2).to_broadcast([128, 8, 17]), op=ALU.add)
        for g, nm in ((1, "norm_mix%d" % li), (4, "norm_mlp%d" % li)):
            STT("dve", [mods, vecs], [mods], out=mods.ap[:, g * 8:(g + 1) * 8, :], in0=mods.ap[:, g * 8:(g + 1) * 8, :],
                scalar=1.0, in1=V(nm).unsqueeze(2).to_broadcast([128, 8, 17]), op0=ALU.add, op1=ALU.mult)

    def colsum_rstd(srcs, w, eps_name, scratch, scale):
        pb = ps()
        n = len(srcs)
        for c, (sap, sbuf_) in enumerate(srcs):
            sq = scratch["sq"][c % 2]
            ACT([sbuf_], [sq], out=sq.ap[:, 0:w], in_=sap, func=AF.Square)
            MM(pb.ap[:, 0:w], onesb.ap, sq.ap[:, 0:w], c == 0, c == n - 1, R=[sq, onesb], W=[pb], signal=True)
        rs = scratch["rs"]
        ACT([pb, vecs], [rs], out=rs.ap[:, 0:w], in_=pb.ap[:, 0:w], func=AF.Ln, bias=V(eps_name), scale=scale)
        ACT([rs], [rs], out=rs.ap[:, 0:w], in_=rs.ap[:, 0:w], func=AF.Exp, scale=-0.5)
        return rs

    def norm_mod_src(xbuf, xap, w, smp, gA, gB, out_buf, scratch):
        rs = colsum_rstd([(xap[:, c, :], xbuf) for c in range(8)], w, "eps_rms", scratch, 1.0 / D)
        for c in range(8):
            tmp = scratch["tmp"][c % 2]
            TT("dve", [xbuf, rs], [tmp], out=tmp.ap[:, 0:w], in0=xap[:, c, :], in1=rs.ap[:, 0:w], op=ALU.mult)
            if not smp:
                ACT([tmp, mods], [out_buf], out=out_buf.ap[:, c, 0:w], in_=tmp.ap[:, 0:w], func=AF.Identity,
                    bias=mods.ap[:, gB * 8 + c, 0:1], scale=mods.ap[:, gA * 8 + c, 0:1])
            else:
                TT("dve", [tmp, mods], [tmp], out=tmp.ap[:, 0:w], in0=tmp.ap[:, 0:w], in1=mods.ap[:, gA * 8 + c, 1:17],
                   op=ALU.mult)
                TT("dve", [tmp, mods], [out_buf], out=out_buf.ap[:, c, 0:w], in0=tmp.ap[:, 0:w],
                   in1=mods.ap[:, gB * 8 + c, 1:17], op=ALU.add)

    def rstd_tile(ti, scratch):
        t0, w, _ = tiles[ti]
        return colsum_rstd([(xb[ti].ap[:, c, :], xb[ti]) for c in range(8)], w, "eps_rms", scratch, 1.0 / D)

    def norm_mod(ti, gA, gB, out_buf, scratch):
        t0, w, smp = tiles[ti]
        norm_mod_src(xb[ti], xb[ti].ap, w, smp, gA, gB, out_buf, scratch)

    def resid_add(ti, c, pb_ap, pbuf, gG, scratch, bias_ap=None, col0=0, w=None):
        t0, wt, smp = tiles[ti]
        if w is None:
            w = wt
        xs = xb[ti].ap[:, c, col0:col0 + w]
        src = pb_ap
        extraR = [pbuf]
        if bias_ap is not None:
            tb = scratch["tmp"][c % 2]
            ACT([pbuf, vecs], [tb], out=tb.ap[:, 0:w], in_=pb_ap, func=AF.Identity, bias=bias_ap)
            src = tb.ap[:, 0:w]
            extraR = [tb]
        if not smp:
            STT("dve", extraR + [mods, xb[ti]], [xb[ti]], out=xs, in0=src,
                scalar=mods.ap[:, gG * 8 + c, 0:1], in1=xs, op0=ALU.mult, op1=ALU.add)
        else:
            t2 = scratch["tmp2"]
            TT("dve", extraR + [mods], [t2], out=t2.ap[:, 0:w], in0=src, in1=mods.ap[:, gG * 8 + c, 1:17], op=ALU.mult)
            TT("dve", [t2, xb[ti]], [xb[ti]], out=xs, in0=xs, in1=t2.ap[:, 0:w], op=ALU.add)

    def to_token_major(srcs, sbufs, scratch_tok):
        n = len(srcs)
        for b0 in range(0, n, 4):
            pb = ps()
            for j in range(b0, min(n, b0 + 4)):
                MM(pb.ap[0:NS, (j - b0) * 128:(j - b0 + 1) * 128], srcs[j], CMf("ident"), True, True,
                   R=sbufs + [cmat], W=[pb], signal=True)
            nn = min(n, b0 + 4) - b0
            CP("dve", [pb], [scratch_tok], out=scratch_tok.ap[0:NS, b0 * 128:(b0 + nn) * 128], in_=pb.ap[0:NS, 0:nn * 128])

    def conf_mixer():
        nt = len(tiles) - 1
        s0, sl0 = ws.acquire("pw1_0")
        s1, sl1 = ws.acquire("pw1_1")
        s2, sl2 = ws.acquire("pw2")
        pw1 = [s0, s1]

        def glu(hT_b, w, outs_fn, scratch):
            for c in range(8):
                sb = pw1[c // 4]
                j = c % 4
                pa = ps()
                mm_group(P, pa.ap[:, 0:w], [(sb.ap[:, k, j * 128:(j + 1) * 128], hT_b.ap[:, k, 0:w]) for k in range(8)],
                         R=[sb, hT_b], W=[pa])
                pg = ps()
                mm_group(P, pg.ap[:, 0:w], [(sb.ap[:, k, 512 + j * 128:512 + (j + 1) * 128], hT_b.ap[:, k, 0:w]) for k in range(8)],
                         R=[sb, hT_b], W=[pg])
                sg = scratch["tmp"][c % 2]
                ACT([pg, vecs], [sg], out=sg.ap[:, 0:w], in_=pg.ap[:, 0:w], func=AF.Sigmoid, bias=V("conf_b_pw1", 8 + c, 1))
                for (oap, obuf, lo, hi) in outs_fn(c):
                    STT("dve", [pa, sg, vecs], [obuf], out=oap, in0=pa.ap[:, lo:hi], scalar=V("conf_b_pw1", c, 1),
                        in1=sg.ap[:, lo:hi], op0=ALU.add, op1=ALU.mult)

        def ln_silu_pw2(yf, w, ti, col0, zb, st1, st2, scratch):
            pS = ps()
            pQ = ps()
            for c in range(8):
                yb_ = scratch["sq"][c % 2]
                ys_ = scratch["sq2"][c % 2]
                ACT([yf], [yb_], out=yb_.ap[:, 0:w], in_=yf.ap[:, c, 0:w], func=AF.Identity)
                MM(pS.ap[:, 0:w], onesb.ap, yb_.ap[:, 0:w], c == 0, c == 7, R=[yb_, onesb], W=[pS], signal=True)
                ACT([yf], [ys_], out=ys_.ap[:, 0:w], in_=yf.ap[:, c, 0:w], func=AF.Square)
                MM(pQ.ap[:, 0:w], onesb.ap, ys_.ap[:, 0:w], c == 0, c == 7, R=[ys_, onesb], W=[pQ], signal=True)
            ACT([pS], [st1], out=st1.ap[:, 0:w], in_=pS.ap[:, 0:w], func=AF.Identity, scale=1.0 / D)
            msq = scratch["tmp"][0]
            TT("dve", [st1], [msq], out=msq.ap[:, 0:w], in0=st1.ap[:, 0:w], in1=st1.ap[:, 0:w], op=ALU.mult)
            STT("dve", [pQ, msq], [st2], out=st2.ap[:, 0:w], in0=pQ.ap[:, 0:w], scalar=1.0 / D, in1=msq.ap[:, 0:w],
                op0=ALU.mult, op1=ALU.subtract)
            ACT([st2, vecs], [st2], out=st2.ap[:, 0:w], in_=st2.ap[:, 0:w], func=AF.Ln, bias=V("eps_ln"), scale=1.0)
            ACT([st2], [st2], out=st2.ap[:, 0:w], in_=st2.ap[:, 0:w], func=AF.Exp, scale=-0.5)
            for c in range(8):
                t = scratch["tmp"][c % 2]
                TT("dve", [yf, st1], [t], out=t.ap[:, 0:w], in0=yf.ap[:, c, 0:w], in1=st1.ap[:, 0:w], op=ALU.subtract)
                TT("dve", [t, st2], [t], out=t.ap[:, 0:w], in0=t.ap[:, 0:w], in1=st2.ap[:, 0:w], op=ALU.mult)
                ACT([t, vecs], [zb], out=zb.ap[:, c, 0:w], in_=t.ap[:, 0:w], func=AF.Silu, scale=V("conf_ln_g", c, 1),
                    bias=V("conf_ln_b", c, 1))
            for n2 in range(8):
                pb = ps()
                mm_group(P, pb.ap[:, 0:w], [(s2.ap[:, k, n2 * 128:(n2 + 1) * 128], zb.ap[:, k, 0:w]) for k in range(8)],
                         R=[s2, zb], W=[pb])
                resid_add(ti, n2, pb.ap[:, 0:w], pb, 2, scratch, bias_ap=V("conf_b_pw2", n2, 1), col0=col0, w=w)

        arena.reset()
        scratch = mk_scratch()
        scratch["sq2"] = [arena.alloc([TW], BF16) for _ in range(2)]
        SW = 256
        if CFG.get("SKIP") == "prompt":
            nt = 0
        uW = arena.alloc([8, 30 + TW], BF16)
        hT = arena.alloc([8, TW], BF16)
        yf = arena.alloc([8, SW], F32)
        zb = arena.alloc([8, SW], BF16)
        dg = [arena.alloc([31, 128], BF16) for _ in range(2)]
        st1 = arena.alloc([SW], F32)
        st2 = arena.alloc([SW], F32)
        uL = arena.alloc([8, 30], F32)
        uh = arena.alloc([8, 30], F32)
        norm_mod_src(xh, xh.ap[:, :, 0:30], 30, False, 1, 0, hT, scratch)
        glu(hT, 30, lambda c: [(uh.ap[:, c, :], uh, 0, 30)], scratch)
        TS("dve", [uh, flag], [uW], out=uW.ap[:, :, 0:30], in0=uh.ap, scalar1=flag.ap[:, 0:1], scalar2=None, op0=ALU.mult)
        wdw = V("conf_w_dw").rearrange("p (c k) -> p c k", c=8)
        for ti in range(nt):
            t0, w, _ = tiles[ti]
            norm_mod(ti, 1, 0, hT, scratch)
            last = (ti == nt - 1)
            glu(hT, w, (lambda c: [(uW.ap[:, c, 30:30 + w], uW, 0, w)] + ([(uL.ap[:, c, :], uL, w - 30, w)] if last else [])),
                scratch)
            for sub in range(w // SW):
                cs = sub * SW
                for c in range(8):
                    d = dg[c % 2]
                    TT("pool", [cmatb, vecs], [d], out=d.ap, in0=CMb("ident").unsqueeze(1).to_broadcast([128, 31, 128]),
                       in1=wdw[:, c, :].unsqueeze(2).to_broadcast([128, 31, 128]), op=ALU.mult)
                    pb = ps()
                    for k in range(31):
                        MM(pb.ap[:, 0:SW], d.ap[:, k, :], uW.ap[:, c, cs + k:cs + k + SW], k == 0, k == 30, R=[d, uW], W=[pb])
                    ACT([pb, vecs], [yf], out=yf.ap[:, c, :], in_=pb.ap[:, 0:SW], func=AF.Identity, bias=V("conf_b_dw", c, 1))
                ln_silu_pw2(yf, SW, ti, cs, zb, st1, st2, scratch)
            if not last:
                CP("dve", [uW], [uW], out=uW.ap[:, :, 0:30], in_=uW.ap[:, :, w:w + 30])
        if nt > 0:
            P.dma("sp", confp_d.rearrange("(c p) t -> p c t", p=128), uL.ap, "st_confp", R=[uL])
        nt = len(tiles) - 1
        if CFG.get("SKIP") == "sample":
            ws.release(sl0)
            ws.release(sl1)
            ws.release(sl2)
            return

        arena.reset()
        scratch = mk_scratch()
        scratch["sq2"] = [arena.alloc([TW], BF16) for _ in range(2)]
        hT = arena.alloc([8, NS], BF16)
        uS = arena.alloc([8, NS], F32)
        yf = arena.alloc([8, NS], F32)
        y1 = arena.alloc([8, NS], F32)
        zb = arena.alloc([8, NS], BF16)
        st1 = arena.alloc([NS], F32)
        st2 = arena.alloc([NS], F32)
        stt = arena.alloc([8, NS, 30], F32)
        prod = arena.alloc([8, NS, 30], F32)
        tok = arena.alloc([D], F32)
        ti = nt
        P.dma("sp", stt.ap, stconfT_d.rearrange("(c p) s k -> p c s k", p=128), "ld_stconf", W=[stt])
        norm_mod(ti, 1, 0, hT, scratch)
        glu(hT, NS, lambda c: [(uS.ap[:, c, :], uS, 0, NS)], scratch)
        TT("dve", [stt, vecs], [prod], out=prod.ap, in0=stt.ap,
           in1=wdw[:, :, 0:30].unsqueeze(2).to_broadcast([128, 8, NS, 30]), op=ALU.mult)
        P.op("dve", ("tensor_reduce", dict(out=y1.ap, in_=prod.ap, axis=AX.X, op=ALU.add)), R=[prod], W=[y1])
        for c in range(8):
            STT("dve", [uS, y1, vecs], [yf], out=yf.ap[:, c, :], in0=uS.ap[:, c, :], scalar=wdw[:, c, 30:31],
                in1=y1.ap[:, c, :], op0=ALU.mult, op1=ALU.add)
            TS("dve", [yf, vecs], [yf], out=yf.ap[:, c, :], in0=yf.ap[:, c, :], scalar1=V("conf_b_dw", c, 1), scalar2=None,
               op0=ALU.add)
        ln_silu_pw2(yf, NS, ti, 0, zb, st1, st2, scratch)
        P.dma("sp", confs_d[:, 0:29, :], stconf_d[:, 1:30, :], "st_confs")
        to_token_major([uS.ap[:, c, :] for c in range(8)], [uS], tok)
        P.dma("sp", confs_d[:, 29, :], tok.ap[0:NS, :], "st_confs", R=[tok])
        ws.release(sl0)
        ws.release(sl1)
        ws.release(sl2)

    def swa_mixer():
        nt = len(tiles) - 1
        sq_, slq = ws.acquire("qkv_0")
        sk_, slk = ws.acquire("qkv_1")
        so_, slo = ws.acquire("wo")
        SC = 0.125

        def rope(pb, w, rc, rsn, out_ap, out_buf, scratch, f32_out=None):
            qf = scratch["qf"]
            rC = scratch["rC"]
            rS = scratch["rS"]
            ACT([pb], [qf], out=qf.ap[:, 0:w], in_=pb.ap[:, 0:w], func=AF.Identity)
            pr = ps()
            MM(pr.ap[:, 0:w], CMf("ropeRT"), qf.ap[:, 0:w], True, True, R=[cmat, qf], W=[pr])
            t1 = scratch["tmp"][0]
            t2 = scratch["tmp"][1]
            TT("dve", [qf, rC], [t1], out=t1.ap[:, 0:w], in0=qf.ap[:, 0:w], in1=rc, op=ALU.mult)
            TT("dve", [pr, rS], [t2], out=t2.ap[:, 0:w], in0=pr.ap[:, 0:w], in1=rsn, op=ALU.mult)
            TT("dve", [t1, t2], [out_buf], out=out_ap, in0=t1.ap[:, 0:w], in1=t2.ap[:, 0:w], op=ALU.add)
            if f32_out is not None:
                TT("dve", [t1, t2], [f32_out[1]], out=f32_out[0], in0=t1.ap[:, 0:w], in1=t2.ap[:, 0:w], op=ALU.add)

        def proj_k(hT_b, col0, w, rc, rsn, out_fn, scratch):
            for g in range(4):
                pb = ps()
                mm_group(P, pb.ap[:, 0:w], [(sk_.ap[:, k, g * 128:(g + 1) * 128], hT_b.ap[:, k, col0:col0 + w]) for k in range(8)],
                         R=[sk_, hT_b], W=[pb])
                oap, obuf, f32o = out_fn(g)
                rope(pb, w, rc, rsn, oap, obuf, scratch, f32o)

        def proj_v_block(hT_b, col0, out_aps, out_buf, f32_out=None):
            pb = ps()
            mm_group(P, pb.ap[:, 0:256], [(hT_b.ap[:, k, col0:col0 + 128], sk_.ap[:, k, 512:768]) for k in range(8)],
                     R=[sk_, hT_b], W=[pb])
            for oap in out_aps:
                ACT([pb], [out_buf], out=oap, in_=pb.ap[:, 0:256].rearrange("p (g x) -> p g x", g=4), func=AF.Identity)
            if f32_out is not None:
                ACT([pb], [f32_out[1]], out=f32_out[0], in_=pb.ap[:, 0:256], func=AF.Identity)

        arena.reset()
        scratch = mk_scratch()
        scratch["qf"] = arena.alloc([TW], F32)
        scratch["qq"] = arena.alloc([TW], BF16)
        hT = arena.alloc([8, TW], BF16)
        rC = arena.alloc([TW], F32)
        rS = arena.alloc([TW], F32)
        scratch["rC"] = rC
        scratch["rS"] = rS
        qAB = [arena.alloc([TW], BF16) for _ in range(2)]
        kW = arena.alloc([4, 128 + TW], BF16)
        vW = arena.alloc([5, 4 * 2 * 128], BF16)
        onesAB = arena.alloc([2, 128], BF16)
        P.op("dve", ("memset", dict(ap=vW.ap, constant=0.0)), W=[vW])
        P.op("dve", ("memset", dict(ap=onesAB.ap, constant=0.0)), W=[onesAB])
        P.op("dve", ("memset", dict(ap=onesAB.ap[:, 0, 0:64], constant=1.0)), W=[onesAB])
        P.op("dve", ("memset", dict(ap=onesAB.ap[:, 1, 64:128], constant=1.0)), W=[onesAB])

        def vview(blk, var):
            v5 = vW.ap[:, blk, :].rearrange("p (g v x) -> p g v x", g=4, v=2)
            return v5[:, :, var, var * 64:(var + 1) * 64]

        def v_lhsT(blk, g, var):
            o = (g * 2 + var) * 128
            return vW.ap[:, blk, o:o + 128]
        pT = [arena.alloc([512], BF16) for _ in range(2)]
        oT = arena.alloc([8, TW], BF16)
        mask4 = arena.alloc([512], BF16)
        mask4f = arena.alloc([512], BF16)
        rec = arena.alloc([TW], F32)
        esink = arena.alloc([8], F32)
        hst = arena.alloc([768], F32)
        ACT([vecs], [esink], out=esink.ap, in_=V("swa_sink"), func=AF.Exp)
        for q in range(4):
            nm = "triGE" if q % 2 == 0 else "triLE"
            CP("dve", [cmatb], [mask4], out=mask4.ap[:, q * 128:(q + 1) * 128], in_=CMb(nm))
            if q % 2 == 0:
                TS("dve", [cmatb, flag], [mask4f], out=mask4f.ap[:, q * 128:(q + 1) * 128], in0=CMb(nm),
                   scalar1=flag.ap[:, 0:1], scalar2=None, op0=ALU.mult)
            else:
                CP("dve", [cmatb], [mask4f], out=mask4f.ap[:, q * 128:(q + 1) * 128], in_=CMb(nm))
        tl = nt - 1
        t0l, wl, _ = tiles[tl]
        lc = wl - 128
        P.dma("sp", rC.ap[:, 0:128], ropeC_d[:, t0l + lc:t0l + wl], "ld_rC", W=[rC])
        P.dma("sp", rS.ap[:, 0:128], ropeS_d[:, t0l + lc:t0l + wl], "ld_rS", W=[rS])
        norm_mod_src(xb[tl], xb[tl].ap[:, :, lc:wl], 128, False, 1, 0, hT, scratch)
        hk = hst.ap[:, 0:512].rearrange("p (g t) -> p g t", g=4)
        proj_k(hT, 0, 128, rC.ap[:, 0:128], rS.ap[:, 0:128],
               lambda g: (kW.ap[:, g, 0:128], kW, (hk[:, g, :], hst)), scratch)
        proj_v_block(hT, 0, [vview(0, 0), vview(0, 1)], vW, (hst.ap[:, 512:768], hst))
        P.dma("sp", kp_d[:, :, :], hk, "st_kvp", R=[hst])
        P.dma("sp", vp_d[:, :], hst.ap[:, 512:768], "st_kvp", R=[hst])
        if CFG.get("COLL", True):
            hsrc = Buf(swah_src.ap())
            hdst = Buf(swah_dst.ap())
            P.dma("sp", hsrc.ap[:, :], hst.ap, "swah_w", R=[hst], W=[hsrc])
            P.collective(swah_src.ap().opt(), swah_dst.ap().opt(), "cc_swa", R=[hsrc], W=[hdst])
            P.dma("sp", hst.ap, hdst.ap[0:128, :], "swah_r", R=[hdst], W=[hst])
            CP("dve", [hst], [kW], out=kW.ap[:, :, 0:128], in_=hk)
            for var in range(2):
                CP("dve", [hst], [vW], out=vview(0, var), in_=hst.ap[:, 512:768].rearrange("p (g x) -> p g x", g=4))
        for ti in range(0 if CFG.get("SKIP") == "swa_prompt" else nt):
            t0, w, _ = tiles[ti]
            nb = w // 128
            P.dma("sp", rC.ap[:, 0:w], ropeC_d[:, t0:t0 + w], "ld_rC", W=[rC])
            P.dma("sp", rS.ap[:, 0:w], ropeS_d[:, t0:t0 + w], "ld_rS", W=[rS])
            norm_mod(ti, 1, 0, hT, scratch)
            proj_k(hT, 0, w, rC.ap[:, 0:w], rS.ap[:, 0:w], lambda g: (kW.ap[:, g, 128:128 + w], kW, None), scratch)
            for b in range(nb):
                proj_v_block(hT, b * 128, [vview(1 + b, 0), vview(1 + b, 1)], vW)
            for c in range(0 if CFG.get("SKIP") == "swa_noattn" else 8):
                g = c // 2
                pb = ps()
                mm_group(P, pb.ap[:, 0:w], [(sq_.ap[:, k, c * 128:(c + 1) * 128], hT.ap[:, k, 0:w]) for k in range(8)],
                         R=[sq_, hT], W=[pb])
                qq = scratch["qq"]
                rope(pb, w, rC.ap[:, 0:w], rS.ap[:, 0:w], qq.ap[:, 0:w], qq, scratch)
                for hh in range(2):
                    TS("dve", [qq, vecs], [qAB[hh]], out=qAB[hh].ap[:, 0:w], in0=qq.ap[:, 0:w], scalar1=V("hmask", hh, 1), scalar2=None,
                       op0=ALU.mult)
                po = ps()
                pd = ps()
                for qb in range(nb):
                    psc = ps()
                    for hh in range(2):
                        for kb in range(2):
                            MM(psc.ap[:, (hh * 2 + kb) * 128:(hh * 2 + kb + 1) * 128],
                               kW.ap[:, g, (qb + kb) * 128:(qb + kb + 1) * 128],
                               qAB[hh].ap[:, qb * 128:(qb + 1) * 128], True, True,
                               R=[kW, qAB[hh]], W=[psc], signal=(hh == 1 and kb == 1))
                    pt = pT[qb % 2]
                    ACT([psc], [pt], out=pt.ap, in_=psc.ap, func=AF.Exp, scale=SC)
                    mk = mask4f if (ti == 0 and qb == 0) else mask4
                    TT("dve", [pt, mk], [pt], out=pt.ap, in0=pt.ap, in1=mk.ap, op=ALU.mult)
                    i_ = 0
                    for hh in range(2):
                        for kb in range(2):
                            MM(po.ap[:, qb * 128:(qb + 1) * 128], v_lhsT(qb + kb, g, hh),
                               pt.ap[:, (hh * 2 + kb) * 128:(hh * 2 + kb + 1) * 128], i_ == 0, i_ == 3, R=[vW, pt], W=[po], signal=False)
                            i_ += 1
                    i_ = 0
                    for hh in range(2):
                        for kb in range(2):
                            MM(pd.ap[:, qb * 128:(qb + 1) * 128], onesAB.ap[:, hh, :],
                               pt.ap[:, (hh * 2 + kb) * 128:(hh * 2 + kb + 1) * 128], i_ == 0, i_ == 3, R=[onesAB, pt], W=[pd],
                               signal=(i_ == 3))
                            i_ += 1
                TS("dve", [pd, esink], [rec], out=rec.ap[:, 0:w], in0=pd.ap[:, 0:w], scalar1=esink.ap[:, c:c + 1], scalar2=None,
                   op0=ALU.add)
                P.op("dve", ("reciprocal", dict(out=rec.ap[:, 0:w], in_=rec.ap[:, 0:w])), R=[rec], W=[rec])
                TT("dve", [po, rec], [oT], out=oT.ap[:, c, 0:w], in0=po.ap[:, 0:w], in1=rec.ap[:, 0:w], op=ALU.mult)
            for n2 in range(8):
                pb = ps()
                mm_group(P, pb.ap[:, 0:w], [(so_.ap[:, k, n2 * 128:(n2 + 1) * 128], oT.ap[:, k, 0:w]) for k in range(8)],
                         R=[so_, oT], W=[pb])
                resid_add(ti, n2, pb.ap[:, 0:w], pb, 2, scratch)
            if ti != nt - 1:
                CP("dve", [kW], [kW], out=kW.ap[:, :, 0:128], in_=kW.ap[:, :, w:w + 128])
                CP("dve", [vW], [vW], out=vW.ap[:, 0, :], in_=vW.ap[:, nb, :])

        if CFG.get("SKIP") == "swa_sample":
            ws.release(slq)
            ws.release(slk)
            ws.release(slo)
            return
        arena.reset()
        scratch = mk_scratch()
        scratch["qf"] = arena.alloc([TW], F32)
        ti = nt
        t0 = tiles[ti][0]
        hT = arena.alloc([8, NS], BF16)
        rC = arena.alloc([NS], F32)
        rS = arena.alloc([NS], F32)
        scratch["rC"] = rC
        scratch["rS"] = rS
        qS = arena.alloc([8, NS], BF16)
        knT = arena.alloc([4, NS], F32)
        knb = arena.alloc([4, NS], BF16)
        vnT = arena.alloc([4, NS], F32)
        ckT = arena.alloc([NS, 4, 128], BF16)
        cV = arena.alloc([NS, 256], BF16)
        pP = arena.alloc([256], BF16)
        prodn = arena.alloc([8, NS], BF16)
        pnew = arena.alloc([8, NS], F32)
        num = arena.alloc([8, NS], F32)
        den = arena.alloc([8, NS], F32)
        oS = arena.alloc([8, NS], BF16)
        esink = arena.alloc([8], F32)
        tok = arena.alloc([512], F32)
        tok2 = arena.alloc([512], F32)
        ACT([vecs], [esink], out=esink.ap, in_=V("swa_sink"), func=AF.Exp)
        P.dma("pool", ckT.ap, ckT_d[:, :, :, :], "ld_ck", W=[ckT])
        P.dma("pool", cV.ap, cv_d.rearrange("s j c -> j s c"), "ld_cv", W=[cV])
        P.dma("sp", rC.ap, ropeC_d[:, t0:t0 + NS], "ld_rC", W=[rC])
        P.dma("sp", rS.ap, ropeS_d[:, t0:t0 + NS], "ld_rS", W=[rS])
        norm_mod(ti, 1, 0, hT, scratch)
        for c in range(8):
            pb = ps()
            mm_group(P, pb.ap[:, 0:NS], [(sq_.ap[:, k, c * 128:(c + 1) * 128], hT.ap[:, k, :]) for k in range(8)],
                     R=[sq_, hT], W=[pb])
            rope(pb, NS, rC.ap, rS.ap, qS.ap[:, c, :], qS, scratch)
        proj_k(hT, 0, NS, rC.ap, rS.ap, lambda g: (knb.ap[:, g, :], knb, (knT.ap[:, g, :], knT)), scratch)
        pv = ps()
        for g in range(4):
            for hh in range(2):
                mm_group(P, pv.ap[hh * 64:(hh + 1) * 64, g * NS:(g + 1) * NS],
                         [(sk_.ap[:, k, 512 + g * 64:512 + (g + 1) * 64], hT.ap[:, k, :]) for k in range(8)],
                         R=[sk_, hT], W=[pv])
        CP("dve", [pv], [vnT], out=vnT.ap, in_=pv.ap[:, 0:4 * NS].rearrange("p (g s) -> p g s", g=4))
        pS_ = ps()
        for s_ in range(NS):
            for g in range(4):
                for hh in range(2):
                    col = ((s_ * 4 + g) * 2 + hh) * 2
                    MM(pS_.ap[:, col:col + 2], ckT.ap[hh * 64:(hh + 1) * 64, s_, g, :],
                       qS.ap[hh * 64:(hh + 1) * 64, 2 * g:2 * g + 2, s_], True, True, R=[ckT, qS], W=[pS_],
                       signal=(s_ == NS - 1 and g == 3 and hh == 1))
        ACT([pS_], [pP], out=pP.ap, in_=pS_.ap[:, 0:256], func=AF.Exp, scale=SC)
        TT("dve", [qS, knb], [prodn], out=prodn.ap.rearrange("p (g i) s -> p g i s", g=4),
           in0=qS.ap.rearrange("p (g i) s -> p g i s", g=4),
           in1=knb.ap.unsqueeze(2).to_broadcast([128, 4, 2, NS]), op=ALU.mult)
        pn = ps()
        MM(pn.ap[:, 0:128], CMb("half"), prodn.ap.rearrange("p c s -> p (c s)"), True, True, R=[cmatb, prodn], W=[pn])
        ACT([pn], [pnew], out=pnew.ap.rearrange("p c s -> p (c s)"), in_=pn.ap[:, 0:128], func=AF.Exp, scale=SC)
        po2 = ps()
        for s_ in range(NS):
            for g in range(4):
                for hh in range(2):
                    col = ((s_ * 4 + g) * 2 + hh) * 2
                    MM(po2.ap[hh * 64:(hh + 1) * 64, col:col + 2], cV.ap[:, s_, g * 64:(g + 1) * 64], pP.ap[:, col:col + 2],
                       True, True, R=[cV, pP], W=[po2], signal=(s_ == NS - 1 and g == 3 and hh == 1))
        pd2 = ps()
        MM(pd2.ap[:, 0:256], onesb.ap, pP.ap, True, True, R=[onesb, pP], W=[pd2])
        for hh in range(2):
            sl_ = slice(hh * 64, (hh + 1) * 64)
            pov = po2.ap[sl_, 0:256].rearrange("p (s g h i) -> p g i h s", s=NS, g=4, h=2)[:, :, :, hh, :]
            pdv = pd2.ap[sl_, 0:256].rearrange("p (s g h i) -> p g i h s", s=NS, g=4, h=2)[:, :, :, hh, :]
            n4 = num.ap[sl_].rearrange("p (g i) s -> p g i s", g=4)
            d4 = den.ap[sl_].rearrange("p (g i) s -> p g i s", g=4)
            pn4 = pnew.ap[sl_].rearrange("p (g i) s -> p g i s", g=4)
            TT("dve", [pnew, vnT], [num], out=n4, in0=pn4, in1=vnT.ap[sl_].unsqueeze(2).to_broadcast([64, 4, 2, NS]), op=ALU.mult)
            TT("dve", [num, po2], [num], out=n4, in0=n4, in1=pov, op=ALU.add)
            TT("dve", [pnew, pd2], [den], out=d4, in0=pn4, in1=pdv, op=ALU.add)
            TT("dve", [den, esink], [den], out=den.ap[sl_], in0=den.ap[sl_],
               in1=esink.ap[sl_].unsqueeze(2).to_broadcast([64, 8, NS]), op=ALU.add)
            P.op("dve", ("reciprocal", dict(out=den.ap[sl_], in_=den.ap[sl_])), R=[den], W=[den])
            TT("dve", [num, den], [oS], out=oS.ap[sl_], in0=num.ap[sl_], in1=den.ap[sl_], op=ALU.mult)
        for n2 in range(8):
            pb = ps()
            mm_group(P, pb.ap[:, 0:NS], [(so_.ap[:, k, n2 * 128:(n2 + 1) * 128], oS.ap[:, k, :]) for k in range(8)],
                     R=[so_, oS], W=[pb])
            resid_add(ti, n2, pb.ap[:, 0:NS], pb, 2, scratch)
        P.dma("sp", ks_d[:, 0:127, :], ck_d[:, 1:128, :], "st_kvs")
        P.dma("sp", vs_d[:, 0:127, :], cv_d[:, 1:128, :], "st_kvs")
        to_token_major([knT.ap[:, g, :] for g in range(4)], [knT], tok)
        P.dma("sp", ks_d[:, 127, :].rearrange("s (g x) -> s g x", g=4),
              tok.ap[0:NS, :].rearrange("s (g x) -> s g x", g=4)[:, :, 0:64], "st_kvs", R=[tok])
        to_token_major([vnT.ap[:, g, :] for g in range(4)], [vnT], tok2)
        P.dma("sp", vs_d[:, 127, :].rearrange("s (g x) -> s g x", g=4),
              tok2.ap[0:NS, :].rearrange("s (g x) -> s g x", g=4)[:, :, 0:64], "st_kvs", R=[tok2])
        ws.release(slq)
        ws.release(slk)
        ws.release(slo)

    def gdn_mixer():
        nt = len(tiles) - 1
        NB = TP // 128
        arena.reset()
        scratch = mk_scratch()
        hf32 = [arena.alloc([8, TW], F32) for _ in range(1)]
        hb16 = arena.alloc([8, TW], BF16)
        ghs = [Buf(t.ap()) for t in gh_src]
        ghd = [Buf(t.ap()) for t in gh_dst]

        def gather(src_t, dst_t, sb_, db_, key):
            if CFG.get("COLL", True):
                P.collective(src_t.ap().opt(), dst_t.ap().opt(), key, R=[sb_], W=[db_])
            else:
                P.dma("sp", db_.ap[0:D, :], sb_.ap[:, :], key + "_cp", R=[sb_], W=[db_])
                P.dma("sp", db_.ap[D:2 * D, :], sb_.ap[:, :], key + "_cp", R=[sb_], W=[db_])
        for ti, (t0, w, smp) in enumerate(tiles):
            norm_mod(ti, 1, 0, hb16, scratch)
            CP("dve", [hb16], [hf32[0]], out=hf32[0].ap[:, :, 0:w], in_=hb16.ap[:, :, 0:w])
            P.dma("sp", ghs[ti].ap[:, :].rearrange("(c p) t -> p c t", p=128), hf32[0].ap[:, :, 0:w], "gh_w",
                  R=[hf32[0]], W=[ghs[ti]])
            gather(gh_src[ti], gh_dst[ti], ghs[ti], ghd[ti], "cc_gh%d" % ti)

        arena.reset()
        sqk, slqk = ws.acquire("g_qk")
        sv, slv = ws.acquire("g_v")
        sz, slz = ws.acquire("g_z")
        gos = [Buf(t.ap()) for t in go_src]
        god = [Buf(t.ap()) for t in go_dst]
        wcv = V("gdn_wconv").rearrange("p (c k) -> p c k", c=16)
        BW = 128
        hT = arena.alloc([8, BW], BF16)
        pre = arena.alloc([16, 3 + BW], F32)
        cacc = None
        csil = None
        sqb = arena.alloc([8, BW], BF16)
        rn = arena.alloc([8, BW], F32, name="rn")
        qkT = arena.alloc([8, BW], BF16, name="qkT")
        vTb = arena.alloc([8, BW], BF16, name="vTb")
        k_tok = arena.alloc([4, 128], BF16, name="k_tok")
        S = arena.alloc([8, 128], F32, name="S")
        Sb = arena.alloc([8, 128], BF16)
        ogT = arena.alloc([8, BW], F32)
        sm = {nm: arena.alloc([8], F32, name="sm_" + nm) for nm in ["g", "beta", "nbeta", "gc", "gl", "ek", "eg", "gam", "ss", "t8"]}

        def mkset():
            B = {}
            blk = arena.alloc([7 * 256], F32)
            for i_, nm in enumerate(("i1", "i2", "i3", "i4", "i5", "i6", "grhs")):
                B[nm] = Buf(blk.ap[:, i_ * 256:(i_ + 1) * 256].rearrange("p (a b) -> p a b", a=2))
            B["blk"] = blk
            for nm in ("sA", "sB", "sC", "sD", "sE", "sF", "attnT", "qtT", "kgT", "ktil", "v_tok", "z_tok"):
                B[nm] = arena.alloc([2, 128], BF16)
            B["ss"] = arena.alloc([2], F32)
            return B
        gsets = [mkset(), mkset()]
        cacc = Buf(gsets[0]["blk"].ap[:, 0:1024].rearrange("p (a b) -> p a b", a=8))
        csil = Buf(gsets[1]["blk"].ap[:, 0:1024].rearrange("p (a b) -> p a b", a=8))
        alias_tiles = [gsets[q][nm] for q in range(2) for nm in ("i1", "i2", "i3", "i4")]
        fsc = arena.alloc([2], F32)

        def fence(reads, writes):
            P.op("dve", ("memset", dict(ap=fsc.ap, constant=0.0)), R=reads, W=writes + [fsc])
        P.op("dve", ("memset", dict(ap=S.ap, constant=0.0)), W=[S])
        P.op("dve", ("memset", dict(ap=Sb.ap, constant=0.0)), W=[Sb])
        P.op("dve", ("memset", dict(ap=pre.ap, constant=0.0)), W=[pre])
        nalog = arena.alloc([8], F32)
        ACT([vecs], [nalog], out=nalog.ap, in_=V("gdn_alog"), func=AF.Exp)
        TS("dve", [nalog], [nalog], out=nalog.ap, in0=nalog.ap, scalar1=-1.0, scalar2=None, op0=ALU.mult)

        def load_h(col0, w):
            for r in range(1):
                pass

        def inproj_conv(hT_b, w, hist_io):
            for grp in range(4):
                pb = ps()
                for j in range(4):
                    ch = grp * 4 + j
                    sb_, col = (sqk, ch * 128) if ch < 8 else (sv, (ch - 8) * 128)
                    mm_group(P, pb.ap[:, j * w:(j + 1) * w], [(sb_.ap[:, k, col:col + 128], hT_b.ap[:, k, 0:w]) for k in range(8)],
                             R=[sb_, hT_b], W=[pb])
                ACT([pb], [pre], out=pre.ap[:, grp * 4:(grp + 1) * 4, 3:3 + w],
                    in_=pb.ap[:, 0:4 * w].rearrange("p (c t) -> p c t", c=4), func=AF.Identity)
            for part in range(2):
                cs8 = slice(part * 8, (part + 1) * 8)
                for k in range(4):
                    wk = wcv[:, cs8, k:k + 1].to_broadcast([128, 8, w])
                    if k == 0:
                        TT("dve", [pre, vecs], [cacc], out=cacc.ap[:, :, 0:w], in0=pre.ap[:, cs8, k:k + w], in1=wk, op=ALU.mult)
                    else:
                        TT("dve", [pre, vecs], [csil], out=csil.ap[:, :, 0:w], in0=pre.ap[:, cs8, k:k + w], in1=wk, op=ALU.mult)
                        TT("dve", [csil, cacc], [cacc], out=cacc.ap[:, :, 0:w], in0=cacc.ap[:, :, 0:w], in1=csil.ap[:, :, 0:w], op=ALU.add)
                ACT([cacc], [csil], out=csil.ap[:, :, 0:w], in_=cacc.ap[:, :, 0:w], func=AF.Silu)
                if part == 1:
                    CP("dve", [csil], [vTb], out=vTb.ap[:, :, 0:w], in_=csil.ap[:, :, 0:w])
                    continue
                ACT([csil], [sqb], out=sqb.ap[:, :, 0:w], in_=csil.ap[:, :, 0:w], func=AF.Square)
                for hh in range(2):
                    pb = ps()
                    for j in range(4):
                        MM(pb.ap[:, j * w:(j + 1) * w], onesb.ap, sqb.ap[:, hh * 4 + j, 0:w], True, True, R=[onesb, sqb], W=[pb],
                           signal=(j == 3))
                    ACT([pb, vecs], [rn], out=rn.ap[:, hh * 4:(hh + 1) * 4, 0:w], in_=pb.ap[:, 0:4 * w].rearrange("p (c t) -> p c t", c=4),
                        func=AF.Ln, bias=V("eps_rms"), scale=1.0)
                ACT([rn], [rn], out=rn.ap[:, :, 0:w], in_=rn.ap[:, :, 0:w], func=AF.Exp, scale=-0.5)
                STT("dve", [csil, rn], [qkT], out=qkT.ap[:, 0:4, 0:w], in0=csil.ap[:, 0:4, 0:w], scalar=float(128 ** -0.5),
                    in1=rn.ap[:, 0:4, 0:w], op0=ALU.mult, op1=ALU.mult)
                TT("dve", [csil, rn], [qkT], out=qkT.ap[:, 4:8, 0:w], in0=csil.ap[:, 4:8, 0:w], in1=rn.ap[:, 4:8, 0:w], op=ALU.mult)

        def softplus_gate(src_ps, np_, gout, bout):
            ACT([src_ps], [bout], out=bout.ap[0:np_, :], in_=src_ps.ap[0:np_, 0:8], func=AF.Sigmoid)
            t8 = sm["t8"]
            TT("dve", [src_ps, vecs], [t8], out=t8.ap[0:np_, :], in0=src_ps.ap[0:np_, 8:16], in1=V("gdn_dtb")[0:np_, :], op=ALU.add)
            ACT([t8], [t8], out=t8.ap[0:np_, :], in_=t8.ap[0:np_, :], func=AF.Exp)
            ACT([t8], [t8], out=t8.ap[0:np_, :], in_=t8.ap[0:np_, :], func=AF.Ln, bias=V("one")[0:np_, :], scale=1.0)
            TT("dve", [t8, nalog], [gout], out=gout.ap[0:np_, :], in0=t8.ap[0:np_, :], in1=nalog.ap[0:np_, :], op=ALU.mult)

        ntile_g = 2 * NB
        for gb in range(ntile_g):
            r, lb = gb // NB, gb % NB
            col0 = lb * 128
            tsrc = ghd[lb // 4]
            cc0 = (lb % 4) * 128
            P.dma("pool", hT.ap, tsrc.ap[r * D:(r + 1) * D, cc0:cc0 + 128].rearrange("(c p) t -> p c t", p=128), "ld_gh",
                  R=[tsrc], W=[hT])
            fence(alias_tiles, [cacc, csil])
            inproj_conv(hT, BW, None)
            fence([cacc, csil], alias_tiles)
            if gb == ntile_g - 1:
                P.dma("sp", gcvp_d.rearrange("(c p) k -> p c k", p=128), pre.ap[:, :, BW:BW + 3], "st_gcvp", R=[pre])
            CP("dve", [pre], [pre], out=pre.ap[:, :, 0:3], in_=pre.ap[:, :, BW:BW + 3])
            pb = ps()
            for j in range(4):
                MM(pb.ap[:, j * 128:(j + 1) * 128], qkT.ap[:, 4 + j, :], CMb("ident"), True, True, R=[qkT, cmatb], W=[pb], signal=(j == 3))
            CP("dve", [pb], [k_tok], out=k_tok.ap, in_=pb.ap.rearrange("p (c t) -> p c t", c=4))
            pba = ps()
            mm_group(P, pba.ap[:, 0:16], [(hT.ap[:, k, :], baw.ap[:, k, :]) for k in range(8)], R=[baw, hT], W=[pba])
            softplus_gate(pba, 128, sm["g"], sm["beta"])
            TS("dve", [sm["beta"]], [sm["nbeta"]], out=sm["nbeta"].ap, in0=sm["beta"].ap, scalar1=-1.0, scalar2=None, op0=ALU.mult)
            pg = ps()
            MM(pg.ap[:, 0:8], CMf("triLE"), sm["g"].ap, True, True, R=[cmat, sm["g"]], W=[pg])
            MM(pg.ap[:, 8:16], onesf.ap, sm["g"].ap, True, True, R=[onesf, sm["g"]], W=[pg])
            CP("dve", [pg], [sm["gc"]], out=sm["gc"].ap, in_=pg.ap[:, 0:8])
            ACT([pg], [sm["eg"]], out=sm["eg"].ap, in_=pg.ap[:, 0:8], func=AF.Exp)
            ACT([pg], [sm["gam"]], out=sm["gam"].ap, in_=pg.ap[:, 8:16], func=AF.Exp)
            TT("dve", [pg, sm["gc"]], [sm["ek"]], out=sm["ek"].ap, in0=pg.ap[:, 8:16], in1=sm["gc"].ap, op=ALU.subtract)
            ACT([sm["ek"]], [sm["ek"]], out=sm["ek"].ap, in_=sm["ek"].ap, func=AF.Exp)
            def chain(hq, B):
                hs = slice(hq * 2, hq * 2 + 2)
                i1, i2, i3, i4, i5, i6, grhs = [B[n_] for n_ in ("i1", "i2", "i3", "i4", "i5", "i6", "grhs")]
                decI, decS, osq, dgE, og, rb, vnew, TTb = B["sA"], B["sB"], B["sB"], B["sC"], B["sC"], B["sD"], B["sE"], B["sF"]
                attnT, qtT, kgT, ktil, v_tok, z_tok = B["attnT"], B["qtT"], B["kgT"], B["ktil"], B["v_tok"], B["z_tok"]

                def p2(pq_):
                    return pq_.ap[:, 0:256].rearrange("p (c t) -> p c t", c=2)

                def bcf(nm):
                    return CMf(nm).unsqueeze(1).to_broadcast([128, 2, 128])

                def bcb(nm):
                    return CMb(nm).unsqueeze(1).to_broadcast([128, 2, 128])

                def mm2(lhs, rhs, Rb):
                    pq_ = ps()
                    for j in range(2):
                        MM(pq_.ap[:, j * 128:(j + 1) * 128], lhs.ap[:, j, :], rhs.ap[:, j, :] if rhs is not None else CMf("ident"),
                           True, True, R=Rb, W=[pq_], signal=(j == 1))
                    return pq_
                pb = ps()
                for j in range(2):
                    MM(pb.ap[:, j * 128:(j + 1) * 128], vTb.ap[:, hq * 2 + j, :], CMb("ident"), True, True, R=[vTb, cmatb], W=[pb], signal=(j == 1))
                CP("dve", [pb], [v_tok], out=v_tok.ap, in_=p2(pb))
                pb = ps()
                mm_group(P, pb.ap[:, 0:256], [(hT.ap[:, k, :], sz.ap[:, k, hq * 256:(hq + 1) * 256]) for k in range(8)], R=[sz, hT], W=[pb])
                ACT([pb], [z_tok], out=z_tok.ap, in_=p2(pb), func=AF.Silu)
                TT("dve", [cmat, sm["g"]], [grhs], out=grhs.ap, in0=bcf("triLE"),
                   in1=sm["g"].ap[:, hs].unsqueeze(2).to_broadcast([128, 2, 128]), op=ALU.mult)
                pdec = ps()
                MM(pdec.ap[:, 0:256], CMf("triGT"), grhs.ap.rearrange("p c t -> p (c t)"), True, True, R=[cmat, grhs], W=[pdec])
                ACT([pdec], [decI], out=decI.ap, in_=p2(pdec), func=AF.Exp)
                yield
                TT("dve", [decI, cmatb], [decS], out=decS.ap, in0=decI.ap, in1=bcb("triLT"), op=ALU.mult)
                TT("dve", [decI, cmatb], [decI], out=decI.ap, in0=decI.ap, in1=bcb("triLE"), op=ALU.mult)
                TT("dve", [cmatb, sm["eg"]], [dgE], out=dgE.ap, in0=bcb("ident"),
                   in1=sm["eg"].ap[:, hs].unsqueeze(2).to_broadcast([128, 2, 128]), op=ALU.mult)
                pE = ps()
                MM(pE.ap[:, 0:256], onesb.ap, dgE.ap.rearrange("p c t -> p (c t)"), True, True, R=[onesb, dgE], W=[pE])
                TT("dve", [qkT, pE], [qtT], out=qtT.ap, in0=qkT.ap[:, hq:hq + 1, :].to_broadcast([128, 2, 128]), in1=p2(pE), op=ALU.mult)
                TT("dve", [qkT, pE], [kgT], out=kgT.ap, in0=qkT.ap[:, 4 + hq:5 + hq, :].to_broadcast([128, 2, 128]), in1=p2(pE), op=ALU.mult)
                TT("dve", [k_tok, sm["ek"]], [ktil], out=ktil.ap, in0=k_tok.ap[:, hq:hq + 1, :].to_broadcast([128, 2, 128]),
                   in1=sm["ek"].ap[:, hs].unsqueeze(2).to_broadcast([128, 2, 128]), op=ALU.mult)
                pkk = ps()
                MM(pkk.ap[:, 0:128], qkT.ap[:, 4 + hq, :], qkT.ap[:, 4 + hq, :], True, True, R=[qkT], W=[pkk], signal=False)
                MM(pkk.ap[:, 128:256], qkT.ap[:, 4 + hq, :], qkT.ap[:, hq, :], True, True, R=[qkT], W=[pkk], signal=True)
                X0 = i1
                TT("dve", [pkk, decS], [X0], out=X0.ap, in0=decS.ap, in1=pkk.ap[:, 0:128].unsqueeze(1).to_broadcast([128, 2, 128]), op=ALU.mult)
                TT("dve", [X0, sm["nbeta"]], [X0], out=X0.ap, in0=X0.ap, in1=sm["nbeta"].ap[:, hs].unsqueeze(2).to_broadcast([128, 2, 128]),
                   op=ALU.mult)
                TT("dve", [pkk, decI], [attnT], out=attnT.ap, in0=decI.ap, in1=pkk.ap[:, 128:256].unsqueeze(1).to_broadcast([128, 2, 128]),
                   op=ALU.mult)
                yield
                Y0 = i2
                py = mm2(X0, None, [X0, cmat])
                ACT([py], [Y0], out=Y0.ap, in_=p2(py), func=AF.Identity)
                Xbd, Ybd, Rm = i3, i4, i5
                TT("dve", [X0, cmat], [Xbd], out=Xbd.ap, in0=X0.ap, in1=bcf("bd8"), op=ALU.mult)
                TT("dve", [Xbd, cmat], [Rm], out=Rm.ap, in0=Xbd.ap, in1=bcf("ident"), op=ALU.add)
                yield
                TT("dve", [Y0, cmat], [Ybd], out=Ybd.ap, in0=Y0.ap, in1=bcf("bd8"), op=ALU.mult)
                X1, Y1 = i1, i6
                px = mm2(Ybd, Xbd, [Xbd, Ybd])
                py = mm2(Xbd, Ybd, [Xbd, Ybd])
                CP("dve", [px], [X1], out=X1.ap, in_=p2(px))
                ACT([py], [Y1], out=Y1.ap, in_=p2(py), func=AF.Identity)
                yield
                pr_ = mm2(Y1, Rm, [Y1, Rm])
                Y2 = i3
                py = mm2(X1, Y1, [X1, Y1])
                TT("dve", [pr_, Rm], [Rm], out=Rm.ap, in0=Rm.ap, in1=p2(pr_), op=ALU.add)
                ACT([py], [Y2], out=Y2.ap, in_=p2(py), func=AF.Identity)
                yield
                pr_ = mm2(Y2, Rm, [Y2, Rm])
                TT("dve", [pr_, Rm], [Rm], out=Rm.ap, in0=Rm.ap, in1=p2(pr_), op=ALU.add)
                yield
                RT_ = i6
                pt2 = mm2(Rm, None, [Rm, cmat])
                ACT([pt2], [RT_], out=RT_.ap, in_=p2(pt2), func=AF.Identity)
                Ct, Z1s = i4, i1
                for bsz in (8, 16, 32, 64):
                    TT("dve", [Y0, cmat], [Ct], out=Ct.ap, in0=Y0.ap, in1=bcf("cm%d" % bsz), op=ALU.mult)
                    yield
                    pz1 = mm2(Ct, Rm, [Ct, Rm])
                    ACT([pz1], [Z1s], out=Z1s.ap, in_=p2(pz1), func=AF.Identity)
                    yield
                    pz2 = mm2(RT_, Z1s, [RT_, Z1s])
                    if bsz != 64:
                        TT("dve", [pz2, Rm], [Rm], out=Rm.ap, in0=Rm.ap, in1=p2(pz2), op=ALU.add)
                        yield
                        pt2 = mm2(Rm, None, [Rm, cmat])
                        ACT([pt2], [RT_], out=RT_.ap, in_=p2(pt2), func=AF.Identity)
                    else:
                        TT("dve", [pz2, Rm], [TTb], out=TTb.ap, in0=Rm.ap, in1=p2(pz2), op=ALU.add)
                yield
                TTt = TTb
                pks = ps()
                for j in range(2):
                    MM(pks.ap[:, j * 128:(j + 1) * 128], kgT.ap[:, j, :], Sb.ap[:, hq * 2 + j, :], True, True, R=[kgT, Sb], W=[pks], signal=(j == 1))
                TT("dve", [v_tok, pks], [rb], out=rb.ap, in0=v_tok.ap, in1=p2(pks), op=ALU.subtract)
                yield
                pvn = ps()
                for j in range(2):
                    MM(pvn.ap[:, j * 128:(j + 1) * 128], TTt.ap[:, j, :], rb.ap[:, j, :], True, True, R=[TTt, rb], W=[pvn], signal=(j == 1))
                TT("dve", [pvn, sm["beta"]], [vnew], out=vnew.ap, in0=p2(pvn),
                   in1=sm["beta"].ap[:, hs].unsqueeze(2).to_broadcast([128, 2, 128]), op=ALU.mult)
                yield
                po_ = ps()
                for j in range(2):
                    MM(po_.ap[:, j * 128:(j + 1) * 128], qtT.ap[:, j, :], Sb.ap[:, hq * 2 + j, :], True, False, R=[qtT, Sb], W=[po_], signal=False)
                    MM(po_.ap[:, j * 128:(j + 1) * 128], attnT.ap[:, j, :], vnew.ap[:, j, :], False, True, R=[attnT, vnew], W=[po_], signal=(j == 1))
                pss = ps()
                for j in range(2):
                    MM(pss.ap[:, j * 128:(j + 1) * 128], ktil.ap[:, j, :], vnew.ap[:, j, :], True, True, R=[ktil, vnew], W=[pss], signal=(j == 1))
                TT("dve", [S, sm["gam"]], [S], out=S.ap[:, hs, :], in0=S.ap[:, hs, :],
                   in1=sm["gam"].ap[:, hs].unsqueeze(2).to_broadcast([128, 2, 128]), op=ALU.mult)
                TT("dve", [S, pss], [S], out=S.ap[:, hs, :], in0=S.ap[:, hs, :], in1=p2(pss), op=ALU.add)
                ACT([S], [Sb], out=Sb.ap[:, hs, :], in_=S.ap[:, hs, :], func=AF.Identity)
                po4 = p2(po_)
                ACT([po_], [osq], out=osq.ap, in_=po4, func=AF.Square)
                ssb = B["ss"]
                P.op("dve", ("tensor_reduce", dict(out=ssb.ap, in_=osq.ap, axis=AX.X, op=ALU.add)), R=[osq], W=[ssb])
                yield
                ACT([ssb, vecs], [ssb], out=ssb.ap, in_=ssb.ap, func=AF.Ln, bias=V("eps_rms"), scale=1.0 / 128)
                ACT([ssb], [ssb], out=ssb.ap, in_=ssb.ap, func=AF.Exp, scale=-0.5)
                TT("dve", [po_, ssb], [og], out=og.ap, in0=po4, in1=ssb.ap.unsqueeze(2).to_broadcast([128, 2, 128]), op=ALU.mult)
                TT("dve", [og, vecs], [og], out=og.ap, in0=og.ap, in1=V("gdn_norm").unsqueeze(1).to_broadcast([128, 2, 128]), op=ALU.mult)
                TT("dve", [og, z_tok], [og], out=og.ap, in0=og.ap, in1=z_tok.ap, op=ALU.mult)
                yield
                pt_ = ps()
                for j in range(2):
                    MM(pt_.ap[:, j * 128:(j + 1) * 128], og.ap[:, j, :], CMb("ident"), True, True, R=[og, cmatb], W=[pt_], signal=(j == 1))
                ACT([pt_], [ogT], out=ogT.ap[:, hs, :], in_=p2(pt_), func=AF.Identity)

            for pair in ((0, 1), (2, 3)):
                gens = [chain(pair[0], gsets[0]), chain(pair[1], gsets[1])]
                alive = [True, True]
                while any(alive):
                    for gi_ in range(2):
                        if alive[gi_]:
                            try:
                                next(gens[gi_])
                            except StopIteration:
                                alive[gi_] = False
            gi = r * NTI + lb // 4
            P.dma("sp", gos[gi].ap[:, cc0:cc0 + 128].rearrange("(c p) t -> p c t", p=128), ogT.ap, "go_w", R=[ogT], W=[gos[gi]])
            if lb % 4 == 3:
                gather(go_src[gi], go_dst[gi], gos[gi], god[gi], "cc_go%d" % gi)
        P.dma("sp", ssmp_d.rearrange("h k v -> k h v"), S.ap, "st_ssmp", R=[S])

        NS2 = 2 * NS
        arena.reset()
        nalog = arena.alloc([8], F32)
        ACT([vecs], [nalog], out=nalog.ap, in_=V("gdn_alog"), func=AF.Exp)
        TS("dve", [nalog], [nalog], out=nalog.ap, in0=nalog.ap, scalar1=-1.0, scalar2=None, op0=ALU.mult)
        sm = {"t8": arena.alloc([8], F32)}
        hS = arena.alloc([8, NS2], BF16)
        preS = arena.alloc([16, NS2, 4], F32)
        prodS = arena.alloc([16, NS2, 4], F32)
        cs_ = arena.alloc([16, NS2], F32)
        sqS = arena.alloc([8, NS2], BF16)
        rnS = arena.alloc([8, NS2], F32)
        qkS = arena.alloc([8, NS2], F32)
        vS = arena.alloc([8, NS2], F32)
        i32 = CMf("ident")[0:NS2, 0:NS2]
        for r in range(2):
            P.dma("pool", hS.ap[:, :, r * NS:(r + 1) * NS], ghd[nt].ap[r * D:(r + 1) * D, 0:NS].rearrange("(c p) t -> p c t", p=128),
                  "ld_ghs%d" % r, R=[ghd[nt]], W=[hS])
        P.dma("sp", preS.ap, gcvT_d.rearrange("(c p) s k -> p c s k", p=128), "ld_gcv", W=[preS])
        for grp in range(4):
            pb = ps()
            for j in range(4):
                ch = grp * 4 + j
                sb_, col = (sqk, ch * 128) if ch < 8 else (sv, (ch - 8) * 128)
                mm_group(P, pb.ap[:, j * NS2:(j + 1) * NS2], [(sb_.ap[:, k, col:col + 128], hS.ap[:, k, :]) for k in range(8)],
                         R=[sb_, hS], W=[pb])
            ACT([pb], [preS], out=preS.ap[:, grp * 4:(grp + 1) * 4, :, 3],
                in_=pb.ap[:, 0:4 * NS2].rearrange("p (c t) -> p c t", c=4), func=AF.Identity)
        TT("dve", [preS, vecs], [prodS], out=prodS.ap, in0=preS.ap, in1=wcv.unsqueeze(2).to_broadcast([128, 16, NS2, 4]), op=ALU.mult)
        P.op("dve", ("tensor_reduce", dict(out=cs_.ap, in_=prodS.ap, axis=AX.X, op=ALU.add)), R=[prodS], W=[cs_])
        ACT([cs_], [cs_], out=cs_.ap, in_=cs_.ap, func=AF.Silu)
        ACT([cs_], [sqS], out=sqS.ap, in_=cs_.ap[:, 0:8, :], func=AF.Square)
        pb = ps()
        for j in range(8):
            MM(pb.ap[:, j * NS2:(j + 1) * NS2], onesb.ap, sqS.ap[:, j, :], True, True, R=[onesb, sqS], W=[pb], signal=(j == 7))
        ACT([pb, vecs], [rnS], out=rnS.ap, in_=pb.ap[:, 0:8 * NS2].rearrange("p (c t) -> p c t", c=8), func=AF.Ln, bias=V("eps_rms"), scale=1.0)
        ACT([rnS], [rnS], out=rnS.ap, in_=rnS.ap, func=AF.Exp, scale=-0.5)
        STT("dve", [cs_, rnS], [qkS], out=qkS.ap[:, 0:4, :], in0=cs_.ap[:, 0:4, :], scalar=float(128 ** -0.5), in1=rnS.ap[:, 0:4, :],
            op0=ALU.mult, op1=ALU.mult)
        TT("dve", [cs_, rnS], [qkS], out=qkS.ap[:, 4:8, :], in0=cs_.ap[:, 4:8, :], in1=rnS.ap[:, 4:8, :], op=ALU.mult)
        CP("dve", [cs_], [vS], out=vS.ap, in_=cs_.ap[:, 8:16, :])
        P.dma("sp", gcvs_d[:, 0:2, :], gcv_d[:, 1:3, :], "st_gcvs")
        tokc = Buf(prodS.ap.rearrange("p c s k -> p (c s k)"), share=prodS)
        for b0 in range(0, 16, 4):
            pb = ps()
            for j in range(4):
                MM(pb.ap[0:NS2, j * 128:(j + 1) * 128], preS.ap[:, b0 + j, :, 3], CMf("ident"), True, True, R=[preS, cmat], W=[pb], signal=(j == 3))
            CP("dve", [pb], [tokc], out=tokc.ap[0:NS2, b0 * 128:(b0 + 4) * 128], in_=pb.ap[0:NS2, :])
        P.dma("sp", gcvs_d[:, 2, :], tokc.ap[0:NS2, :], "st_gcvs", R=[tokc])
        ktS = arena.alloc([512], F32)
        vtS = arena.alloc([1024], F32)
        ztS = arena.alloc([1024], F32)
        gS = arena.alloc([8], F32)
        bS = arena.alloc([8], F32)
        pb = ps()
        for j in range(4):
            MM(pb.ap[0:NS2, j * 128:(j + 1) * 128], qkS.ap[:, 4 + j, :], CMf("ident"), True, True, R=[qkS, cmat], W=[pb], signal=(j == 3))
        CP("dve", [pb], [ktS], out=ktS.ap[0:NS2, :], in_=pb.ap[0:NS2, :])
        for hh in range(2):
            pb = ps()
            for j in range(4):
                MM(pb.ap[0:NS2, j * 128:(j + 1) * 128], vS.ap[:, hh * 4 + j, :], CMf("ident"), True, True, R=[vS, cmat], W=[pb], signal=(j == 3))
            CP("dve", [pb], [vtS], out=vtS.ap[0:NS2, hh * 512:(hh + 1) * 512], in_=pb.ap[0:NS2, :])
            pb = ps()
            mm_group(P, pb.ap[0:NS2, :], [(hS.ap[:, k, :], sz.ap[:, k, hh * 512:(hh + 1) * 512]) for k in range(8)], R=[sz, hS], W=[pb])
            ACT([pb], [ztS], out=ztS.ap[0:NS2, hh * 512:(hh + 1) * 512], in_=pb.ap[0:NS2, :], func=AF.Silu)
        pba = ps()
        mm_group(P, pba.ap[0:NS2, 0:16], [(hS.ap[:, k, :], baw.ap[:, k, :]) for k in range(8)], R=[baw, hS], W=[pba])
        softplus_gate(pba, NS2, gS, bS)
        ACT([gS], [gS], out=gS.ap[0:NS2, :], in_=gS.ap[0:NS2, :], func=AF.Exp)
        ws.release(slqk)
        ws.release(slv)
        ws.release(slz)
        St = [[arena.alloc([4, 128], F32) for _ in range(2)] for _ in range(2)]
        rowk = arena.alloc([512], F32)
        rowdh = [arena.alloc([512], F32) for _ in range(2)]
        rowoh = [arena.alloc([512], F32) for _ in range(2)]
        rowzh = [arena.alloc([512], F32) for _ in range(2)]
        ss1h = [arena.alloc([4], F32) for _ in range(2)]
        rows = arena.alloc([16], F32)
        ogS = arena.alloc([8, NS2], F32)
        egb = arena.alloc([8], F32)
        pogS = psb[7]
        for s_ in range(NS2):
            Sxh = St[s_ % 2]
            for hh in range(2):
                P.dma("sp", Sxh[hh].ap, ssm_d[s_, hh * 4:(hh + 1) * 4].rearrange("h k v -> k h v"), "ld_ssm%d_%d" % (s_ % 2, hh), W=[Sxh[hh]])
            sel = i32[:, s_:s_ + 1]
            prow = ps()
            MM(prow.ap[0:1, 0:512], sel, ktS.ap[0:NS2, :], True, True, R=[cmat, ktS], W=[prow])
            CP("dve", [prow], [rowk], out=rowk.ap[0:1, :], in_=prow.ap[0:1, 0:512])
            pge = ps()
            MM(pge.ap[0:1, 0:8], sel, gS.ap[0:NS2, :], True, True, R=[cmat, gS], W=[pge], signal=False)
            MM(pge.ap[0:1, 8:16], sel, bS.ap[0:NS2, :], True, True, R=[cmat, bS], W=[pge])
            CP("dve", [pge], [rows], out=rows.ap[0:1, :], in_=pge.ap[0:1, 0:16])
            pbe = ps()
            MM(pbe.ap[:, 0:8], onesf.ap[0:1, :], rows.ap[0:1, 0:8], True, True, R=[onesf, rows], W=[pbe])
            CP("dve", [pbe], [egb], out=egb.ap, in_=pbe.ap[:, 0:8])

            def hchain(hh):
                Sx = Sxh[hh]
                rowd, rowo, rowz, ss1 = rowdh[hh], rowoh[hh], rowzh[hh], ss1h[hh]
                TT("dve", [Sx, egb], [Sx], out=Sx.ap, in0=Sx.ap, in1=egb.ap[:, hh * 4:(hh + 1) * 4].unsqueeze(2).to_broadcast([128, 4, 128]),
                   op=ALU.mult)
                pv_ = ps()
                MM(pv_.ap[0:1, 0:512], sel, vtS.ap[0:NS2, hh * 512:(hh + 1) * 512], True, True, R=[cmat, vtS], W=[pv_])
                dl = rowd.ap[0:1, :]
                CP("dve", [pv_], [rowd], out=dl, in_=pv_.ap[0:1, 0:512])
                pz_ = ps()
                MM(pz_.ap[0:1, 0:512], sel, ztS.ap[0:NS2, hh * 512:(hh + 1) * 512], True, True, R=[cmat, ztS], W=[pz_])
                ACT([pz_], [rowz], out=rowz.ap[0:1, :], in_=pz_.ap[0:1, 0:512], func=AF.Identity)
                yield
                pkv = ps()
                for j in range(4):
                    h_ = hh * 4 + j
                    MM(pkv.ap[0:1, j * 128:(j + 1) * 128], qkS.ap[:, 4 + h_ // 2, s_:s_ + 1], Sx.ap[:, j, :], True, True, R=[qkS, Sx], W=[pkv],
                       signal=(j == 3))
                TT("dve", [rowd, pkv], [rowd], out=dl, in0=dl, in1=pkv.ap[0:1, 0:512], op=ALU.subtract)
                TT("dve", [rowd, rows], [rowd], out=dl.rearrange("p (c t) -> p c t", c=4), in0=dl.rearrange("p (c t) -> p c t", c=4),
                   in1=rows.ap[0:1, 8 + hh * 4:8 + hh * 4 + 4].unsqueeze(2).to_broadcast([1, 4, 128]), op=ALU.mult)
                yield
                pou = ps()
                for j in range(4):
                    h_ = hh * 4 + j
                    MM(pou.ap[:, j * 128:(j + 1) * 128], rowk.ap[0:1, (h_ // 2) * 128:(h_ // 2 + 1) * 128], rowd.ap[0:1, j * 128:(j + 1) * 128],
                       True, True, R=[rowk, rowd], W=[pou], signal=(j == 3))
                TT("dve", [Sx, pou], [Sx], out=Sx.ap, in0=Sx.ap, in1=pou.ap.rearrange("p (c t) -> p c t", c=4), op=ALU.add)
                yield
                P.dma("sp", ssms_d[s_, hh * 4:(hh + 1) * 4].rearrange("h k v -> k h v"), Sx.ap, "st_ssms%d_%d" % (s_ % 2, hh), R=[Sx])
                pq = ps()
                for j in range(4):
                    h_ = hh * 4 + j
                    MM(pq.ap[0:1, j * 128:(j + 1) * 128], qkS.ap[:, h_ // 2, s_:s_ + 1], Sx.ap[:, j, :], True, True, R=[qkS, Sx], W=[pq],
                       signal=(j == 3))
                ol = rowo.ap[0:1, :]
                o4 = ol.rearrange("p (c t) -> p c t", c=4)
                ACT([pq], [rowo], out=ol, in_=pq.ap[0:1, 0:512], func=AF.Square)
                P.op("dve", ("tensor_reduce", dict(out=ss1.ap[0:1, :], in_=o4, axis=AX.X, op=ALU.add)), R=[rowo], W=[ss1])
                ACT([ss1, vecs], [ss1], out=ss1.ap[0:1, :], in_=ss1.ap[0:1, :], func=AF.Ln, bias=V("eps_rms")[0:1, :], scale=1.0 / 128)
                ACT([ss1], [ss1], out=ss1.ap[0:1, :], in_=ss1.ap[0:1, :], func=AF.Exp, scale=-0.5)
                TT("dve", [pq, ss1], [rowo], out=o4, in0=pq.ap[0:1, 0:512].rearrange("p (c t) -> p c t", c=4),
                   in1=ss1.ap[0:1, :].unsqueeze(2).to_broadcast([1, 4, 128]), op=ALU.mult)
                yield
                TT("dve", [rowo, vecs], [rowo], out=o4, in0=o4, in1=V("gdn_norm")[0:1, :].unsqueeze(1).to_broadcast([1, 4, 128]), op=ALU.mult)
                TT("dve", [rowo, rowz], [rowo], out=ol, in0=ol, in1=rowz.ap[0:1, :], op=ALU.mult)
                for j in range(4):
                    h_ = hh * 4 + j
                    MM(pogS.ap[:, h_ * NS2 + s_:h_ * NS2 + s_ + 1], rowo.ap[0:1, j * 128:(j + 1) * 128], onesf.ap[0:1, 0:1], True, True,
                       R=[rowo, onesf], W=[pogS], signal=(j == 3))
            gens = [hchain(0), hchain(1)]
            alive = [True, True]
            while any(alive):
                for gi_ in range(2):
                    if alive[gi_]:
                        try:
                            next(gens[gi_])
                        except StopIteration:
                            alive[gi_] = False
        CP("dve", [pogS], [ogS], out=ogS.ap, in_=pogS.ap[:, 0:8 * NS2].rearrange("p (c t) -> p c t", c=8))
        gsi = 2 * NTI
        P.dma("sp", gos[gsi].ap[:, :].rearrange("(c p) t -> p c t", p=128), ogS.ap, "go_ws", R=[ogS], W=[gos[gsi]])
        gather(go_src[gsi], go_dst[gsi], gos[gsi], god[gsi], "cc_gos")

        arena.reset()
        scratch = mk_scratch()
        wo0, sl0 = ws.acquire("g_wo0")
        wo1, sl1 = ws.acquire("g_wo1")
        oA = arena.alloc([16, TW], BF16)
        oB = arena.alloc([16, TW], BF16)
        osel = arena.alloc([16, TW], BF16)
        for ti, (t0, w, smp) in enumerate(tiles):
            for (dst, rr_) in ((oA, 0), (oB, 1)):
                if smp:
                    gsrc, cA = god[2 * NTI], rr_ * NS
                else:
                    gsrc, cA = god[rr_ * NTI + ti], 0
                P.dma("pool", dst.ap[:, :, 0:w], gsrc.ap[:, cA:cA + w].rearrange("(c p) t -> p c t", p=128), "ld_go%d" % rr_, R=[gsrc], W=[dst])
            TS("dve", [oA, flag], [osel], out=osel.ap[:, :, 0:w], in0=oA.ap[:, :, 0:w], scalar1=flag.ap[:, 1:2], scalar2=None, op0=ALU.mult)
            STT("dve", [oB, flag, osel], [osel], out=osel.ap[:, :, 0:w], in0=oB.ap[:, :, 0:w], scalar=flag.ap[:, 0:1], in1=osel.ap[:, :, 0:w],
                op0=ALU.mult, op1=ALU.add)
            for n2 in range(8):
                pb = ps()
                mm_group(P, pb.ap[:, 0:w], [((wo0 if k < 8 else wo1).ap[:, k % 8, n2 * 128:(n2 + 1) * 128], osel.ap[:, k, 0:w]) for k in range(16)],
                         R=[wo0, wo1, osel], W=[pb])
                resid_add(ti, n2, pb.ap[:, 0:w], pb, 2, scratch)
        ws.release(sl0)
        ws.release(sl1)

    def sconv_mixer():
        nt = len(tiles) - 1
        c0_, sl0 = ws.acquire("sc_0")
        c1_, sl1 = ws.acquire("sc_1")
        cb_, slb = ws.acquire("sc_b")
        co_, slo = ws.acquire("sc_o")
        scs = [c0_, c1_]
        wsc = V("sconv_w_conv").rearrange("p (c k) -> p c k", c=8)

        def proj_p(hT_b, col0, w, out_fn, scratch):
            for c in range(8):
                sb = scs[c // 4]
                j = c % 4
                pg = ps()
                mm_group(P, pg.ap[:, 0:w], [(sb.ap[:, k, j * 128:(j + 1) * 128], hT_b.ap[:, k, col0:col0 + w]) for k in range(8)],
                         R=[sb, hT_b], W=[pg])
                ph = ps()
                mm_group(P, ph.ap[:, 0:w], [(sb.ap[:, k, 512 + j * 128:512 + (j + 1) * 128], hT_b.ap[:, k, col0:col0 + w]) for k in range(8)],
                         R=[sb, hT_b], W=[ph])
                gt = scratch["tmp"][c % 2]
                ACT([pg], [gt], out=gt.ap[:, 0:w], in_=pg.ap[:, 0:w], func=AF.Identity)
                oap, obuf = out_fn(c)
                TT("dve", [gt, ph], [obuf], out=oap, in0=gt.ap[:, 0:w], in1=ph.ap[:, 0:w], op=ALU.mult)

        def gate_out(hT_b, w, y_fn, ybufs, ti, zT, scratch):
            for c in range(8):
                pgb = ps()
                mm_group(P, pgb.ap[:, 0:w], [(cb_.ap[:, k, c * 128:(c + 1) * 128], hT_b.ap[:, k, 0:w]) for k in range(8)],
                         R=[cb_, hT_b], W=[pgb])
                yap = y_fn(c)
                TT("dve", ybufs + [pgb], [zT], out=zT.ap[:, c, 0:w], in0=yap, in1=pgb.ap[:, 0:w], op=ALU.mult)
            for n2 in range(8):
                pb = ps()
                mm_group(P, pb.ap[:, 0:w], [(co_.ap[:, k, n2 * 128:(n2 + 1) * 128], zT.ap[:, k, 0:w]) for k in range(8)],
                         R=[co_, zT], W=[pb])
                resid_add(ti, n2, pb.ap[:, 0:w], pb, 2, scratch)

        arena.reset()
        scratch = mk_scratch()
        hT = arena.alloc([8, TW], BF16)
        pW = arena.alloc([8, 2 + TW], F32)
        yy = arena.alloc([8, TW], F32)
        zT = arena.alloc([8, TW], BF16)
        ph2 = arena.alloc([8, 2], F32)
        tl = nt - 1
        t0l, wl, _ = tiles[tl]
        norm_mod_src(xb[tl], xb[tl].ap[:, :, wl - 2:wl], 2, False, 1, 0, hT, scratch)
        proj_p(hT, 0, 2, lambda c: (ph2.ap[:, c, :], ph2), scratch)
        P.dma("sp", scp_d.rearrange("(c p) t -> p c t", p=128), ph2.ap, "st_scp", R=[ph2])
        if CFG.get("COLL", True):
            hsrc = Buf(sch_src.ap())
            hdst = Buf(sch_dst.ap())
            P.dma("sp", hsrc.ap[:, :], ph2.ap.rearrange("p c t -> p (c t)"), "sch_w", R=[ph2], W=[hsrc])
            P.collective(sch_src.ap().opt(), sch_dst.ap().opt(), "cc_sc", R=[hsrc], W=[hdst])
            P.dma("sp", ph2.ap.rearrange("p c t -> p (c t)"), hdst.ap[0:128, :], "sch_r", R=[hdst], W=[ph2])
        TS("dve", [ph2, flag], [pW], out=pW.ap[:, :, 0:2], in0=ph2.ap, scalar1=flag.ap[:, 0:1], scalar2=None, op0=ALU.mult)
        for ti in range(nt):
            t0, w, _ = tiles[ti]
            norm_mod(ti, 1, 0, hT, scratch)
            proj_p(hT, 0, w, lambda c: (pW.ap[:, c, 2:2 + w], pW), scratch)
            for c in range(8):
                TS("dve", [pW, vecs], [yy], out=yy.ap[:, c, 0:w], in0=pW.ap[:, c, 0:w], scalar1=wsc[:, c, 0:1], scalar2=None, op0=ALU.mult)
                for k in (1, 2):
                    STT("dve", [pW, vecs, yy], [yy], out=yy.ap[:, c, 0:w], in0=pW.ap[:, c, k:k + w], scalar=wsc[:, c, k:k + 1],
                        in1=yy.ap[:, c, 0:w], op0=ALU.mult, op1=ALU.add)
            gate_out(hT, w, lambda c: yy.ap[:, c, 0:w], [yy], ti, zT, scratch)
            if ti != nt - 1:
                CP("dve", [pW], [pW], out=pW.ap[:, :, 0:2], in_=pW.ap[:, :, w:w + 2])

        arena.reset()
        scratch = mk_scratch()
        ti = nt
        hT = arena.alloc([8, NS], BF16)
        pS_ = arena.alloc([8, NS], F32)
        st = arena.alloc([8, NS, 2], F32)
        yy = arena.alloc([8, NS], F32)
        t3 = arena.alloc([8, NS], F32)
        zT = arena.alloc([8, NS], BF16)
        tok = arena.alloc([D], F32)
        P.dma("sp", st.ap, stscT_d.rearrange("(c p) s k -> p c s k", p=128), "ld_stsc", W=[st])
        norm_mod(ti, 1, 0, hT, scratch)
        proj_p(hT, 0, NS, lambda c: (pS_.ap[:, c, :], pS_), scratch)
        TT("dve", [st, vecs], [yy], out=yy.ap, in0=st.ap[:, :, :, 0], in1=wsc[:, :, 0:1].to_broadcast([128, 8, NS]), op=ALU.mult)
        TT("dve", [st, vecs], [t3], out=t3.ap, in0=st.ap[:, :, :, 1], in1=wsc[:, :, 1:2].to_broadcast([128, 8, NS]), op=ALU.mult)
        TT("dve", [yy, t3], [yy], out=yy.ap, in0=yy.ap, in1=t3.ap, op=ALU.add)
        TT("dve", [pS_, vecs], [t3], out=t3.ap, in0=pS_.ap, in1=wsc[:, :, 2:3].to_broadcast([128, 8, NS]), op=ALU.mult)
        TT("dve", [yy, t3], [yy], out=yy.ap, in0=yy.ap, in1=t3.ap, op=ALU.add)
        gate_out(hT, NS, lambda c: yy.ap[:, c, :], [yy], ti, zT, scratch)
        P.dma("sp", scs_d[:, 0, :], stsc_d[:, 1, :], "st_scs")
        to_token_major([pS_.ap[:, c, :] for c in range(8)], [pS_], tok)
        P.dma("sp", scs_d[:, 1, :], tok.ap[0:NS, :], "st_scs", R=[tok])
        ws.release(sl0)
        ws.release(sl1)
        ws.release(slb)
        ws.release(slo)

    def mk_scratch():
        return {"sq": [arena.alloc([TW], BF16) for _ in range(2)],
                "tmp": [arena.alloc([TW], F32) for _ in range(2)],
                "tmp2": arena.alloc([NS], F32),
                "rs": arena.alloc([TW], F32)}

    def mlp(li):
        arena.reset()
        scratch = mk_scratch()
        hT = [arena.alloc([8, w], BF16) for (t0, w, _) in tiles]
        hid = [arena.alloc([8, TW], BF16) for _ in range(2)]
        rr = scratch["tmp"]
        for f in range(4):
            up, us = ws.acquire("up%d_%d" % (li, f))
            dn, ds = ws.acquire("down%d_%d" % (li, f))

            def up_stage(ti, hb):
                t0, w, smp = tiles[ti]
                for n in range(8):
                    pb = ps()
                    mm_group(P, pb.ap[:, 0:w], [(up.ap[:, k, n * 128:(n + 1) * 128], hT[ti].ap[:, k, :]) for k in range(8)],
                             R=[up, hT[ti]], W=[pb])
                    r = rr[n % 2]
                    ACT([pb], [r], out=r.ap[:, 0:w], in_=pb.ap[:, 0:w], func=AF.Relu)
                    TT("dve", [r], [hb], out=hb.ap[:, n, 0:w], in0=r.ap[:, 0:w], in1=r.ap[:, 0:w], op=ALU.mult)

            def down_stage(ti, hb):
                t0, w, smp = tiles[ti]
                for n2 in range(8):
                    pb = ps()
                    mm_group(P, pb.ap[:, 0:w], [(dn.ap[:, k, n2 * 128:(n2 + 1) * 128], hb.ap[:, k, 0:w]) for k in range(8)],
                             R=[dn, hb], W=[pb])
                    resid_add(ti, n2, pb.ap[:, 0:w], pb, 5, scratch)
            ntl = len(tiles)
            if f == 0:
                norm_mod(0, 4, 3, hT[0], scratch)
                if ntl > 1:
                    norm_mod(1, 4, 3, hT[1], scratch)
            up_stage(0, hid[0])
            for ti in range(ntl):
                if ti + 1 < ntl:
                    up_stage(ti + 1, hid[(ti + 1) % 2])
                    if f == 0 and ti + 2 < ntl:
                        norm_mod(ti + 2, 4, 3, hT[ti + 2], scratch)
                down_stage(ti, hid[ti % 2])
            ws.release(us)
            ws.release(ds)

    def dbg_dump(idx):
        if dbg_d is None:
            return
        for ti, (t0, w, _) in enumerate(tiles):
            P.dma("sp", dbg_d[idx, :, t0:t0 + w].rearrange("(c p) t -> p c t", p=128), xb[ti].ap, "st_dbg", R=[xb[ti]])

    def final_norm():
        arena.reset()
        scratch = mk_scratch()
        yo = [arena.alloc([8, TW], F32) for _ in range(2)]
        for ti, (t0, w, smp) in enumerate(tiles):
            rs = rstd_tile(ti, scratch)
            yb = yo[ti % 2]
            for c in range(8):
                STT("dve", [xb[ti], rs, vecs], [yb], out=yb.ap[:, c, 0:w], in0=xb[ti].ap[:, c, :],
                    scalar=V("norm_final", c, 1), in1=rs.ap[:, 0:w], op0=ALU.mult, op1=ALU.mult)
            P.dma("sp", yT_d[:, t0:t0 + w].rearrange("(c p) t -> p c t", p=128), yb.ap[:, :, 0:w], "st_y%d" % (ti % 2), R=[yb])

    for li in range(CFG["LAYERS"]):
        adaln(li)
        if CFG["MIX"] and li == 0:
            conf_mixer()
        if CFG["MIX"] and li == 1:
            swa_mixer()
        if CFG["MIX"] and li == 2:
            gdn_mixer()
        if CFG["MIX"] and li == 3:
            sconv_mixer()
        dbg_dump(2 * li)
        mlp(li)
        dbg_dump(2 * li + 1)
    final_norm()

    for k in P.dkeys:
        if k.startswith("st_"):
            P._wait("sp", k, P.cnt[k])

    with nc.Block() as block:
        @block.tensor
        def _(e):
            P.replay(e, "pe")

        @block.scalar
        def _(e):
            P.replay(e, "act")

        @block.vector
        def _(e):
            P.replay(e, "dve")

        @block.gpsimd
        def _(e):
            P.replay(e, "pool")

        @block.sync
        def _(e):
            P.replay(e, "sp")
    print("[kernel] instructions:", P.ninstr, {k: len(v) for k, v in P.ops.items()})
    nc._dbg_names = P.names
    return nc


def make_in_maps(inp):
    NTI = CFG["NTI"]
    TP = NTI * TW
    vecs_hf = [vec_layout(inp, 0).build(), vec_layout(inp, 1).build()]
    cm = const_mats()
    cmats = np.concatenate([cm[n] for n in CM_NAMES], axis=1).astype(np.float32)
    maps = []
    for c in range(8):
        s, hf = c // 2, c % 2
        base = hf * 2048
        xp = inp["x_prompt"][s, base:base + TP, :]
        xs = inp["x_sample"][c * NS:(c + 1) * NS, 0, :]
        xT = np.ascontiguousarray(np.concatenate([xp, xs], 0).T)
        cT = np.ascontiguousarray(np.concatenate([inp["c_prompt"][s:s + 1], inp["c_sample"][c * NS:(c + 1) * NS]], 0).T)
        vecs = vecs_hf[hf]
        m = {"xT": xT, "cT": cT, "vecs": vecs, "cmats": cmats,
             "w_ada": inp["w_ada"], "w_up": inp["w_up"], "w_down": inp["w_down"]}
        xh = np.zeros((D, 32), np.float32)
        if hf == 1:
            xh[:, 0:30] = inp["x_prompt"][s, base - 30:base, :].T
        m["xhalo"] = xh
        fl = np.zeros((128, 2), np.float32)
        fl[:, 0] = hf
        fl[:, 1] = 1 - hf
        m["flag"] = fl
        sl = slice(c * NS, (c + 1) * NS)
        m["conf_w_pw1"] = inp["conf_w_pw1"]
        m["conf_w_pw2"] = inp["conf_w_pw2"]
        m["st_conf"] = np.ascontiguousarray(inp["state_conf_conv"][0, sl])
        m["swa_w_qkv"] = inp["swa_w_qkv"]
        m["swa_w_o"] = inp["swa_w_o"]
        pos = np.concatenate([np.arange(base, base + TP), np.full(NS, 8192)]).astype(np.float32)
        inv = np.power(np.float32(500000.0), -np.arange(8, dtype=np.float32) * np.float32(2.0) / np.float32(16.0)).astype(np.float32)
        ang = (pos[None, :] * inv[:, None]).astype(np.float32)
        rc = np.ones((128, TP + NS), np.float32)
        rs_ = np.zeros((128, TP + NS), np.float32)
        for hb in (0, 64):
            rc[hb:hb + 8] = np.cos(ang)
            rc[hb + 8:hb + 16] = np.cos(ang)
            rs_[hb:hb + 8] = np.sin(ang)
            rs_[hb + 8:hb + 16] = np.sin(ang)
        m["ropeC"] = rc
        m["ropeS"] = rs_
        ck = inp["cache_swa_k"][0, sl].reshape(NS, 128, 256)
        m["ck"] = np.ascontiguousarray(ck)
        m["cv"] = np.ascontiguousarray(inp["cache_swa_v"][0, sl].reshape(NS, 128, 256))
        ckt = ck.reshape(NS, 128, 4, 64).transpose(3, 0, 2, 1)
        m["ckT"] = np.ascontiguousarray(np.concatenate([ckt, ckt], 0))
        chans = gdn_channels(hf)
        wi = inp["gdn_w_in"][0]
        zc = 4096 + np.concatenate([np.arange(h * 128, (h + 1) * 128) for h in range(8 * hf, 8 * hf + 8)])
        m["gdn_w_in_c"] = np.ascontiguousarray(np.concatenate([wi[:, chans], wi[:, zc]], axis=1))
        m["gdn_ba_c"] = np.ascontiguousarray(np.concatenate([wi[:, 6144 + 8 * hf:6144 + 8 * hf + 8], wi[:, 6160 + 8 * hf:6160 + 8 * hf + 8]], axis=1))
        m["gdn_w_o"] = inp["gdn_w_o"]
        ps_ = slice((c // 2) * 2 * NS, (c // 2 + 1) * 2 * NS)
        m["g_ssm"] = np.ascontiguousarray(inp["state_gdn_ssm"][0, ps_, 8 * hf:8 * hf + 8])
        gcv = inp["state_gdn_conv"][0, ps_][:, :, chans]
        m["g_conv"] = np.ascontiguousarray(gcv)
        gT = np.zeros((2048, 2 * NS, 4), np.float32)
        gT[:, :, 0:3] = gcv.transpose(2, 0, 1)
        m["g_convT"] = gT
        m["sconv_w_in"] = inp["sconv_w_in"]
        m["sconv_w_out"] = inp["sconv_w_out"]
        m["st_sc"] = np.ascontiguousarray(inp["state_sconv"][0, sl])
        m["st_scT"] = np.ascontiguousarray(inp["state_sconv"][0, sl].transpose(2, 0, 1))
        m["st_confT"] = np.ascontiguousarray(inp["state_conf_conv"][0, sl].transpose(2, 0, 1))
        maps.append(m)
    return maps


_NC_CACHE = {}


def kernel(**inp):
    inp = {k: np.asarray(v) for k, v in inp.items()}
    NTI = CFG["NTI"]
    TP = NTI * TW
    key = (NTI, CFG["LAYERS"], CFG["MIX"], CFG["DBG"])
    if key not in _NC_CACHE:
        _NC_CACHE[key] = build_program()
    nc = _NC_CACHE[key]
    maps = make_in_maps(inp)
    res = run_bass_kernel_spmd(nc, maps, core_ids=list(range(8)))
    R = res.results
    y_p = np.zeros((4, 4096, D), np.float32)
    y_s = np.zeros((128, 1, D), np.float32)
    conf_p = np.zeros((1, 4, 30, D), np.float32)
    conf_s = np.zeros((1, 128, 30, D), np.float32)
    k_p = np.zeros((1, 4, 128, 4, 64), np.float32)
    v_p = np.zeros((1, 4, 128, 4, 64), np.float32)
    k_s = np.zeros((1, 128, 128, 4, 64), np.float32)
    v_s = np.zeros((1, 128, 128, 4, 64), np.float32)
    ssm_p = np.zeros((1, 4, 16, 128, 128), np.float32)
    ssm_s = np.zeros((1, 128, 16, 128, 128), np.float32)
    gc_p = np.zeros((1, 4, 3, 4096), np.float32)
    gc_s = np.zeros((1, 128, 3, 4096), np.float32)
    sc_p = np.zeros((1, 4, 2, D), np.float32)
    sc_s = np.zeros((1, 128, 2, D), np.float32)
    for c in range(8):
        s, hf = c // 2, c % 2
        r = R[c]
        sl = slice(c * NS, (c + 1) * NS)
        ps_ = slice(s * 2 * NS, (s + 1) * 2 * NS)
        yT = np.asarray(r["yT"]).reshape(D, -1)
        y_p[s, hf * 2048:hf * 2048 + TP] = yT[:, :TP].T
        y_s[sl, 0] = yT[:, TP:].T
        if "conf_s" in r:
            conf_s[0, sl] = np.asarray(r["conf_s"]).reshape(NS, 30, D)
            if hf == 1:
                conf_p[0, s] = np.asarray(r["conf_p"]).reshape(D, 30).T
        if "swa_ks" in r:
            k_s[0, sl] = np.asarray(r["swa_ks"]).reshape(NS, 128, 4, 64)
            v_s[0, sl] = np.asarray(r["swa_vs"]).reshape(NS, 128, 4, 64)
            if hf == 1:
                kp = np.asarray(r["swa_kp"]).reshape(128, 4, 128)[0:64]
                k_p[0, s] = kp.transpose(2, 1, 0)
                v_p[0, s] = np.asarray(r["swa_vp"]).reshape(128, 4, 64)
        if "g_ssm_s" in r:
            chans = gdn_channels(hf)
            ssm_p[0, s, 8 * hf:8 * hf + 8] = np.asarray(r["g_ssm_p"]).reshape(8, 128, 128)
            ssm_s[0, ps_, 8 * hf:8 * hf + 8] = np.asarray(r["g_ssm_s"]).reshape(2 * NS, 8, 128, 128)
            gc_p[0, s][:, chans] = np.asarray(r["g_conv_p"]).reshape(2048, 3).T
            gs = np.asarray(r["g_conv_s"]).reshape(2 * NS, 3, 2048)
            tmp = gc_s[0, ps_]
            tmp[:, :, chans] = gs
            gc_s[0, ps_] = tmp
        if "sc_s" in r:
            sc_s[0, sl] = np.asarray(r["sc_s"]).reshape(NS, 2, D)
            if hf == 1:
                sc_p[0, s] = np.asarray(r["sc_p"]).reshape(D, 2).T
    kernel.last = R
    return (y_p, y_s, conf_p, conf_s, k_p, k_s, v_p, v_s, ssm_p, ssm_s, gc_p, gc_s, sc_p, sc_s)
```
